# Optimizing a Trainium2 kernel written in Bass

```python
import math
import jax, jax.numpy as jnp
from jax import lax
import numpy as np

D_MODEL = 1024
BATCH = 8
SEQ = 4096
DEPTH = 4

N_META = 16
BLOCK_Q = 128
N_A_LAYERS = DEPTH // 2
N_B_LAYERS = DEPTH - N_A_LAYERS
SB_HEADS = 16
SB_HEAD_DIM = D_MODEL // SB_HEADS
DIFF_HEADS = 8
DIFF_QK_DIM = D_MODEL // DIFF_HEADS // 2
DIFF_V_DIM = 2 * DIFF_QK_DIM
ROPE_DIM = DIFF_QK_DIM // 4
ROPE_THETA = 500000.0
D_FF = 4 * D_MODEL
EPS = 1e-6

kernel_name = "yoco_stickbreak_diffattn_hybrid"


def rmsnorm(x, g):
    xf = x.astype(jnp.float32)
    y = xf * lax.rsqrt(jnp.mean(xf * xf, axis=-1, keepdims=True) + EPS)
    return (y * g.astype(jnp.float32)).astype(x.dtype)


def rope_tables(T):
    inv_freq = jnp.power(ROPE_THETA, -jnp.arange(0, ROPE_DIM, 2, dtype=jnp.float32) / ROPE_DIM)
    ang = jnp.arange(T, dtype=jnp.float32)[:, None] * inv_freq[None, :]
    return jnp.cos(ang), jnp.sin(ang)


def apply_partial_rope(x, cos, sin):
    half = ROPE_DIM // 2
    x1 = x[..., :half]
    x2 = x[..., half:ROPE_DIM]
    c = cos[:, None, :]
    s = sin[:, None, :]
    return jnp.concatenate([x1 * c - x2 * s, x2 * c + x1 * s, x[..., ROPE_DIM:]], axis=-1)


def to_blocks(q):
    B, H, T, D = q.shape
    return q.reshape(B, H, T // BLOCK_Q, BLOCK_Q, D).transpose(2, 0, 1, 3, 4)


def from_blocks(o):
    NB, B, H, BQ, D = o.shape
    return o.transpose(1, 2, 0, 3, 4).reshape(B, H, NB * BQ, D)


def merge_heads(o):
    B, H, T, D = o.shape
    return o.transpose(0, 2, 1, 3).reshape(B, T, H * D)


def stick_breaking_attention(q, k, v):
    T = q.shape[2]
    scale = SB_HEAD_DIM ** -0.5
    kf = k.astype(jnp.float32)
    key_pos = jnp.arange(T)

    def one_block(args):
        qb, q0 = args
        z = jnp.einsum('bhqd,bhkd->bhqk', qb.astype(jnp.float32), kf) * scale
        q_pos = q0 + jnp.arange(BLOCK_Q)
        past = key_pos[None, :] < q_pos[:, None]
        log_keep = jnp.where(past, jax.nn.log_sigmoid(-z), 0.0)
        after = lax.cumsum(log_keep, axis=3, reverse=True) - log_keep
        w = jnp.where(past, jnp.exp(jax.nn.log_sigmoid(z) + after), 0.0)
        return jnp.einsum('bhqk,bhkd->bhqd', w.astype(v.dtype), v)

    starts = jnp.arange(T // BLOCK_Q) * BLOCK_Q
    return from_blocks(lax.map(one_block, (to_blocks(q), starts)))


def differential_attention(q1, q2, k1, k2, v, lam):
    T = q1.shape[2]
    scale = DIFF_QK_DIM ** -0.5
    k1f = k1.astype(jnp.float32)
    k2f = k2.astype(jnp.float32)
    key_pos = jnp.arange(T)

    def one_block(args):
        q1b, q2b, q0 = args
        q_pos = q0 + jnp.arange(BLOCK_Q)
        causal = key_pos[None, :] <= q_pos[:, None]

        def probs(qb, kk):
            s = jnp.einsum('bhqd,bhkd->bhqk', qb.astype(jnp.float32), kk) * scale
            return jax.nn.softmax(jnp.where(causal, s, -jnp.inf), axis=-1)

        w = probs(q1b, k1f) - lam * probs(q2b, k2f)
        return jnp.einsum('bhqk,bhkd->bhqd', w.astype(v.dtype), v)

    starts = jnp.arange(T // BLOCK_Q) * BLOCK_Q
    return from_blocks(lax.map(one_block, (to_blocks(q1), to_blocks(q2), starts)))


def squared_relu_mlp(x, g, w_in, w_out):
    h = rmsnorm(x, g)
    return jnp.square(jax.nn.relu(h @ w_in)) @ w_out


def setup_inputs(seed: int = 0) -> dict:
    key = jax.random.key(seed)
    ks = jax.random.split(key, 20)
    nrm = jax.random.normal
    f32 = jnp.float32

    def w(k, shape, fan_in):
        return nrm(k, shape, f32) * fan_in ** -0.5

    def gain(k, shape):
        return 1.0 + 0.05 * nrm(k, shape, f32)

    return {
        "x": nrm(ks[0], (BATCH, SEQ, D_MODEL), f32),
        "meta_tokens": nrm(ks[1], (N_META, D_MODEL), f32),
        "a_norm_g": gain(ks[2], (N_A_LAYERS, D_MODEL)),
        "a_w_qkv": w(ks[3], (N_A_LAYERS, D_MODEL, 3 * D_MODEL), D_MODEL),
        "a_w_o": w(ks[4], (N_A_LAYERS, D_MODEL, D_MODEL), D_MODEL),
        "kv_norm_g": gain(ks[5], (D_MODEL,)),
        "kv_w": w(ks[6], (D_MODEL, DIFF_HEADS * (2 * DIFF_QK_DIM + DIFF_V_DIM)), D_MODEL),
        "kv_k_norm_g": gain(ks[7], (2, DIFF_QK_DIM)),
        "b_norm_g": gain(ks[8], (N_B_LAYERS, D_MODEL)),
        "b_w_q": w(ks[9], (N_B_LAYERS, D_MODEL, DIFF_HEADS * 2 * DIFF_QK_DIM), D_MODEL),
        "b_q_norm_g": gain(ks[10], (N_B_LAYERS, 2, DIFF_QK_DIM)),
        "b_lambda": 0.1 * nrm(ks[11], (N_B_LAYERS, 4, DIFF_QK_DIM), f32),
        "b_subln_g": gain(ks[12], (N_B_LAYERS, DIFF_V_DIM)),
        "b_w_o": w(ks[13], (N_B_LAYERS, DIFF_HEADS * DIFF_V_DIM, D_MODEL), DIFF_HEADS * DIFF_V_DIM),
        "mlp_norm_g": gain(ks[14], (DEPTH, D_MODEL)),
        "mlp_w_in": w(ks[15], (DEPTH, D_MODEL, D_FF), D_MODEL),
        "mlp_w_out": w(ks[16], (DEPTH, D_FF, D_MODEL), D_FF),
    }


def reference(x, meta_tokens, a_norm_g, a_w_qkv, a_w_o, kv_norm_g, kv_w, kv_k_norm_g,
              b_norm_g, b_w_q, b_q_norm_g, b_lambda, b_subln_g, b_w_o,
              mlp_norm_g, mlp_w_in, mlp_w_out):
    B, S, D = x.shape
    L = S + N_META
    pad = (-L) % BLOCK_Q
    meta = jnp.broadcast_to(meta_tokens.astype(x.dtype)[None], (B, N_META, D))
    h = jnp.concatenate([meta, x, jnp.zeros((B, pad, D), x.dtype)], axis=1)
    T = L + pad
    cos, sin = rope_tables(T)

    shared = None
    for layer in range(DEPTH):
        if layer < N_A_LAYERS:
            i = layer
            hn = rmsnorm(h, a_norm_g[i])
            qkv = (hn @ a_w_qkv[i]).reshape(B, T, 3, SB_HEADS, SB_HEAD_DIM)
            q = qkv[:, :, 0].transpose(0, 2, 1, 3)
            k = qkv[:, :, 1].transpose(0, 2, 1, 3)
            v = qkv[:, :, 2].transpose(0, 2, 1, 3)
            o = stick_breaking_attention(q, k, v)
            h = h + merge_heads(o) @ a_w_o[i]
        else:
            j = layer - N_A_LAYERS
            if shared is None:
                kvn = rmsnorm(h, kv_norm_g)
                kv = kvn @ kv_w
                kk = kv[..., :DIFF_HEADS * 2 * DIFF_QK_DIM].reshape(B, T, DIFF_HEADS, 2, DIFF_QK_DIM)
                k1 = apply_partial_rope(rmsnorm(kk[..., 0, :], kv_k_norm_g[0]), cos, sin)
                k2 = apply_partial_rope(rmsnorm(kk[..., 1, :], kv_k_norm_g[1]), cos, sin)
                vv = kv[..., DIFF_HEADS * 2 * DIFF_QK_DIM:].reshape(B, T, DIFF_HEADS, DIFF_V_DIM)
                shared = (k1.transpose(0, 2, 1, 3), k2.transpose(0, 2, 1, 3), vv.transpose(0, 2, 1, 3))
            k1, k2, vv = shared
            lam_init = 0.8 - 0.6 * math.exp(-0.3 * layer)
            lp = b_lambda[j].astype(jnp.float32)
            lam = jnp.exp(jnp.sum(lp[0] * lp[1])) - jnp.exp(jnp.sum(lp[2] * lp[3])) + lam_init
            hn = rmsnorm(h, b_norm_g[j])
            qq = (hn @ b_w_q[j]).reshape(B, T, DIFF_HEADS, 2, DIFF_QK_DIM)
            q1 = apply_partial_rope(rmsnorm(qq[..., 0, :], b_q_norm_g[j, 0]), cos, sin)
            q2 = apply_partial_rope(rmsnorm(qq[..., 1, :], b_q_norm_g[j, 1]), cos, sin)
            o = differential_attention(q1.transpose(0, 2, 1, 3), q2.transpose(0, 2, 1, 3), k1, k2, vv, lam)
            o = rmsnorm(o, b_subln_g[j]) * (1.0 - lam_init)
            h = h + merge_heads(o).astype(h.dtype) @ b_w_o[j]
        h = h + squared_relu_mlp(h, mlp_norm_g[layer], mlp_w_in[layer], mlp_w_out[layer])

    return h[:, N_META:N_META + S]
```

```python
import math
from contextlib import ExitStack

import numpy as np
import ml_dtypes

import concourse.bass as bass
import concourse.mybir as mybir
from concourse.bass_utils import run_bass_kernel_spmd

F32 = mybir.dt.float32
BF16 = mybir.dt.bfloat16
AF = mybir.ActivationFunctionType
ALU = mybir.AluOpType
AX = mybir.AxisListType

D = 1024
S = 4096
NM = 16
T = 4224
NT = 33
DFF = 4096
EPS = 1e-6
CHUNKS = [(0, 128)] + [(128 + 512 * c, 512) for c in range(8)]
CHUNKS_A = [(0, 128)] + [(128 + 512 * c, 512) for c in range(7)] + [(3712, NM + S - 3712)]

ENGS = ("pe", "act", "dve", "pool", "sp")
NRING = 8
NDUMMY = 0
DMAQ = ("sp", "act")


class Tok:
    __slots__ = ("w", "rs")

    def __init__(self):
        self.w = None
        self.rs = []


def toks(n):
    return [Tok() for _ in range(n)]


class Op:
    __slots__ = ("eng", "fn", "deps", "sig", "val", "dma", "ring", "rval", "waits")

    def __init__(self, eng, fn, dma):
        self.eng = eng
        self.fn = fn
        self.deps = ()
        self.sig = False
        self.val = 0
        self.dma = dma
        self.ring = 0
        self.rval = 0
        self.waits = None


class Sync:
    def __init__(self, nc, stack):
        self.sem = {e: stack.enter_context(nc.semaphore(f"s_{e}")) for e in ENGS}
        self.cnt = {e: 0 for e in ENGS}
        self.ring = {q: [stack.enter_context(nc.semaphore(f"ring_{q}{i}")) for i in range(NRING)] for q in DMAQ}
        self.ndma = {q: 0 for q in DMAQ}


class Prog:
    def __init__(self, sync):
        self.sy = sync
        self.ops = []

    def op(self, eng, fn, r=(), w=(), dma=False):
        o = Op(eng, fn, dma)
        deps = set()
        for t in r:
            if t.w is not None:
                deps.add(t.w)
        for t in w:
            if t.w is not None:
                deps.add(t.w)
            deps.update(t.rs)
        o.deps = deps
        for t in r:
            t.rs.append(o)
        for t in w:
            t.w = o
            t.rs = []
        self.ops.append(o)
        return o

    def emit(self, block):
        sy = self.sy
        for o in self.ops:
            for a in o.deps:
                if a.dma:
                    continue
                if a.eng == "pe" and o.eng == "pe":
                    continue
                a.sig = True
        waited = {e: {} for e in ENGS}
        ring_out = {}
        for o in self.ops:
            ws = {}
            wd = waited[o.eng]
            for a in o.deps:
                if a.dma:
                    key = ("r", a.eng, a.ring)
                    v = a.rval
                else:
                    if a.eng == "pe" and o.eng == "pe":
                        continue
                    key = ("e", a.eng)
                    v = a.val
                if wd.get(key, 0) >= v:
                    continue
                if ws.get(key, 0) < v:
                    ws[key] = v
            if o.dma:
                j = sy.ndma[o.eng]
                sy.ndma[o.eng] += 1
                o.ring = j % NRING
                o.rval = 16 * (j // NRING + 1)
                if j >= NRING:
                    key = ("r", o.eng, o.ring)
                    v = 16 * (j // NRING)
                    if wd.get(key, 0) < v and ws.get(key, 0) < v:
                        ws[key] = v
                ring_out[(o.eng, o.ring)] = o.rval
            elif o.sig:
                sy.cnt[o.eng] += 1
                o.val = sy.cnt[o.eng]
            for k, v in ws.items():
                wd[k] = v
            o.waits = list(ws.items())
        final_sp = [(("r", q, r), v) for (q, r), v in ring_out.items() if waited["sp"].get(("r", q, r), 0) < v]
        per = {e: [o for o in self.ops if o.eng == e] for e in ENGS}

        def semof(key):
            return sy.ring[key[1]][key[2]] if key[0] == "r" else sy.sem[key[1]]

        def run(e, eng):
            for o in per[e]:
                for k, v in o.waits:
                    eng.wait_ge(semof(k), v)
                ins = o.fn(eng)
                if o.dma:
                    ins.then_inc(sy.ring[e][o.ring], 16)
                elif o.sig:
                    ins.then_inc(sy.sem[e], 1)
            if e == "sp":
                for k, v in final_sp:
                    eng.wait_ge(semof(k), v)

        block.tensor(lambda eng: run("pe", eng))
        block.scalar(lambda eng: run("act", eng))
        block.vector(lambda eng: run("dve", eng))
        block.gpsimd(lambda eng: run("pool", eng))
        block.sync(lambda eng: run("sp", eng))


def mm(p, out, lhsT, rhs, start, stop, r, w, skip=False):
    if skip:
        p.op("pe", lambda e: e.matmul(out, lhsT=lhsT, rhs=rhs, start=start, stop=stop, skip_group_check=True), r, w)
    else:
        p.op("pe", lambda e: e.matmul(out, lhsT=lhsT, rhs=rhs, start=start, stop=stop), r, w)


def tr(p, out, in_, ident, r, w):
    p.op("pe", lambda e: e.transpose(out, in_, ident), r, w)


def act(p, out, in_, func, r, w, **kw):
    p.op("act", lambda e: e.activation(out=out, in_=in_, func=func, **kw), r, w)


def tt(p, eng, out, in0, in1, op, r, w):
    p.op(eng, lambda e: e.tensor_tensor(out=out, in0=in0, in1=in1, op=op), r, w)


def ts(p, eng, out, in0, s1, op0, r, w, s2=None, op1=None):
    if op1 is None:
        p.op(eng, lambda e: e.tensor_scalar(out=out, in0=in0, scalar1=s1, scalar2=None, op0=op0), r, w)
    else:
        p.op(eng, lambda e: e.tensor_scalar(out=out, in0=in0, scalar1=s1, scalar2=s2, op0=op0, op1=op1), r, w)


def stt(p, eng, out, in0, scalar, in1, op0, op1, r, w):
    p.op(eng, lambda e: e.scalar_tensor_tensor(out=out, in0=in0, scalar=scalar, in1=in1, op0=op0, op1=op1), r, w)


def cp(p, eng, out, in_, r, w, scale=None):
    if eng == "act":
        if scale is None:
            p.op("act", lambda e: e.activation(out=out, in_=in_, func=AF.Copy), r, w)
        else:
            p.op("act", lambda e: e.activation(out=out, in_=in_, func=AF.Copy, scale=scale), r, w)
    else:
        p.op(eng, lambda e: e.tensor_copy(out=out, in_=in_), r, w)


def recip(p, out, in_, r, w):
    p.op("dve", lambda e: e.reciprocal(out=out, in_=in_), r, w)


def dma(p, out, in_, r, w, q="sp"):
    p.op(q, lambda e: e.dma_start(out=out, in_=in_), r, w, dma=True)


class Ctx:
    pass


class Phase:
    def __init__(self, C, name):
        self.C = C
        self.nc = C.nc
        self.name = name
        self.st = ExitStack()
        self.p = Prog(C.sy)
        self.n = 0

    def sb(self, shape, dt):
        self.n += 1
        return self.st.enter_context(self.nc.sbuf_tensor(f"{self.name}_t{self.n}", shape, dt))

    def psum(self):
        return self.st.enter_context(self.nc.psum_tensor(f"{self.name}_ps", [128, 4096], F32))

    def consts(self):
        c = self.sb([128, 6, 128], BF16)
        t = Tok()
        dma(self.p, c[:], self.C.cst.ap().rearrange("p (a b) -> p a b", a=6), [], [t])
        return c, t

    def finish(self):
        blk = self.st.enter_context(self.nc.Block())
        self.p.emit(blk)
        self.st.close()


def load_w(ph, stg, dst, dst_tok, src, nk, ncols, gain=None, gain_tok=None, engs=("pool",), cnt=[0], cap=2048, q="sp"):
    p = ph.p
    srcv = src.rearrange("(kc p) n -> p kc n", p=128)
    cw = min(ncols, cap)
    kb = max(1, min(nk, cap // cw))
    chunks = []
    for k0 in range(0, nk, kb):
        k1 = min(nk, k0 + kb)
        for c0 in range(0, ncols, cw):
            sa, stok = stg[cnt[0] % len(stg)]
            eng = engs[cnt[0] % len(engs)]
            cnt[0] += 1
            chunks.append((k0, k1, c0, sa, stok, eng))

    def issue(ch):
        k0, k1, c0, sa, stok, eng = ch
        sv = sa[:, 0:(k1 - k0) * cw].rearrange("p (k n) -> p k n", n=cw)
        dma(p, sv, srcv[:, k0:k1, c0:c0 + cw], [], [stok], q=q)

    def conv(ch):
        k0, k1, c0, sa, stok, eng = ch
        sv = sa[:, 0:(k1 - k0) * cw].rearrange("p (k n) -> p k n", n=cw)
        if gain is None:
            cp(p, eng, dst[:, k0:k1, c0:c0 + cw], sv, [stok], [dst_tok])
        else:
            for k in range(k0, k1):
                if eng == "act":
                    act(p, dst[:, k, c0:c0 + cw], sv[:, k - k0, :], AF.Copy, [stok, gain_tok], [dst_tok],
                        scale=gain[:, k:k + 1])
                else:
                    ts(p, eng, dst[:, k, c0:c0 + cw], sv[:, k - k0, :], gain[:, k:k + 1], ALU.mult,
                       [stok, gain_tok], [dst_tok])

    depth = len(stg) - 1
    n = len(chunks)
    for k in range(n + depth):
        if k < n:
            issue(chunks[k])
        if k - depth >= 0:
            conv(chunks[k - depth])


def norm_tile(p, hs, t_hs, hnb, t_hnb, stt_, t_st, junk, t_junk):
    act(p, junk, hs, AF.Square, [t_hs], [t_junk, t_st], accum_out=stt_[:, 0:1])
    act(p, stt_[:, 1:2], stt_[:, 0:1], AF.Ln, [t_st], [t_st], scale=1.0 / D, bias=EPS)
    act(p, stt_[:, 2:3], stt_[:, 1:2], AF.Exp, [t_st], [t_st], scale=-0.5)
    ts(p, "dve", hnb, hs, stt_[:, 2:3], ALU.mult, [t_hs, t_st], [t_hnb])


def transpose8(p, src, t_src, psbf_bank, t_bank, ident, t_c, dst, t_dst, ev="dve"):
    for kc in range(8):
        tr(p, psbf_bank[:, kc * 128:(kc + 1) * 128], src[:, kc * 128:(kc + 1) * 128], ident, [t_src, t_c], [t_bank])
    cp(p, ev, dst, psbf_bank.rearrange("p (a b) -> p a b", a=8), [t_bank], [t_dst])


def zero_oT_pad(p, z, tz, C=None):
    npad = T - NM - S
    zb = z.bitcast(BF16)[:, 0:8 * npad].rearrange("p (k t) -> p k t", k=8)
    dma(p, zero_oT_pad.C.oT[:, NM + S:T].rearrange("(kc p) t -> p kc t", p=128), zb, [tz], [Tok()])


def phase_init(C):
    ph = Phase(C, "init")
    p = ph.p
    z = ph.sb([128, D], F32)
    tz = Tok()
    th = Tok()
    p.op("pool", lambda e: e.memset(z[:], 0.0), [], [tz])
    dma(p, C.h[0:NM, :], C.meta.ap(), [], [th])
    for j in range(8):
        dma(p, C.h[NM + 512 * j:NM + 512 * (j + 1), :], C.x.ap()[512 * j:512 * (j + 1), :], [], [th])
    dma(p, C.h[NM + S:T, :], z[0:T - NM - S, :], [tz], [th])
    zero_oT_pad(p, z, tz)
    ph.finish()


def phase_a_attn(C, l):
    ph = Phase(C, f"aat{l}")
    p = ph.p
    nc = C.nc
    cst, t_c = ph.consts()
    ident, negtri, ones, maskS = cst[:, 0, :], cst[:, 1, :], cst[:, 2, :], cst[:, 3, :]
    hnT = ph.sb([128, 8, T], BF16)
    t_hnT = toks(NT)
    hs = [ph.sb([128, D], F32) for _ in range(2)]
    t_hs = toks(2)
    hnb = [ph.sb([128, D], BF16) for _ in range(2)]
    t_hnb = toks(2)
    junk = ph.sb([128, D], BF16)
    t_junk = Tok()
    stat = [ph.sb([128, 4], F32) for _ in range(2)]
    t_stat = toks(2)
    gA = ph.sb([128, 8], F32)
    t_gA = Tok()
    wst = [ph.sb([128, 8 * 384], F32) for _ in range(2)]
    t_wst = toks(2)
    wbf = [ph.sb([128, 8, 384], BF16) for _ in range(2)]
    t_wbf = toks(2)
    QT2 = [ph.sb([128, T], BF16) for _ in range(2)]
    KT2 = [ph.sb([128, T], BF16) for _ in range(2)]
    V2 = [ph.sb([128, NT, 128], BF16) for _ in range(2)]
    t_QT2, t_KT2, t_V2 = toks(2), toks(2), toks(2)
    eb = [ph.sb([128, 512], F32) for _ in range(3)]
    t_e = toks(3)
    SPt = [ph.sb([128, 512], BF16) for _ in range(4)]
    t_S = toks(4)
    Xb = [ph.sb([128, 512], F32) for _ in range(3)]
    t_X = toks(3)
    wt = [ph.sb([128, 512], BF16) for _ in range(4)]
    t_W = toks(4)
    Racc = [ph.sb([128, 512], F32) for _ in range(2)]
    t_R = toks(2)
    oS = [ph.sb([128, 512], BF16) for _ in range(2)]
    t_oS = toks(2)
    ps = ph.psum()
    psbf = ps.bitcast(BF16)
    t_bank = toks(8)
    t_Oh = toks(2)
    t_od = Tok()

    def bank(b):
        return ps[:, 512 * b:512 * (b + 1)]

    dma(p, gA[:], C.a_norm_g.ap()[l], [], [t_gA])
    def p1_norm(i):
        b = i % 2
        dma(p, hs[b][:], C.h[128 * i:128 * (i + 1), :], [], [t_hs[b]])
        norm_tile(p, hs[b][:], t_hs[b], hnb[b][:], t_hnb[b], stat[b], t_stat[b], junk[:], t_junk)

    p1_norm(0)
    for i in range(NT):
        b = i % 2
        if i + 1 < NT:
            p1_norm(i + 1)
        transpose8(p, hnb[b], t_hnb[b], psbf[:, 7168:8192], t_bank[7], ident, t_c,
                   hnT[:, :, 128 * i:128 * (i + 1)], t_hnT[i], ev="dve")
    wq = C.a_w_qkv.ap()[l]

    def proj_units(g):
        wb_ = g % 2
        QT, KT, V = QT2[g % 2], KT2[g % 2], V2[g % 2]
        t_QT, t_KT, t_V = t_QT2[g % 2], t_KT2[g % 2], t_V2[g % 2]
        units = []

        def u_w():
            for sec in range(3):
                dma(p, wst[wb_][:].rearrange("p (k s n) -> p k s n", k=8, s=3)[:, :, sec, :],
                    wq[:, sec * D + g * 128: sec * D + (g + 1) * 128].rearrange("(kc p) n -> p kc n", p=128),
                    [], [t_wst[wb_]])
            for kc in range(8):
                ts(p, "dve", wbf[wb_][:, kc, :], wst[wb_][:, kc * 384:(kc + 1) * 384], gA[:, kc:kc + 1], ALU.mult,
                   [t_wst[wb_], t_gA], [t_wbf[wb_]])
        units.append(u_w)

        def mk_qk(sec, dst, t_dst, scl, t0, W):
            def u():
                for kc in range(8):
                    mm(p, bank(7)[:, 0:W], wbf[wb_][:, kc, sec * 128:(sec + 1) * 128], hnT[:, kc, t0:t0 + W],
                       kc == 0, kc == 7, [t_wbf[wb_]] + t_hnT[t0 // 128:(t0 + W) // 128], [t_bank[7]])
                if scl is None:
                    cp(p, "dve", dst[:, t0:t0 + W], bank(7)[:, 0:W], [t_bank[7]], [t_dst])
                else:
                    ts(p, "dve", dst[:, t0:t0 + W], bank(7)[:, 0:W], scl, ALU.mult, [t_bank[7]], [t_dst])
            return u

        def mk_v(q4):
            def u():
                tiles = list(range(4 * q4, min(NT, 4 * q4 + 4)))
                for j, i in enumerate(tiles):
                    for kc in range(8):
                        mm(p, bank(7)[:, 128 * j:128 * (j + 1)], hnT[:, kc, 128 * i:128 * (i + 1)],
                           wbf[wb_][:, kc, 256:384], kc == 0 and j == 0, kc == 7 and j == len(tiles) - 1,
                           [t_wbf[wb_], t_hnT[i]], [t_bank[7]])
                n = len(tiles)
                cp(p, "dve", V[:, tiles[0]:tiles[0] + n, :],
                   bank(7)[:, 0:128 * n].rearrange("p (a b) -> p a b", b=128), [t_bank[7]], [t_V])
            return u

        for (t0, W) in CHUNKS:
            units.append(mk_qk(0, QT, t_QT, 0.125, t0, W))
            units.append(mk_qk(1, KT, t_KT, None, t0, W))
        for q4 in range(9):
            units.append(mk_v(q4))
        return units

    for u in proj_units(0):
        u()
    for g in range(8):
        QT, KT, V = QT2[g % 2], KT2[g % 2], V2[g % 2]
        t_QT, t_KT, t_V = t_QT2[g % 2], t_KT2[g % 2], t_V2[g % 2]
        nxt = proj_units(g + 1) if g + 1 < 8 else []
        jobs = []
        for ci, (t0, W) in enumerate(CHUNKS_A):
            imax = (t0 + W + 127) // 128 - 1
            for i in range(imax, -1, -1):
                for hh in range(2):
                    jobs.append((hh, ci, t0, W, imax, i))

        def stA(n, job):
            hh, ci, t0, W, imax, i = job
            off = max(0, 128 * i - t0)
            diag = 128 * i - t0 >= 0
            pb = 64 * hh
            zb, e_, s_ = n % 2, n % 3, n % 4
            if i == imax:
                p.op("pool", lambda e: e.memset(Racc[hh][:, 0:W], 0.0), [], [t_R[hh]])
            mm(p, bank(zb)[:, off:W], KT[pb:pb + 64, 128 * i:128 * (i + 1)], QT[pb:pb + 64, t0 + off:t0 + W],
               True, True, [t_KT, t_QT], [t_bank[zb]])
            act(p, eb[e_][:, off:W], bank(zb)[:, off:W], AF.Exp, [t_bank[zb]], [t_e[e_]])
            act(p, SPt[s_][:, off:W], eb[e_][:, off:W], AF.Ln, [t_e[e_]], [t_S[s_]], bias=1.0)
            if diag:
                hi = min(off + 128, W)
                tt(p, "pool", SPt[s_][:, off:hi], SPt[s_][:, off:hi], maskS[:, 0:hi - off], ALU.mult,
                   [t_S[s_], t_c], [t_S[s_]])

        def stB(n, job):
            hh, ci, t0, W, imax, i = job
            off = max(0, 128 * i - t0)
            zb, s_, x_, cb = 2 + n % 2, n % 4, n % 3, 4 + n % 2
            last = i == 0
            pb = 64 * hh
            mm(p, bank(zb)[:, off:W], KT[pb:pb + 64, 128 * i:128 * (i + 1)], QT[pb:pb + 64, t0 + off:t0 + W],
               True, False, [t_KT, t_QT], [t_bank[zb]])
            mm(p, bank(zb)[:, off:W], negtri, SPt[s_][:, off:W], False, True, [t_S[s_], t_c], [t_bank[zb]])
            if not last:
                mm(p, bank(cb)[:, off:W], ones, SPt[s_][:, off:W], True, True, [t_S[s_], t_c], [t_bank[cb]])
            tt(p, "dve", Xb[x_][:, off:W], bank(zb)[:, off:W], Racc[hh][:, off:W], ALU.subtract,
               [t_bank[zb], t_R[hh]], [t_X[x_]])
            if not last:
                tt(p, "dve", Racc[hh][:, off:W], bank(cb)[:, off:W], Racc[hh][:, off:W], ALU.add,
                   [t_bank[cb], t_R[hh]], [t_R[hh]])

        def stC(n, job):
            hh, ci, t0, W, imax, i = job
            off = max(0, 128 * i - t0)
            diag = 128 * i - t0 >= 0
            x_, w_ = n % 3, n % 4
            act(p, wt[w_][:, off:W], Xb[x_][:, off:W], AF.Exp, [t_X[x_]], [t_W[w_]])
            if diag:
                hi = min(off + 128, W)
                tt(p, "pool", wt[w_][:, off:hi], wt[w_][:, off:hi], maskS[:, 0:hi - off], ALU.mult,
                   [t_W[w_], t_c], [t_W[w_]])

        def stD(n, job):
            hh, ci, t0, W, imax, i = job
            off = max(0, 128 * i - t0)
            pb = 64 * hh
            w_ = n % 4
            mm(p, bank(6)[pb:pb + 64, off:W], V[:, i, pb:pb + 64], wt[w_][:, off:W], i == imax, i == 0,
               [t_W[w_], t_V], [t_Oh[hh]], skip=(i != imax and 128 * i - t0 >= 0))
            if i == 0:
                ob = ci % 2
                cp(p, "dve", oS[ob][pb:pb + 64, 0:W], bank(6)[pb:pb + 64, 0:W], [t_Oh[hh]], [t_oS[ob]])
                row = (2 * g + hh) * 64
                dma(p, C.oT[row:row + 64, t0:t0 + W], oS[ob][pb:pb + 64, 0:W], [t_oS[ob]], [Tok()])

        nj = len(jobs)
        step = max(1, (nj - 20) // max(1, len(nxt)))
        ui = 0
        for n in range(nj + 2):
            if n < nj:
                stA(n, jobs[n])
            if 0 <= n - 1 < nj:
                stB(n - 1, jobs[n - 1])
                stC(n - 1, jobs[n - 1])
            if 0 <= n - 2 < nj:
                stD(n - 2, jobs[n - 2])
            if ui < len(nxt) and n >= 4 and (n - 4) % step == 0:
                nxt[ui]()
                ui += 1
        while ui < len(nxt):
            nxt[ui]()
            ui += 1
    ph.finish()


def phase_oproj(C, w_o):
    ph = Phase(C, f"op{C.uid()}")
    p = ph.p
    Wo = ph.sb([128, 8, D], BF16)
    t_Wo = Tok()
    stg = [(ph.sb([128, 2048], F32), Tok()) for _ in range(3)]
    load_w(ph, stg, Wo, t_Wo, w_o, 8, D, engs=("dve", "act"))
    oTs = [ph.sb([128, 8, 512], BF16) for _ in range(2)]
    t_oTs = toks(2)
    hs = [ph.sb([128, D], F32) for _ in range(4)]
    t_hs = toks(4)
    ps = ph.psum()
    t_bank = toks(8)
    t_hd = toks(NT)
    nb = 0
    for ci, (t0, W) in enumerate(CHUNKS):
        ob = ci % 2
        dma(p, oTs[ob][:, :, 0:W], C.oT[:, t0:t0 + W].rearrange("(kc p) t -> p kc t", p=128), [], [t_oTs[ob]])
        for j in range(W // 128):
            i = t0 // 128 + j
            hb = i % 4
            dma(p, hs[hb][:], C.h[128 * i:128 * (i + 1), :], [t_hd[i]], [t_hs[hb]])
            for nck in range(2):
                b = nb % 8
                nb += 1
                for kc in range(8):
                    mm(p, ps[:, 512 * b:512 * (b + 1)], oTs[ob][:, kc, 128 * j:128 * (j + 1)],
                       Wo[:, kc, 512 * nck:512 * (nck + 1)], kc == 0, kc == 7, [t_oTs[ob], t_Wo], [t_bank[b]])
                tt(p, "dve", hs[hb][:, 512 * nck:512 * (nck + 1)], ps[:, 512 * b:512 * (b + 1)],
                   hs[hb][:, 512 * nck:512 * (nck + 1)], ALU.add, [t_bank[b], t_hs[hb]], [t_hs[hb]])
            dma(p, C.h[128 * i:128 * (i + 1), :], hs[hb][:], [t_hs[hb]], [t_hd[i]])
    ph.finish()


def phase_mlp(C, l, final):
    ph = Phase(C, f"mlp{l}")
    p = ph.p
    cst, t_c = ph.consts()
    ident = cst[:, 0, :]
    W1 = ph.sb([128, 8, DFF], BF16)
    W2 = ph.sb([128, 32, D], BF16)
    t_W1, t_W2 = Tok(), Tok()
    gM = ph.sb([128, 8], F32)
    t_gM = Tok()
    dma(p, gM[:], C.mlp_norm_g.ap()[l], [], [t_gM])
    stg = [(ph.sb([128, 2048], F32), Tok()) for _ in range(3)]
    load_w(ph, stg, W1, t_W1, C.mlp_w_in.ap()[l], 8, DFF, gain=gM, gain_tok=t_gM, engs=("dve", "act"))
    load_w(ph, stg, W2, t_W2, C.mlp_w_out.ap()[l], 32, D, engs=("dve", "act"))
    hs = [ph.sb([128, D], F32) for _ in range(4)]
    t_hs = toks(4)
    hnb = [ph.sb([128, D], BF16) for _ in range(2)]
    t_hnb = toks(2)
    junk = ph.sb([128, D], BF16)
    t_junk = Tok()
    stat = [ph.sb([128, 4], F32) for _ in range(2)]
    t_stat = toks(2)
    hnTc = [ph.sb([128, 8, 256], BF16) for _ in range(2)]
    t_hnTc = toks(2)
    aT = ph.sb([128, 32, 256], BF16)
    t_aT = Tok()
    rb = [ph.sb([128, 256], F32) for _ in range(3)]
    t_rb = toks(3)
    ps = ph.psum()
    psbf = ps.bitcast(BF16)
    t_bank = toks(8)
    t_hd = Tok()
    nchunk = (NT + 1) // 2
    nbc = [0]

    def tiles_of(c):
        return list(range(2 * c, min(NT, 2 * c + 2)))

    def stA1(c):
        for j, i in enumerate(tiles_of(c)):
            hb = i % 4
            b2 = i % 2
            dma(p, hs[hb][:], C.h[128 * i:128 * (i + 1), :], [], [t_hs[hb]])
            norm_tile(p, hs[hb][:], t_hs[hb], hnb[b2][:], t_hnb[b2], stat[b2], t_stat[b2], junk[:], t_junk)

    def stA2(c):
        cb = c % 2
        for j, i in enumerate(tiles_of(c)):
            b2 = i % 2
            transpose8(p, hnb[b2], t_hnb[b2], psbf[:, 7168:8192], t_bank[7], ident, t_c,
                       hnTc[cb][:, :, 128 * j:128 * (j + 1)], t_hnTc[cb], ev="dve")

    def stB(c):
        cb = c % 2
        W = 128 * len(tiles_of(c))
        for f in range(32):
            b = nbc[0] % 7
            nbc[0] += 1
            r_ = f % 3
            for kc in range(8):
                mm(p, ps[:, 512 * b:512 * b + W], W1[:, kc, 128 * f:128 * (f + 1)], hnTc[cb][:, kc, 0:W],
                   kc == 0, kc == 7, [t_W1, t_hnTc[cb]], [t_bank[b]])
            act(p, rb[r_][:, 0:W], ps[:, 512 * b:512 * b + W], AF.Relu, [t_bank[b]], [t_rb[r_]])
            tt(p, "pool" if f % 2 else "dve", aT[:, f, 0:W], rb[r_][:, 0:W], rb[r_][:, 0:W], ALU.mult,
               [t_rb[r_]], [t_aT])

    def stC(c):
        for j, i in enumerate(tiles_of(c)):
            hb = i % 4
            for nck in range(2):
                b = nbc[0] % 7
                nbc[0] += 1
                for f in range(32):
                    mm(p, ps[:, 512 * b:512 * (b + 1)], aT[:, f, 128 * j:128 * (j + 1)],
                       W2[:, f, 512 * nck:512 * (nck + 1)], f == 0, f == 31, [t_aT, t_W2], [t_bank[b]])
                tt(p, "dve", hs[hb][:, 512 * nck:512 * (nck + 1)], ps[:, 512 * b:512 * (b + 1)],
                   hs[hb][:, 512 * nck:512 * (nck + 1)], ALU.add, [t_bank[b], t_hs[hb]], [t_hs[hb]])
            if not final:
                dma(p, C.h[128 * i:128 * (i + 1), :], hs[hb][:], [t_hs[hb]], [Tok()])
            else:
                lo = max(128 * i, NM)
                hi = min(128 * (i + 1), NM + S)
                if hi > lo:
                    dma(p, C.out.ap()[lo - NM:hi - NM, :], hs[hb][lo - 128 * i:hi - 128 * i, :], [t_hs[hb]], [Tok()])

    stA1(0)
    stA2(0)
    for c in range(nchunk):
        if c + 1 < nchunk:
            stA1(c + 1)
        stB(c)
        if c + 1 < nchunk:
            stA2(c + 1)
        stC(c)
    ph.finish()


def phase_post(C, l, w_o, final):
    ph = Phase(C, f"post{l}")
    p = ph.p
    cst, t_c = ph.consts()
    ident = cst[:, 0, :]
    W1 = ph.sb([128, 8, DFF], BF16)
    W2 = ph.sb([128, 32, D], BF16)
    t_W1, t_W2 = Tok(), Tok()
    gM = ph.sb([128, 8], F32)
    t_gM = Tok()
    dma(p, gM[:], C.mlp_norm_g.ap()[l], [], [t_gM])
    Wo = ph.sb([128, 8, D], BF16)
    t_Wo = Tok()
    stg = [(ph.sb([128, 512], F32), Tok()) for _ in range(4)]
    load_w(ph, stg, Wo, t_Wo, w_o, 8, D, engs=("act",), cap=512, q="act")
    load_w(ph, stg, W1, t_W1, C.mlp_w_in.ap()[l], 8, DFF, gain=gM, gain_tok=t_gM, engs=("act",), cap=512, q="act")
    load_w(ph, stg, W2, t_W2, C.mlp_w_out.ap()[l], 32, D, engs=("act",), cap=512, q="act")
    hs = [ph.sb([128, D], F32) for _ in range(4)]
    t_hs = toks(4)
    hnb = [ph.sb([128, D], BF16) for _ in range(2)]
    t_hnb = toks(2)
    oTs = [ph.sb([128, 8, 128], BF16) for _ in range(2)]
    t_oTs = toks(2)
    t_hd = toks(NT)
    stat = [ph.sb([128, 4], F32) for _ in range(2)]
    t_stat = toks(2)
    hnTc = [ph.sb([128, 8, 256], BF16) for _ in range(2)]
    t_hnTc = toks(2)
    aT = ph.sb([128, 32, 256], BF16)
    t_aT = Tok()
    rb = [ph.sb([128, 256], F32) for _ in range(3)]
    t_rb = toks(3)
    ps = ph.psum()
    psbf = ps.bitcast(BF16)
    t_bank = toks(8)
    nchunk = (NT + 1) // 2
    nbc = [0]

    def tiles_of(c):
        return list(range(2 * c, min(NT, 2 * c + 2)))

    def stA1(c):
        for j, i in enumerate(tiles_of(c)):
            hb = i % 4
            b2 = i % 2
            dma(p, hs[hb][:], C.h[128 * i:128 * (i + 1), :], [t_hd[i]], [t_hs[hb]])
            norm_tile(p, hs[hb][:], t_hs[hb], hnb[b2][:], t_hnb[b2], stat[b2], t_stat[b2], hnb[b2][:], t_hnb[b2])

    def stA2(c):
        cb = c % 2
        for j, i in enumerate(tiles_of(c)):
            b2 = i % 2
            transpose8(p, hnb[b2], t_hnb[b2], psbf[:, 7168:8192], t_bank[7], ident, t_c,
                       hnTc[cb][:, :, 128 * j:128 * (j + 1)], t_hnTc[cb], ev="dve")

    def stB(c):
        cb = c % 2
        W = 128 * len(tiles_of(c))
        for f in range(32):
            b = nbc[0] % 7
            nbc[0] += 1
            r_ = f % 3
            for kc in range(8):
                mm(p, ps[:, 512 * b:512 * b + W], W1[:, kc, 128 * f:128 * (f + 1)], hnTc[cb][:, kc, 0:W],
                   kc == 0, kc == 7, [t_W1, t_hnTc[cb]], [t_bank[b]])
            act(p, rb[r_][:, 0:W], ps[:, 512 * b:512 * b + W], AF.Relu, [t_bank[b]], [t_rb[r_]])
            tt(p, "pool" if f % 2 else "dve", aT[:, f, 0:W], rb[r_][:, 0:W], rb[r_][:, 0:W], ALU.mult,
               [t_rb[r_]], [t_aT])

    def stC(c):
        for j, i in enumerate(tiles_of(c)):
            hb = i % 4
            for nck in range(2):
                b = nbc[0] % 7
                nbc[0] += 1
                for f in range(32):
                    mm(p, ps[:, 512 * b:512 * (b + 1)], aT[:, f, 128 * j:128 * (j + 1)],
                       W2[:, f, 512 * nck:512 * (nck + 1)], f == 0, f == 31, [t_aT, t_W2], [t_bank[b]])
                tt(p, "dve", hs[hb][:, 512 * nck:512 * (nck + 1)], ps[:, 512 * b:512 * (b + 1)],
                   hs[hb][:, 512 * nck:512 * (nck + 1)], ALU.add, [t_bank[b], t_hs[hb]], [t_hs[hb]])
            if not final:
                dma(p, C.h[128 * i:128 * (i + 1), :], hs[hb][:], [t_hs[hb]], [Tok()])
            else:
                lo = max(128 * i, NM)
                hi = min(128 * (i + 1), NM + S)
                if hi > lo:
                    dma(p, C.out.ap()[lo - NM:hi - NM, :], hs[hb][lo - 128 * i:hi - 128 * i, :], [t_hs[hb]], [Tok()])

    def op_ld(i):
        dma(p, oTs[i % 2][:], C.oT[:, 128 * i:128 * (i + 1)].rearrange("(kc p) t -> p kc t", p=128), [], [t_oTs[i % 2]])
        dma(p, hs[i % 4][:], C.h[128 * i:128 * (i + 1), :], [], [t_hs[i % 4]])

    op_ld(0)
    for i in range(NT):
        if i + 1 < NT:
            op_ld(i + 1)
        hb = i % 4
        for nck in range(2):
            b = nbc[0] % 7
            nbc[0] += 1
            for kc in range(8):
                mm(p, ps[:, 512 * b:512 * (b + 1)], oTs[i % 2][:, kc, :], Wo[:, kc, 512 * nck:512 * (nck + 1)],
                   kc == 0, kc == 7, [t_oTs[i % 2], t_Wo], [t_bank[b]])
            tt(p, "dve", hs[hb][:, 512 * nck:512 * (nck + 1)], ps[:, 512 * b:512 * (b + 1)],
               hs[hb][:, 512 * nck:512 * (nck + 1)], ALU.add, [t_bank[b], t_hs[hb]], [t_hs[hb]])
        dma(p, C.h[128 * i:128 * (i + 1), :], hs[hb][:], [t_hs[hb]], [t_hd[i]])

    stA1(0)
    stA2(0)
    for c in range(nchunk):
        if c + 1 < nchunk:
            stA1(c + 1)
        stB(c)
        if c + 1 < nchunk:
            stA2(c + 1)
        stC(c)
    ph.finish()


def qk_path(p, q, b, pin, t_pin, cs_i):
    sq, ss, kn = q.sq[b], q.ss[b], q.kn[b]
    t_sq, t_ss, t_kn = q.t_sq[b], q.t_ss[b], q.t_kn[b]
    gqk, t_gqk = q.gqk, q.t_gqk
    pv = pin.rearrange("p (g d) -> p g d", d=64)
    act(p, sq[:, 0:512], pin[:, 0:512], AF.Square, t_pin, [t_sq])
    act(p, sq[:, 512:1024], pin[:, 512:1024], AF.Square, t_pin, [t_sq])
    p.op("dve", lambda e: e.tensor_reduce(out=ss[:, 0:16], in_=sq[:].rearrange("p (g d) -> p g d", d=64),
                                          axis=AX.X, op=ALU.add), [t_sq], [t_ss])
    act(p, ss[:, 16:32], ss[:, 0:16], AF.Ln, [t_ss], [t_ss], scale=1.0 / 64, bias=EPS)
    act(p, ss[:, 32:48], ss[:, 16:32], AF.Exp, [t_ss], [t_ss], scale=-0.5)
    rb_ = bass.AP(ss, 32, [[48, 128], [1, 16], [0, 64]])
    knv = kn[:].rearrange("p (g d) -> p g d", d=64)
    tt(p, "dve", knv, pv, rb_, ALU.mult, t_pin + [t_ss], [t_kn])
    kn8 = kn[:].rearrange("p (h d) -> p h d", d=128)
    gb_ = bass.AP(gqk, 0, [[128, 128], [0, 8], [1, 128]])
    tt(p, "pool", kn8, kn8, gb_, ALU.mult, [t_kn, t_gqk], [t_kn])


def qk_path2(p, q, b, cs_i):
    kn, kb, tmp = q.kn[b], q.kb[b], q.tmp[b]
    t_kn, t_kb, t_tmp = q.t_kn[b], q.t_kb[b], q.t_tmp[b]
    t_cs = q.t_cs
    knv = kn[:].rearrange("p (g d) -> p g d", d=64)
    cp(p, "act", kb[:], kn[:], [t_kn], [t_kb])
    x1 = knv[:, :, 0:8]
    x2 = knv[:, :, 8:16]
    cb_ = cs_i[:, 0:8]
    sb_ = cs_i[:, 8:16]
    cB = bass.AP(cb_.tensor, cb_.offset, [list(cb_.ap[0]), [0, 16], [1, 8]])
    sB = bass.AP(sb_.tensor, sb_.offset, [list(sb_.ap[0]), [0, 16], [1, 8]])
    tv = tmp[:].rearrange("p (a g d) -> p a g d", a=4, d=8)
    tt(p, "dve", tv[:, 0], x1, cB, ALU.mult, [t_kn, t_cs], [t_tmp])
    tt(p, "pool", tv[:, 1], x2, sB, ALU.mult, [t_kn, t_cs], [t_tmp])
    tt(p, "dve", tv[:, 2], x2, cB, ALU.mult, [t_kn, t_cs], [t_tmp])
    tt(p, "pool", tv[:, 3], x1, sB, ALU.mult, [t_kn, t_cs], [t_tmp])
    kbv = kb[:].rearrange("p (g d) -> p g d", d=64)
    tt(p, "dve", kbv[:, :, 0:8], tv[:, 0], tv[:, 1], ALU.subtract, [t_tmp, t_kb], [t_kb])
    tt(p, "dve", kbv[:, :, 8:16], tv[:, 2], tv[:, 3], ALU.add, [t_tmp, t_kb], [t_kb])


def alloc_qk(ph):
    d = Ctx()
    d.sq = [ph.sb([128, D], F32) for _ in range(2)]
    d.ss = [ph.sb([128, 48], F32) for _ in range(2)]
    d.kn = [ph.sb([128, D], F32) for _ in range(2)]
    d.kb = [ph.sb([128, D], BF16) for _ in range(2)]
    d.tmp = [ph.sb([128, 4 * 16 * 8], F32) for _ in range(2)]
    d.t_sq, d.t_ss, d.t_kn, d.t_tmp, d.t_kb = toks(2), toks(2), toks(2), toks(2), toks(2)
    d.gqk = ph.sb([128, 128], F32)
    d.t_gqk = Tok()
    d.cs = ph.sb([128, NT, 16], F32)
    d.t_cs = Tok()
    dma(ph.p, d.cs[:], ph.C.rope.ap().rearrange("(i p) c -> p i c", p=128), [], [d.t_cs])
    return d


def proj_qk_tiles(ph, C, q, W, t_W, ident, t_c, ps, psbf, t_bank, sink, vproj=None):
    p = ph.p
    hs = [ph.sb([128, D], F32) for _ in range(2)]
    t_hs = toks(2)
    hnb = [ph.sb([128, D], BF16) for _ in range(2)]
    t_hnb = toks(2)
    junk = ph.sb([128, D], BF16)
    t_junk = Tok()
    stat = [ph.sb([128, 4], F32) for _ in range(2)]
    t_stat = toks(2)
    hnTt = [ph.sb([128, 8, 128], BF16) for _ in range(2)]
    t_hnTt = toks(2)
    def ld(i):
        dma(p, hs[i % 2][:], C.h[128 * i:128 * (i + 1), :], [], [t_hs[i % 2]])

    def g0(i):
        b = i % 2
        norm_tile(p, hs[b][:], t_hs[b], hnb[b][:], t_hnb[b], stat[b], t_stat[b], junk[:], t_junk)

    def g1(i):
        b = i % 2
        transpose8(p, hnb[b], t_hnb[b], psbf[:, 7168:8192], t_bank[7], ident, t_c, hnTt[b][:], t_hnTt[b], ev="dve")

    def g2(i):
        b = i % 2
        kb0 = 2 * (i % 2)
        for nck in range(2):
            for kc in range(8):
                mm(p, ps[:, 512 * (kb0 + nck):512 * (kb0 + nck + 1)], hnTt[b][:, kc, :],
                   W[:, kc, 512 * nck:512 * (nck + 1)], kc == 0, kc == 7, [t_hnTt[b], t_W],
                   [t_bank[kb0], t_bank[kb0 + 1]])
        if vproj is not None:
            vproj(i, b, hnTt[b], t_hnTt[b])

    def g3(i):
        b = i % 2
        kb0 = 2 * (i % 2)
        qk_path(p, q, b, ps[:, 512 * kb0:512 * (kb0 + 2)], [t_bank[kb0], t_bank[kb0 + 1]], q.cs[:, i, :])

    def g4(i):
        qk_path2(p, q, i % 2, q.cs[:, i, :])

    def g5(i):
        sink(i, i % 2)

    stages = [g0, g1, g2, g3, g4, g5]
    ld(0)
    for n in range(NT + len(stages) - 1):
        if n + 1 < NT:
            ld(n + 1)
        for k in range(len(stages) - 1, -1, -1):
            if 0 <= n - k < NT:
                stages[k](n - k)


def phase_kv(C):
    ph = Phase(C, "kv")
    p = ph.p
    cst, t_c = ph.consts()
    ident = cst[:, 0, :]
    Wkv = ph.sb([128, 8, 2048], BF16)
    t_W = Tok()
    gK = ph.sb([128, 8], F32)
    t_gK = Tok()
    dma(p, gK[:], C.kv_norm_g.ap(), [], [t_gK])
    stg = [(ph.sb([128, 2048], F32), Tok()) for _ in range(3)]
    load_w(ph, stg, Wkv, t_W, C.kv_w.ap(), 8, 2048, gain=gK, gain_tok=t_gK, engs=("dve", "act"))
    q = alloc_qk(ph)
    dma(p, q.gqk[:], bass.AP(C.kv_k_norm_g, 0, [[0, 128], [1, 128]]), [], [q.t_gqk])
    KTt = [ph.sb([128, 8, 128], BF16) for _ in range(2)]
    t_KTt = toks(2)
    Vt = [ph.sb([128, D], BF16) for _ in range(2)]
    t_Vt = toks(2)
    ps = ph.psum()
    psbf = ps.bitcast(BF16)
    t_bank = toks(8)
    t_kd, t_vd = Tok(), Tok()

    def sink(i, b):
        transpose8(p, q.kb[b], q.t_kb[b], psbf[:, 6144:7168], t_bank[6], ident, t_c, KTt[b][:], t_KTt[b], ev="act")
        dma(p, C.KTd[:, :, 128 * i:128 * (i + 1)].rearrange("h f t -> f h t"), KTt[b][:], [t_KTt[b]], [Tok()])

    def vproj(i, b, hnTt, t_hnTt):
        for nck in range(2):
            for kc in range(8):
                mm(p, ps[:, 512 * (4 + nck):512 * (5 + nck)], hnTt[:, kc, :],
                   Wkv[:, kc, 1024 + 512 * nck:1024 + 512 * (nck + 1)], kc == 0, kc == 7, [t_hnTt, t_W], [t_bank[4]])
        cp(p, "dve", Vt[b][:], ps[:, 2048:3072], [t_bank[4]], [t_Vt[b]])
        dma(p, C.Vd[128 * i:128 * (i + 1), :], Vt[b][:], [t_Vt[b]], [Tok()])

    proj_qk_tiles(ph, C, q, Wkv, t_W, ident, t_c, ps, psbf, t_bank, sink, vproj)
    ph.finish()


def phase_b_attn(C, l):
    j = l - 2
    nc = C.nc
    lam_init = 0.8 - 0.6 * math.exp(-0.3 * l)
    outer = ExitStack()
    QT = outer.enter_context(nc.sbuf_tensor(f"bq{l}_QT", [128, 8, T], BF16))
    lw = outer.enter_context(nc.sbuf_tensor(f"bq{l}_lw", [128, 136], F32))
    gsub = outer.enter_context(nc.sbuf_tensor(f"bq{l}_gs", [128, 2], F32))
    ph = Phase(C, f"bq{l}")
    p = ph.p
    cst, t_c = ph.consts()
    ident = cst[:, 0, :]
    Wq = ph.sb([128, 8, D], BF16)
    t_Wq = Tok()
    gB = ph.sb([128, 8], F32)
    t_gB = Tok()
    dma(p, gB[:], C.b_norm_g.ap()[j], [], [t_gB])
    stg = [(ph.sb([128, 2048], F32), Tok()) for _ in range(3)]
    load_w(ph, stg, Wq, t_Wq, C.b_w_q.ap()[j], 8, D, gain=gB, gain_tok=t_gB, engs=("dve", "act"))
    q = alloc_qk(ph)
    dma(p, q.gqk[:], bass.AP(C.b_q_norm_g, j * 128, [[0, 128], [1, 128]]), [], [q.t_gqk])
    ts(p, "dve", q.gqk[:], q.gqk[:], 0.125, ALU.mult, [q.t_gqk], [q.t_gqk])
    lpb = ph.sb([128, 256], F32)
    t_l = Tok()
    dma(p, lpb[:], bass.AP(C.b_lambda, j * 256, [[0, 128], [1, 256]]), [], [t_l])
    tt(p, "dve", lw[:, 0:64], lpb[:, 0:64], lpb[:, 64:128], ALU.mult, [t_l], [t_l])
    tt(p, "dve", lw[:, 64:128], lpb[:, 128:192], lpb[:, 192:256], ALU.mult, [t_l], [t_l])
    p.op("dve", lambda e: e.tensor_reduce(out=lw[:, 128:130], in_=lw[:, 0:128].rearrange("p (a b) -> p a b", a=2),
                                          axis=AX.X, op=ALU.add), [t_l], [t_l])
    act(p, lw[:, 130:132], lw[:, 128:130], AF.Exp, [t_l], [t_l])
    tt(p, "dve", lw[:, 132:133], lw[:, 131:132], lw[:, 130:131], ALU.subtract, [t_l], [t_l])
    ts(p, "dve", lw[:, 133:134], lw[:, 132:133], -lam_init, ALU.add, [t_l], [t_l])
    t_gs = Tok()
    dma(p, gsub[:, 0:1], C.b_subln_g.ap()[j].rearrange("(p o) -> p o", o=1), [], [t_gs])
    ts(p, "dve", gsub[:, 1:2], gsub[:, 0:1], 1.0 - lam_init, ALU.mult, [t_gs], [t_gs])
    ps = ph.psum()
    psbf = ps.bitcast(BF16)
    t_bank = toks(8)
    t_QT = Tok()

    def sink(i, b):
        transpose8(p, q.kb[b], q.t_kb[b], psbf[:, 6144:7168], t_bank[6], ident, t_c,
                   QT[:, :, 128 * i:128 * (i + 1)], t_QT, ev="act")

    proj_qk_tiles(ph, C, q, Wq, t_Wq, ident, t_c, ps, psbf, t_bank, sink)
    ph.finish()

    ph = Phase(C, f"bat{l}")
    p = ph.p
    cst, t_c = ph.consts()
    ident = cst[:, 0, :]
    maskI2 = bass.AP(cst, 4 * 128, [[768, 128], [0, 2], [1, 128]])
    neglam = lw[:, 133:134]
    t_QT, t_l = Tok(), Tok()
    WQ = 384
    CH3 = [(WQ * c, WQ) for c in range(T // WQ)]
    gsB = ph.sb([128, 128], F32)
    t_gs = Tok()
    dma(p, gsB[:], bass.AP(C.b_subln_g, j * 128, [[0, 128], [1, 128]]), [], [t_gs])
    ts(p, "dve", gsB[:], gsB[:], 1.0 - lam_init, ALU.mult, [t_gs], [t_gs])
    gsB3 = bass.AP(gsB, 0, [[128, 128], [0, 3], [1, 128]])
    KTh = [ph.sb([128, T], BF16) for _ in range(2)]
    Vh = [ph.sb([128, NT, 129], BF16) for _ in range(2)]
    t_KTh, t_Vh = toks(2), toks(2)
    for hb in range(2):
        p.op("pool", (lambda hb_: lambda e: e.memset(Vh[hb_][:, :, 128:129], 1.0))(hb), [], [t_Vh[hb]])
    Pb = [ph.sb([128, 2, WQ], BF16) for _ in range(5)]
    t_P = toks(5)
    rr = ph.sb([128, 2, 3], F32)
    o1 = ph.sb([128, 3, 128], F32)
    o2 = ph.sb([128, 3, 128], F32)
    sqv = ph.sb([128, 3, 128], F32)
    ssb = ph.sb([128, 12], F32)
    onb = ph.sb([128, 3, 128], BF16)
    oS = [ph.sb([128, WQ], BF16) for _ in range(2)]
    t_ep = Tok()
    t_oS = toks(2)
    ps = ph.psum()
    psbf = ps.bitcast(BF16)
    t_S = toks(2)
    t_PV = [toks(2) for _ in range(2)]
    ns = [0]

    def sset():
        k = ns[0] % 2
        ns[0] += 1
        return k

    def sview(k, W, off=0):
        return ps[:, 1024 * k:1024 * (k + 1)].rearrange("p (m w) -> p m w", m=2)[:, :, off:W]

    def load_head(hd):
        hb = hd % 2
        dma(p, KTh[hb][:], C.KTd[hd], [], [t_KTh[hb]])
        dma(p, Vh[hb][:, :, 0:128], C.Vd[:, 128 * hd:128 * (hd + 1)].rearrange("(i p) d -> p i d", p=128),
            [], [t_Vh[hb]])

    jobs = []
    gci = 0
    for hd in range(8):
        for ci, (t0, W) in enumerate(CH3):
            imax = (t0 + W) // 128 - 1
            for i in range(imax + 1):
                jobs.append((hd, ci, gci, t0, W, imax, i))
            gci += 1

    def s0(n, job):
        hd, ci, g_, t0, W, imax, i = job
        hb = hd % 2
        off = max(0, 128 * i - t0)
        k = sset()
        for m in range(2):
            mm(p, ps[:, 512 * (2 * k + m) + off:512 * (2 * k + m) + W], KTh[hb][64 * m:64 * m + 64, 128 * i:128 * (i + 1)],
               QT[64 * m:64 * m + 64, hd, t0 + off:t0 + W], True, True, [t_KTh[hb], t_QT], [t_S[k]])
        act(p, Pb[n % 5][:, :, off:W], sview(k, W, off), AF.Exp, [t_S[k]], [t_P[n % 5]])
        if 128 * i - t0 >= 0:
            tt(p, "pool", Pb[n % 5][:, :, off:off + 128], Pb[n % 5][:, :, off:off + 128], maskI2, ALU.mult,
               [t_P[n % 5], t_c], [t_P[n % 5]])

    def s1(n, job):
        hd, ci, g_, t0, W, imax, i = job
        hb = hd % 2
        off = max(0, 128 * i - t0)
        pvs = g_ % 2
        for m in range(2):
            b_ = 4 + 2 * pvs + m
            for jb in range(off // 128, 3):
                last_i = t0 // 128 + jb
                mm(p, ps[:, 512 * b_ + 129 * jb:512 * b_ + 129 * (jb + 1)], Pb[n % 5][:, m, 128 * jb:128 * (jb + 1)],
                   Vh[hb][:, i, :], (i == 0 and jb == 0), (i == imax and jb == 2), [t_P[n % 5], t_Vh[hb]], [t_PV[pvs][m]])
            for _d in range(NDUMMY):
                mm(p, ps[:, 512 * b_ + 388:512 * b_ + 508], ident, Pb[n % 5][:, m, 0:120], False, False,
                   [t_P[n % 5], t_c], [t_PV[pvs][m]])
        if i == imax:
            pvv = ps[:, 512 * (4 + 2 * pvs):512 * (6 + 2 * pvs)].rearrange("p (m w) -> p m w", m=2)[:, :, 0:387] \
                .rearrange("p m (j c) -> p m j c", c=129)
            tpv = [t_PV[pvs][0], t_PV[pvs][1]]
            recip(p, rr[:], pvv[:, :, :, 128], tpv + [t_ep], [t_ep])
            ts(p, "dve", rr[:, 1, :], rr[:, 1, :], neglam, ALU.mult, [t_ep, t_l], [t_ep])
            r0 = bass.AP(rr, 0, [[6, 128], [1, 3], [0, 128]])
            r1_ = bass.AP(rr, 3, [[6, 128], [1, 3], [0, 128]])
            tt(p, "dve", o1[:], pvv[:, 0, :, 0:128], r0, ALU.mult, tpv + [t_ep], [t_ep])
            tt(p, "dve", o2[:], pvv[:, 1, :, 0:128], r1_, ALU.mult, tpv + [t_ep], [t_ep])
            tt(p, "pool", o1[:], o1[:], o2[:], ALU.add, [t_ep], [t_ep])
            act(p, sqv[:], o1[:], AF.Square, [t_ep], [t_ep])
            p.op("dve", lambda e: e.tensor_reduce(out=ssb[:, 0:3], in_=sqv[:], axis=AX.X, op=ALU.add), [t_ep], [t_ep])
            act(p, ssb[:, 4:7], ssb[:, 0:3], AF.Ln, [t_ep], [t_ep], scale=1.0 / 128, bias=EPS)
            act(p, ssb[:, 8:11], ssb[:, 4:7], AF.Exp, [t_ep], [t_ep], scale=-0.5)
            rsB = bass.AP(ssb, 8, [[12, 128], [1, 3], [0, 128]])
            tt(p, "dve", o2[:], o1[:], rsB, ALU.mult, [t_ep], [t_ep])
            tt(p, "pool", onb[:], o2[:], gsB3, ALU.mult, [t_ep, t_gs], [t_ep])
            k2 = sset()
            for jb in range(3):
                tr(p, psbf[:, 2048 * k2 + 128 * jb:2048 * k2 + 128 * (jb + 1)], onb[:, jb, :], ident, [t_ep, t_c], [t_S[k2]])
            ob = g_ % 2
            cp(p, "act", oS[ob][:], psbf[:, 2048 * k2:2048 * k2 + 384], [t_S[k2]], [t_oS[ob]])
            dma(p, C.oT[128 * hd:128 * (hd + 1), t0:t0 + W], oS[ob][:], [t_oS[ob]], [Tok()])
            if ci == len(CH3) - 1 and hd + 2 < 8:
                load_head(hd + 2)

    load_head(0)
    load_head(1)
    LAG = 3
    nj = len(jobs)
    for n in range(nj + LAG):
        if n < nj:
            s0(n, jobs[n])
        if n - LAG >= 0:
            s1(n - LAG, jobs[n - LAG])
    ph.finish()
    outer.close()


WNAMES = ["a_norm_g", "a_w_qkv", "a_w_o", "kv_norm_g", "kv_w", "kv_k_norm_g", "b_norm_g", "b_w_q",
          "b_q_norm_g", "b_lambda", "b_subln_g", "b_w_o", "mlp_norm_g", "mlp_w_in", "mlp_w_out"]
WSHAPES = {"a_norm_g": [2, 128, 8], "a_w_qkv": [2, 1024, 3072], "a_w_o": [2, 1024, 1024], "kv_norm_g": [128, 8],
           "kv_w": [1024, 2048], "kv_k_norm_g": [2, 64], "b_norm_g": [2, 128, 8], "b_w_q": [2, 1024, 1024],
           "b_q_norm_g": [2, 2, 64], "b_lambda": [2, 4, 64], "b_subln_g": [2, 128], "b_w_o": [2, 1024, 1024],
           "mlp_norm_g": [4, 128, 8], "mlp_w_in": [4, 1024, 4096], "mlp_w_out": [4, 4096, 1024]}


def build(steps=None, h_in=False, h_out=False):
    nc = bass.Bass("TRN2", target_bir_lowering=False)
    C = Ctx()
    C.nc = nc
    C._uid = [0]
    C.uid = lambda: (C._uid.__setitem__(0, C._uid[0] + 1), C._uid[0])[1]
    C.x = nc.dram_tensor("x", [S, D], F32, kind="ExternalInput")
    C.meta = nc.dram_tensor("meta_tokens", [NM, D], F32, kind="ExternalInput")
    for n in WNAMES:
        setattr(C, n, nc.dram_tensor(n, WSHAPES[n], F32, kind="ExternalInput"))
    C.cst = nc.dram_tensor("cst", [128, 768], BF16, kind="ExternalInput")
    C.rope = nc.dram_tensor("rope", [T, 16], F32, kind="ExternalInput")
    if h_in:
        C.hin = nc.dram_tensor("h_in", [T, D], F32, kind="ExternalInput")
    if h_out:
        C.h = nc.dram_tensor("h_out", [T, D], F32, kind="ExternalOutput").ap()
        C.out = None
    else:
        C.h = nc.dram_tensor("h_res", [T, D], F32).ap()
        C.out = nc.dram_tensor("out", [S, D], F32, kind="ExternalOutput")
    C.oT = nc.dram_tensor("oT_s", [D, T], BF16).ap()
    C.KTd = nc.dram_tensor("KT_s", [8, 128, T], BF16).ap()
    C.Vd = nc.dram_tensor("V_s", [T, D], BF16).ap()
    if steps is None:
        steps = ["init"]
        for l in range(4):
            if l == 2:
                steps.append("kv")
            steps += [f"attn{l}", f"post{l}"]
    zero_oT_pad.C = C
    with ExitStack() as top:
        C.sy = Sync(nc, top)
        for s in steps:
            if s == "init":
                phase_init(C)
            elif s == "copyin":
                ph = Phase(C, "cpin")
                th = Tok()
                zz = ph.sb([128, D], F32)
                tzz = Tok()
                ph.p.op("pool", lambda e: e.memset(zz[:], 0.0), [], [tzz])
                zero_oT_pad(ph.p, zz, tzz)
                for j in range(8):
                    r0, r1 = 528 * j, 528 * (j + 1)
                    dma(ph.p, C.h[r0:r1, :], C.hin.ap()[r0:r1, :], [], [th])
                ph.finish()
            elif s == "kv":
                phase_kv(C)
            elif s.startswith("attn"):
                l = int(s[4:])
                if l < 2:
                    phase_a_attn(C, l)
                else:
                    phase_b_attn(C, l)
            elif s.startswith("post"):
                l = int(s[4:])
                phase_post(C, l, C.a_w_o.ap()[l] if l < 2 else C.b_w_o.ap()[l - 2], final=(l == 3 and not h_out))
            elif s.startswith("oproj"):
                l = int(s[5:])
                phase_oproj(C, C.a_w_o.ap()[l] if l < 2 else C.b_w_o.ap()[l - 2])
            elif s.startswith("mlp"):
                l = int(s[3:])
                phase_mlp(C, l, final=(l == 3 and not h_out))
    return nc


def make_consts():
    j = np.arange(128)[:, None]
    s = np.arange(128)[None, :]
    ident = (j == s).astype(np.float32)
    negtri = -(j >= s).astype(np.float32)
    ones = np.ones((128, 128), np.float32)
    maskS = (j < s).astype(np.float32)
    maskI = (j <= s).astype(np.float32)
    onesS = ones / 128.0
    cst = np.concatenate([ident, negtri, ones, maskS, maskI, onesS], axis=1).astype(ml_dtypes.bfloat16)
    inv_freq = np.power(np.float32(500000.0), -np.arange(0, 16, 2, dtype=np.float32) / np.float32(16))
    ang = np.arange(T, dtype=np.float32)[:, None] * inv_freq[None, :]
    rope = np.concatenate([np.cos(ang), np.sin(ang)], axis=1).astype(np.float32)
    return cst, rope


_NC_CACHE = {}


def kernel(**inputs):
    x = np.ascontiguousarray(inputs["x"], dtype=np.float32)
    B = x.shape[0]
    cst, rope = make_consts()
    if "full" not in _NC_CACHE:
        _NC_CACHE["full"] = build()
    nc = _NC_CACHE["full"]
    shared = {n: np.ascontiguousarray(inputs[n], dtype=np.float32) for n in WNAMES}
    for n in ("a_norm_g", "kv_norm_g", "b_norm_g", "mlp_norm_g"):
        g = shared[n]
        shared[n] = np.ascontiguousarray(g.reshape(g.shape[:-1] + (8, 128)).swapaxes(-1, -2))
    shared["meta_tokens"] = np.ascontiguousarray(inputs["meta_tokens"], dtype=np.float32)
    shared["cst"] = cst
    shared["rope"] = rope
    in_maps = []
    for b in range(B):
        m = dict(shared)
        m["x"] = x[b]
        in_maps.append(m)
    res = run_bass_kernel_spmd(nc, in_maps, core_ids=list(range(B)))
    return np.stack([np.asarray(r["out"], dtype=np.float32) for r in res.results], axis=0)
```

```python
import math
from contextlib import ExitStack

import numpy as np
import ml_dtypes

import concourse.bass as bass
import concourse.mybir as mybir
from concourse.bass_utils import run_bass_kernel_spmd

F32 = mybir.dt.float32
BF16 = mybir.dt.bfloat16
AF = mybir.ActivationFunctionType
ALU = mybir.AluOpType
AX = mybir.AxisListType

D = 1024
S = 4096
NM = 16
T = 4224
NT = 33
DFF = 4096
EPS = 1e-6
CHUNKS = [(0, 128)] + [(128 + 512 * c, 512) for c in range(8)]
CHUNKS_A = [(0, 128)] + [(128 + 512 * c, 512) for c in range(7)] + [(3712, NM + S - 3712)]

ENGS = ("pe", "act", "dve", "pool", "sp")
NRING = 8
NDUMMY = 0
DMAQ = ("sp", "act")


class Tok:
    __slots__ = ("w", "rs")

    def __init__(self):
        self.w = None
        self.rs = []


def toks(n):
    return [Tok() for _ in range(n)]


class Op:
    __slots__ = ("eng", "fn", "deps", "sig", "val", "dma", "ring", "rval", "waits")

    def __init__(self, eng, fn, dma):
        self.eng = eng
        self.fn = fn
        self.deps = ()
        self.sig = False
        self.val = 0
        self.dma = dma
        self.ring = 0
        self.rval = 0
        self.waits = None


class Sync:
    def __init__(self, nc, stack):
        self.sem = {e: stack.enter_context(nc.semaphore(f"s_{e}")) for e in ENGS}
        self.cnt = {e: 0 for e in ENGS}
        self.ring = {q: [stack.enter_context(nc.semaphore(f"ring_{q}{i}")) for i in range(NRING)] for q in DMAQ}
        self.ndma = {q: 0 for q in DMAQ}


class Prog:
    def __init__(self, sync):
        self.sy = sync
        self.ops = []

    def op(self, eng, fn, r=(), w=(), dma=False):
        o = Op(eng, fn, dma)
        deps = set()
        for t in r:
            if t.w is not None:
                deps.add(t.w)
        for t in w:
            if t.w is not None:
                deps.add(t.w)
            deps.update(t.rs)
        o.deps = deps
        for t in r:
            t.rs.append(o)
        for t in w:
            t.w = o
            t.rs = []
        self.ops.append(o)
        return o

    def emit(self, block):
        sy = self.sy
        for o in self.ops:
            for a in o.deps:
                if a.dma:
                    continue
                if a.eng == "pe" and o.eng == "pe":
                    continue
                a.sig = True
        waited = {e: {} for e in ENGS}
        ring_out = {}
        for o in self.ops:
            ws = {}
            wd = waited[o.eng]
            for a in o.deps:
                if a.dma:
                    key = ("r", a.eng, a.ring)
                    v = a.rval
                else:
                    if a.eng == "pe" and o.eng == "pe":
                        continue
                    key = ("e", a.eng)
                    v = a.val
                if wd.get(key, 0) >= v:
                    continue
                if ws.get(key, 0) < v:
                    ws[key] = v
            if o.dma:
                j = sy.ndma[o.eng]
                sy.ndma[o.eng] += 1
                o.ring = j % NRING
                o.rval = 16 * (j // NRING + 1)
                if j >= NRING:
                    key = ("r", o.eng, o.ring)
                    v = 16 * (j // NRING)
                    if wd.get(key, 0) < v and ws.get(key, 0) < v:
                        ws[key] = v
                ring_out[(o.eng, o.ring)] = o.rval
            elif o.sig:
                sy.cnt[o.eng] += 1
                o.val = sy.cnt[o.eng]
            for k, v in ws.items():
                wd[k] = v
            o.waits = list(ws.items())
        final_sp = [(("r", q, r), v) for (q, r), v in ring_out.items() if waited["sp"].get(("r", q, r), 0) < v]
        per = {e: [o for o in self.ops if o.eng == e] for e in ENGS}

        def semof(key):
            return sy.ring[key[1]][key[2]] if key[0] == "r" else sy.sem[key[1]]

        def run(e, eng):
            for o in per[e]:
                for k, v in o.waits:
                    eng.wait_ge(semof(k), v)
                ins = o.fn(eng)
                if o.dma:
                    ins.then_inc(sy.ring[e][o.ring], 16)
                elif o.sig:
                    ins.then_inc(sy.sem[e], 1)
            if e == "sp":
                for k, v in final_sp:
                    eng.wait_ge(semof(k), v)

        block.tensor(lambda eng: run("pe", eng))
        block.scalar(lambda eng: run("act", eng))
        block.vector(lambda eng: run("dve", eng))
        block.gpsimd(lambda eng: run("pool", eng))
        block.sync(lambda eng: run("sp", eng))


def mm(p, out, lhsT, rhs, start, stop, r, w, skip=False):
    if skip:
        p.op("pe", lambda e: e.matmul(out, lhsT=lhsT, rhs=rhs, start=start, stop=stop, skip_group_check=True), r, w)
    else:
        p.op("pe", lambda e: e.matmul(out, lhsT=lhsT, rhs=rhs, start=start, stop=stop), r, w)


def tr(p, out, in_, ident, r, w):
    p.op("pe", lambda e: e.transpose(out, in_, ident), r, w)


def act(p, out, in_, func, r, w, **kw):
    p.op("act", lambda e: e.activation(out=out, in_=in_, func=func, **kw), r, w)


def tt(p, eng, out, in0, in1, op, r, w):
    p.op(eng, lambda e: e.tensor_tensor(out=out, in0=in0, in1=in1, op=op), r, w)


def ts(p, eng, out, in0, s1, op0, r, w, s2=None, op1=None):
    if op1 is None:
        p.op(eng, lambda e: e.tensor_scalar(out=out, in0=in0, scalar1=s1, scalar2=None, op0=op0), r, w)
    else:
        p.op(eng, lambda e: e.tensor_scalar(out=out, in0=in0, scalar1=s1, scalar2=s2, op0=op0, op1=op1), r, w)


def stt(p, eng, out, in0, scalar, in1, op0, op1, r, w):
    p.op(eng, lambda e: e.scalar_tensor_tensor(out=out, in0=in0, scalar=scalar, in1=in1, op0=op0, op1=op1), r, w)


def cp(p, eng, out, in_, r, w, scale=None):
    if eng == "act":
        if scale is None:
            p.op("act", lambda e: e.activation(out=out, in_=in_, func=AF.Copy), r, w)
        else:
            p.op("act", lambda e: e.activation(out=out, in_=in_, func=AF.Copy, scale=scale), r, w)
    else:
        p.op(eng, lambda e: e.tensor_copy(out=out, in_=in_), r, w)


def recip(p, out, in_, r, w):
    p.op("dve", lambda e: e.reciprocal(out=out, in_=in_), r, w)


def dma(p, out, in_, r, w, q="sp"):
    p.op(q, lambda e: e.dma_start(out=out, in_=in_), r, w, dma=True)


class Ctx:
    pass


class Phase:
    def __init__(self, C, name):
        self.C = C
        self.nc = C.nc
        self.name = name
        self.st = ExitStack()
        self.p = Prog(C.sy)
        self.n = 0

    def sb(self, shape, dt):
        self.n += 1
        return self.st.enter_context(self.nc.sbuf_tensor(f"{self.name}_t{self.n}", shape, dt))

    def psum(self):
        return self.st.enter_context(self.nc.psum_tensor(f"{self.name}_ps", [128, 4096], F32))

    def consts(self):
        c = self.sb([128, 6, 128], BF16)
        t = Tok()
        dma(self.p, c[:], self.C.cst.ap().rearrange("p (a b) -> p a b", a=6), [], [t])
        return c, t

    def finish(self):
        blk = self.st.enter_context(self.nc.Block())
        self.p.emit(blk)
        self.st.close()


def load_w(ph, stg, dst, dst_tok, src, nk, ncols, gain=None, gain_tok=None, engs=("pool",), cnt=[0], cap=2048, q="sp"):
    p = ph.p
    srcv = src.rearrange("(kc p) n -> p kc n", p=128)
    cw = min(ncols, cap)
    kb = max(1, min(nk, cap // cw))
    chunks = []
    for k0 in range(0, nk, kb):
        k1 = min(nk, k0 + kb)
        for c0 in range(0, ncols, cw):
            sa, stok = stg[cnt[0] % len(stg)]
            eng = engs[cnt[0] % len(engs)]
            cnt[0] += 1
            chunks.append((k0, k1, c0, sa, stok, eng))

    def issue(ch):
        k0, k1, c0, sa, stok, eng = ch
        sv = sa[:, 0:(k1 - k0) * cw].rearrange("p (k n) -> p k n", n=cw)
        dma(p, sv, srcv[:, k0:k1, c0:c0 + cw], [], [stok], q=q)

    def conv(ch):
        k0, k1, c0, sa, stok, eng = ch
        sv = sa[:, 0:(k1 - k0) * cw].rearrange("p (k n) -> p k n", n=cw)
        if gain is None:
            cp(p, eng, dst[:, k0:k1, c0:c0 + cw], sv, [stok], [dst_tok])
        else:
            for k in range(k0, k1):
                if eng == "act":
                    act(p, dst[:, k, c0:c0 + cw], sv[:, k - k0, :], AF.Copy, [stok, gain_tok], [dst_tok],
                        scale=gain[:, k:k + 1])
                else:
                    ts(p, eng, dst[:, k, c0:c0 + cw], sv[:, k - k0, :], gain[:, k:k + 1], ALU.mult,
                       [stok, gain_tok], [dst_tok])

    depth = len(stg) - 1
    n = len(chunks)
    for k in range(n + depth):
        if k < n:
            issue(chunks[k])
        if k - depth >= 0:
            conv(chunks[k - depth])


def norm_tile(p, hs, t_hs, hnb, t_hnb, stt_, t_st, junk, t_junk):
    act(p, junk, hs, AF.Square, [t_hs], [t_junk, t_st], accum_out=stt_[:, 0:1])
    act(p, stt_[:, 1:2], stt_[:, 0:1], AF.Ln, [t_st], [t_st], scale=1.0 / D, bias=EPS)
    act(p, stt_[:, 2:3], stt_[:, 1:2], AF.Exp, [t_st], [t_st], scale=-0.5)
    ts(p, "dve", hnb, hs, stt_[:, 2:3], ALU.mult, [t_hs, t_st], [t_hnb])


def transpose8(p, src, t_src, psbf_bank, t_bank, ident, t_c, dst, t_dst, ev="dve"):
    for kc in range(8):
        tr(p, psbf_bank[:, kc * 128:(kc + 1) * 128], src[:, kc * 128:(kc + 1) * 128], ident, [t_src, t_c], [t_bank])
    cp(p, ev, dst, psbf_bank.rearrange("p (a b) -> p a b", a=8), [t_bank], [t_dst])


def zero_oT_pad(p, z, tz, C=None):
    npad = T - NM - S
    zb = z.bitcast(BF16)[:, 0:8 * npad].rearrange("p (k t) -> p k t", k=8)
    dma(p, zero_oT_pad.C.oT[:, NM + S:T].rearrange("(kc p) t -> p kc t", p=128), zb, [tz], [Tok()])


def phase_init(C):
    ph = Phase(C, "init")
    p = ph.p
    z = ph.sb([128, D], F32)
    tz = Tok()
    th = Tok()
    p.op("pool", lambda e: e.memset(z[:], 0.0), [], [tz])
    dma(p, C.h[0:NM, :], C.meta.ap(), [], [th])
    for j in range(8):
        dma(p, C.h[NM + 512 * j:NM + 512 * (j + 1), :], C.x.ap()[512 * j:512 * (j + 1), :], [], [th])
    dma(p, C.h[NM + S:T, :], z[0:T - NM - S, :], [tz], [th])
    zero_oT_pad(p, z, tz)
    ph.finish()


def phase_a_attn(C, l):
    ph = Phase(C, f"aat{l}")
    p = ph.p
    nc = C.nc
    cst, t_c = ph.consts()
    ident, negtri, ones, maskS = cst[:, 0, :], cst[:, 1, :], cst[:, 2, :], cst[:, 3, :]
    hnT = ph.sb([128, 8, T], BF16)
    t_hnT = toks(NT)
    hs = [ph.sb([128, D], F32) for _ in range(2)]
    t_hs = toks(2)
    hnb = [ph.sb([128, D], BF16) for _ in range(2)]
    t_hnb = toks(2)
    junk = ph.sb([128, D], BF16)
    t_junk = Tok()
    stat = [ph.sb([128, 4], F32) for _ in range(2)]
    t_stat = toks(2)
    gA = ph.sb([128, 8], F32)
    t_gA = Tok()
    wst = [ph.sb([128, 8 * 384], F32) for _ in range(2)]
    t_wst = toks(2)
    wbf = [ph.sb([128, 8, 384], BF16) for _ in range(2)]
    t_wbf = toks(2)
    QT2 = [ph.sb([128, T], BF16) for _ in range(2)]
    KT2 = [ph.sb([128, T], BF16) for _ in range(2)]
    V2 = [ph.sb([128, NT, 128], BF16) for _ in range(2)]
    t_QT2, t_KT2, t_V2 = toks(2), toks(2), toks(2)
    eb = [ph.sb([128, 512], F32) for _ in range(3)]
    t_e = toks(3)
    SPt = [ph.sb([128, 512], BF16) for _ in range(4)]
    t_S = toks(4)
    Xb = [ph.sb([128, 512], F32) for _ in range(3)]
    t_X = toks(3)
    wt = [ph.sb([128, 512], BF16) for _ in range(4)]
    t_W = toks(4)
    Racc = [ph.sb([128, 512], F32) for _ in range(2)]
    t_R = toks(2)
    oS = [ph.sb([128, 512], BF16) for _ in range(2)]
    t_oS = toks(2)
    ps = ph.psum()
    psbf = ps.bitcast(BF16)
    t_bank = toks(8)
    t_Oh = toks(2)
    t_od = Tok()

    def bank(b):
        return ps[:, 512 * b:512 * (b + 1)]

    dma(p, gA[:], C.a_norm_g.ap()[l], [], [t_gA])
    def p1_norm(i):
        b = i % 2
        dma(p, hs[b][:], C.h[128 * i:128 * (i + 1), :], [], [t_hs[b]])
        norm_tile(p, hs[b][:], t_hs[b], hnb[b][:], t_hnb[b], stat[b], t_stat[b], junk[:], t_junk)

    p1_norm(0)
    for i in range(NT):
        b = i % 2
        if i + 1 < NT:
            p1_norm(i + 1)
        transpose8(p, hnb[b], t_hnb[b], psbf[:, 7168:8192], t_bank[7], ident, t_c,
                   hnT[:, :, 128 * i:128 * (i + 1)], t_hnT[i], ev="dve")
    wq = C.a_w_qkv.ap()[l]

    def proj_units(g):
        wb_ = g % 2
        QT, KT, V = QT2[g % 2], KT2[g % 2], V2[g % 2]
        t_QT, t_KT, t_V = t_QT2[g % 2], t_KT2[g % 2], t_V2[g % 2]
        units = []

        def u_w():
            for sec in range(3):
                dma(p, wst[wb_][:].rearrange("p (k s n) -> p k s n", k=8, s=3)[:, :, sec, :],
                    wq[:, sec * D + g * 128: sec * D + (g + 1) * 128].rearrange("(kc p) n -> p kc n", p=128),
                    [], [t_wst[wb_]])
            for kc in range(8):
                ts(p, "dve", wbf[wb_][:, kc, :], wst[wb_][:, kc * 384:(kc + 1) * 384], gA[:, kc:kc + 1], ALU.mult,
                   [t_wst[wb_], t_gA], [t_wbf[wb_]])
        units.append(u_w)

        def mk_qk(sec, dst, t_dst, scl, t0, W):
            def u():
                for kc in range(8):
                    mm(p, bank(7)[:, 0:W], wbf[wb_][:, kc, sec * 128:(sec + 1) * 128], hnT[:, kc, t0:t0 + W],
                       kc == 0, kc == 7, [t_wbf[wb_]] + t_hnT[t0 // 128:(t0 + W) // 128], [t_bank[7]])
                if scl is None:
                    cp(p, "dve", dst[:, t0:t0 + W], bank(7)[:, 0:W], [t_bank[7]], [t_dst])
                else:
                    ts(p, "dve", dst[:, t0:t0 + W], bank(7)[:, 0:W], scl, ALU.mult, [t_bank[7]], [t_dst])
            return u

        def mk_v(q4):
            def u():
                tiles = list(range(4 * q4, min(NT, 4 * q4 + 4)))
                for j, i in enumerate(tiles):
                    for kc in range(8):
                        mm(p, bank(7)[:, 128 * j:128 * (j + 1)], hnT[:, kc, 128 * i:128 * (i + 1)],
                           wbf[wb_][:, kc, 256:384], kc == 0 and j == 0, kc == 7 and j == len(tiles) - 1,
                           [t_wbf[wb_], t_hnT[i]], [t_bank[7]])
                n = len(tiles)
                cp(p, "dve", V[:, tiles[0]:tiles[0] + n, :],
                   bank(7)[:, 0:128 * n].rearrange("p (a b) -> p a b", b=128), [t_bank[7]], [t_V])
            return u

        for (t0, W) in CHUNKS:
            units.append(mk_qk(0, QT, t_QT, 0.125, t0, W))
            units.append(mk_qk(1, KT, t_KT, None, t0, W))
        for q4 in range(9):
            units.append(mk_v(q4))
        return units

    for u in proj_units(0):
        u()
    for g in range(8):
        QT, KT, V = QT2[g % 2], KT2[g % 2], V2[g % 2]
        t_QT, t_KT, t_V = t_QT2[g % 2], t_KT2[g % 2], t_V2[g % 2]
        nxt = proj_units(g + 1) if g + 1 < 8 else []
        jobs = []
        for ci, (t0, W) in enumerate(CHUNKS_A):
            imax = (t0 + W + 127) // 128 - 1
            for i in range(imax, -1, -1):
                for hh in range(2):
                    jobs.append((hh, ci, t0, W, imax, i))

        def stA(n, job):
            hh, ci, t0, W, imax, i = job
            off = max(0, 128 * i - t0)
            diag = 128 * i - t0 >= 0
            pb = 64 * hh
            zb, e_, s_ = n % 2, n % 3, n % 4
            if i == imax:
                p.op("pool", lambda e: e.memset(Racc[hh][:, 0:W], 0.0), [], [t_R[hh]])
            mm(p, bank(zb)[:, off:W], KT[pb:pb + 64, 128 * i:128 * (i + 1)], QT[pb:pb + 64, t0 + off:t0 + W],
               True, True, [t_KT, t_QT], [t_bank[zb]])
            act(p, eb[e_][:, off:W], bank(zb)[:, off:W], AF.Exp, [t_bank[zb]], [t_e[e_]])
            act(p, SPt[s_][:, off:W], eb[e_][:, off:W], AF.Ln, [t_e[e_]], [t_S[s_]], bias=1.0)
            if diag:
                hi = min(off + 128, W)
                tt(p, "pool", SPt[s_][:, off:hi], SPt[s_][:, off:hi], maskS[:, 0:hi - off], ALU.mult,
                   [t_S[s_], t_c], [t_S[s_]])

        def stB(n, job):
            hh, ci, t0, W, imax, i = job
            off = max(0, 128 * i - t0)
            zb, s_, x_, cb = 2 + n % 2, n % 4, n % 3, 4 + n % 2
            last = i == 0
            pb = 64 * hh
            mm(p, bank(zb)[:, off:W], KT[pb:pb + 64, 128 * i:128 * (i + 1)], QT[pb:pb + 64, t0 + off:t0 + W],
               True, False, [t_KT, t_QT], [t_bank[zb]])
            mm(p, bank(zb)[:, off:W], negtri, SPt[s_][:, off:W], False, True, [t_S[s_], t_c], [t_bank[zb]])
            if not last:
                mm(p, bank(cb)[:, off:W], ones, SPt[s_][:, off:W], True, True, [t_S[s_], t_c], [t_bank[cb]])
            tt(p, "dve", Xb[x_][:, off:W], bank(zb)[:, off:W], Racc[hh][:, off:W], ALU.subtract,
               [t_bank[zb], t_R[hh]], [t_X[x_]])
            if not last:
                tt(p, "dve", Racc[hh][:, off:W], bank(cb)[:, off:W], Racc[hh][:, off:W], ALU.add,
                   [t_bank[cb], t_R[hh]], [t_R[hh]])

        def stC(n, job):
            hh, ci, t0, W, imax, i = job
            off = max(0, 128 * i - t0)
            diag = 128 * i - t0 >= 0
            x_, w_ = n % 3, n % 4
            act(p, wt[w_][:, off:W], Xb[x_][:, off:W], AF.Exp, [t_X[x_]], [t_W[w_]])
            if diag:
                hi = min(off + 128, W)
                tt(p, "pool", wt[w_][:, off:hi], wt[w_][:, off:hi], maskS[:, 0:hi - off], ALU.mult,
                   [t_W[w_], t_c], [t_W[w_]])
                if off > 0:
                    p.op("pool", (lambda ap: lambda e: e.memset(ap, 0.0))(wt[w_][:, 0:off]), [], [t_W[w_]])

        def stD(n, job):
            hh, ci, t0, W, imax, i = job
            off = max(0, 128 * i - t0)
            pb = 64 * hh
            w_ = n % 4
            mm(p, bank(6)[pb:pb + 64, 0:W], V[:, i, pb:pb + 64], wt[w_][:, 0:W], i == imax, i == 0,
               [t_W[w_], t_V], [t_Oh[hh]])
            if i == 0:
                ob = ci % 2
                cp(p, "dve", oS[ob][pb:pb + 64, 0:W], bank(6)[pb:pb + 64, 0:W], [t_Oh[hh]], [t_oS[ob]])
                row = (2 * g + hh) * 64
                dma(p, C.oT[row:row + 64, t0:t0 + W], oS[ob][pb:pb + 64, 0:W], [t_oS[ob]], [Tok()])

        nj = len(jobs)
        step = max(1, (nj - 20) // max(1, len(nxt)))
        ui = 0
        for n in range(nj + 2):
            if n < nj:
                stA(n, jobs[n])
            if 0 <= n - 1 < nj:
                stB(n - 1, jobs[n - 1])
                stC(n - 1, jobs[n - 1])
            if 0 <= n - 2 < nj:
                stD(n - 2, jobs[n - 2])
            if ui < len(nxt) and n >= 4 and (n - 4) % step == 0:
                nxt[ui]()
                ui += 1
        while ui < len(nxt):
            nxt[ui]()
            ui += 1
    ph.finish()


def phase_oproj(C, w_o):
    ph = Phase(C, f"op{C.uid()}")
    p = ph.p
    Wo = ph.sb([128, 8, D], BF16)
    t_Wo = Tok()
    stg = [(ph.sb([128, 2048], F32), Tok()) for _ in range(3)]
    load_w(ph, stg, Wo, t_Wo, w_o, 8, D, engs=("dve", "act"))
    oTs = [ph.sb([128, 8, 512], BF16) for _ in range(2)]
    t_oTs = toks(2)
    hs = [ph.sb([128, D], F32) for _ in range(4)]
    t_hs = toks(4)
    ps = ph.psum()
    t_bank = toks(8)
    t_hd = toks(NT)
    nb = 0
    for ci, (t0, W) in enumerate(CHUNKS):
        ob = ci % 2
        dma(p, oTs[ob][:, :, 0:W], C.oT[:, t0:t0 + W].rearrange("(kc p) t -> p kc t", p=128), [], [t_oTs[ob]])
        for j in range(W // 128):
            i = t0 // 128 + j
            hb = i % 4
            dma(p, hs[hb][:], C.h[128 * i:128 * (i + 1), :], [t_hd[i]], [t_hs[hb]])
            for nck in range(2):
                b = nb % 8
                nb += 1
                for kc in range(8):
                    mm(p, ps[:, 512 * b:512 * (b + 1)], oTs[ob][:, kc, 128 * j:128 * (j + 1)],
                       Wo[:, kc, 512 * nck:512 * (nck + 1)], kc == 0, kc == 7, [t_oTs[ob], t_Wo], [t_bank[b]])
                tt(p, "dve", hs[hb][:, 512 * nck:512 * (nck + 1)], ps[:, 512 * b:512 * (b + 1)],
                   hs[hb][:, 512 * nck:512 * (nck + 1)], ALU.add, [t_bank[b], t_hs[hb]], [t_hs[hb]])
            dma(p, C.h[128 * i:128 * (i + 1), :], hs[hb][:], [t_hs[hb]], [t_hd[i]])
    ph.finish()


def phase_mlp(C, l, final):
    ph = Phase(C, f"mlp{l}")
    p = ph.p
    cst, t_c = ph.consts()
    ident = cst[:, 0, :]
    W1 = ph.sb([128, 8, DFF], BF16)
    W2 = ph.sb([128, 32, D], BF16)
    t_W1, t_W2 = Tok(), Tok()
    gM = ph.sb([128, 8], F32)
    t_gM = Tok()
    dma(p, gM[:], C.mlp_norm_g.ap()[l], [], [t_gM])
    stg = [(ph.sb([128, 2048], F32), Tok()) for _ in range(3)]
    load_w(ph, stg, W1, t_W1, C.mlp_w_in.ap()[l], 8, DFF, gain=gM, gain_tok=t_gM, engs=("dve", "act"))
    load_w(ph, stg, W2, t_W2, C.mlp_w_out.ap()[l], 32, D, engs=("dve", "act"))
    hs = [ph.sb([128, D], F32) for _ in range(4)]
    t_hs = toks(4)
    hnb = [ph.sb([128, D], BF16) for _ in range(2)]
    t_hnb = toks(2)
    junk = ph.sb([128, D], BF16)
    t_junk = Tok()
    stat = [ph.sb([128, 4], F32) for _ in range(2)]
    t_stat = toks(2)
    hnTc = [ph.sb([128, 8, 256], BF16) for _ in range(2)]
    t_hnTc = toks(2)
    aT = ph.sb([128, 32, 256], BF16)
    t_aT = Tok()
    rb = [ph.sb([128, 256], F32) for _ in range(3)]
    t_rb = toks(3)
    ps = ph.psum()
    psbf = ps.bitcast(BF16)
    t_bank = toks(8)
    t_hd = Tok()
    nchunk = (NT + 1) // 2
    nbc = [0]

    def tiles_of(c):
        return list(range(2 * c, min(NT, 2 * c + 2)))

    def stA1(c):
        for j, i in enumerate(tiles_of(c)):
            hb = i % 4
            b2 = i % 2
            dma(p, hs[hb][:], C.h[128 * i:128 * (i + 1), :], [], [t_hs[hb]])
            norm_tile(p, hs[hb][:], t_hs[hb], hnb[b2][:], t_hnb[b2], stat[b2], t_stat[b2], junk[:], t_junk)

    def stA2(c):
        cb = c % 2
        for j, i in enumerate(tiles_of(c)):
            b2 = i % 2
            transpose8(p, hnb[b2], t_hnb[b2], psbf[:, 7168:8192], t_bank[7], ident, t_c,
                       hnTc[cb][:, :, 128 * j:128 * (j + 1)], t_hnTc[cb], ev="dve")

    def stB(c):
        cb = c % 2
        W = 128 * len(tiles_of(c))
        for f in range(32):
            b = nbc[0] % 7
            nbc[0] += 1
            r_ = f % 3
            for kc in range(8):
                mm(p, ps[:, 512 * b:512 * b + W], W1[:, kc, 128 * f:128 * (f + 1)], hnTc[cb][:, kc, 0:W],
                   kc == 0, kc == 7, [t_W1, t_hnTc[cb]], [t_bank[b]])
            act(p, rb[r_][:, 0:W], ps[:, 512 * b:512 * b + W], AF.Relu, [t_bank[b]], [t_rb[r_]])
            tt(p, "pool" if f % 2 else "dve", aT[:, f, 0:W], rb[r_][:, 0:W], rb[r_][:, 0:W], ALU.mult,
               [t_rb[r_]], [t_aT])

    def stC(c):
        for j, i in enumerate(tiles_of(c)):
            hb = i % 4
            for nck in range(2):
                b = nbc[0] % 7
                nbc[0] += 1
                for f in range(32):
                    mm(p, ps[:, 512 * b:512 * (b + 1)], aT[:, f, 128 * j:128 * (j + 1)],
                       W2[:, f, 512 * nck:512 * (nck + 1)], f == 0, f == 31, [t_aT, t_W2], [t_bank[b]])
                tt(p, "dve", hs[hb][:, 512 * nck:512 * (nck + 1)], ps[:, 512 * b:512 * (b + 1)],
                   hs[hb][:, 512 * nck:512 * (nck + 1)], ALU.add, [t_bank[b], t_hs[hb]], [t_hs[hb]])
            if not final:
                dma(p, C.h[128 * i:128 * (i + 1), :], hs[hb][:], [t_hs[hb]], [Tok()])
            else:
                lo = max(128 * i, NM)
                hi = min(128 * (i + 1), NM + S)
                if hi > lo:
                    dma(p, C.out.ap()[lo - NM:hi - NM, :], hs[hb][lo - 128 * i:hi - 128 * i, :], [t_hs[hb]], [Tok()])

    stA1(0)
    stA2(0)
    for c in range(nchunk):
        if c + 1 < nchunk:
            stA1(c + 1)
        stB(c)
        if c + 1 < nchunk:
            stA2(c + 1)
        stC(c)
    ph.finish()


def phase_post(C, l, w_o, final):
    ph = Phase(C, f"post{l}")
    p = ph.p
    cst, t_c = ph.consts()
    ident = cst[:, 0, :]
    W1 = ph.sb([128, 8, DFF], BF16)
    W2 = ph.sb([128, 32, D], BF16)
    t_W1, t_W2 = Tok(), Tok()
    gM = ph.sb([128, 8], F32)
    t_gM = Tok()
    dma(p, gM[:], C.mlp_norm_g.ap()[l], [], [t_gM])
    Wo = ph.sb([128, 8, D], BF16)
    t_Wo = Tok()
    stg = [(ph.sb([128, 512], F32), Tok()) for _ in range(4)]
    load_w(ph, stg, Wo, t_Wo, w_o, 8, D, engs=("act",), cap=512, q="act")
    load_w(ph, stg, W1, t_W1, C.mlp_w_in.ap()[l], 8, DFF, gain=gM, gain_tok=t_gM, engs=("act",), cap=512, q="act")
    load_w(ph, stg, W2, t_W2, C.mlp_w_out.ap()[l], 32, D, engs=("act",), cap=512, q="act")
    hs = [ph.sb([128, D], F32) for _ in range(4)]
    t_hs = toks(4)
    hnb = [ph.sb([128, D], BF16) for _ in range(2)]
    t_hnb = toks(2)
    oTs = [ph.sb([128, 8, 128], BF16) for _ in range(2)]
    t_oTs = toks(2)
    t_hd = toks(NT)
    stat = [ph.sb([128, 4], F32) for _ in range(2)]
    t_stat = toks(2)
    hnTc = [ph.sb([128, 8, 256], BF16) for _ in range(2)]
    t_hnTc = toks(2)
    aT = ph.sb([128, 32, 256], BF16)
    t_aT = Tok()
    rb = [ph.sb([128, 256], F32) for _ in range(3)]
    t_rb = toks(3)
    ps = ph.psum()
    psbf = ps.bitcast(BF16)
    t_bank = toks(8)
    nchunk = (NT + 1) // 2
    nbc = [0]

    def tiles_of(c):
        return list(range(2 * c, min(NT, 2 * c + 2)))

    def stA1(c):
        for j, i in enumerate(tiles_of(c)):
            hb = i % 4
            b2 = i % 2
            dma(p, hs[hb][:], C.h[128 * i:128 * (i + 1), :], [t_hd[i]], [t_hs[hb]])
            norm_tile(p, hs[hb][:], t_hs[hb], hnb[b2][:], t_hnb[b2], stat[b2], t_stat[b2], hnb[b2][:], t_hnb[b2])

    def stA2(c):
        cb = c % 2
        for j, i in enumerate(tiles_of(c)):
            b2 = i % 2
            transpose8(p, hnb[b2], t_hnb[b2], psbf[:, 7168:8192], t_bank[7], ident, t_c,
                       hnTc[cb][:, :, 128 * j:128 * (j + 1)], t_hnTc[cb], ev="dve")

    def stB(c):
        cb = c % 2
        W = 128 * len(tiles_of(c))
        for f in range(32):
            b = nbc[0] % 7
            nbc[0] += 1
            r_ = f % 3
            for kc in range(8):
                mm(p, ps[:, 512 * b:512 * b + W], W1[:, kc, 128 * f:128 * (f + 1)], hnTc[cb][:, kc, 0:W],
                   kc == 0, kc == 7, [t_W1, t_hnTc[cb]], [t_bank[b]])
            act(p, rb[r_][:, 0:W], ps[:, 512 * b:512 * b + W], AF.Relu, [t_bank[b]], [t_rb[r_]])
            tt(p, "pool" if f % 2 else "dve", aT[:, f, 0:W], rb[r_][:, 0:W], rb[r_][:, 0:W], ALU.mult,
               [t_rb[r_]], [t_aT])

    def stC(c):
        for j, i in enumerate(tiles_of(c)):
            hb = i % 4
            for nck in range(2):
                b = nbc[0] % 7
                nbc[0] += 1
                for f in range(32):
                    mm(p, ps[:, 512 * b:512 * (b + 1)], aT[:, f, 128 * j:128 * (j + 1)],
                       W2[:, f, 512 * nck:512 * (nck + 1)], f == 0, f == 31, [t_aT, t_W2], [t_bank[b]])
                tt(p, "dve", hs[hb][:, 512 * nck:512 * (nck + 1)], ps[:, 512 * b:512 * (b + 1)],
                   hs[hb][:, 512 * nck:512 * (nck + 1)], ALU.add, [t_bank[b], t_hs[hb]], [t_hs[hb]])
            if not final:
                dma(p, C.h[128 * i:128 * (i + 1), :], hs[hb][:], [t_hs[hb]], [Tok()])
            else:
                lo = max(128 * i, NM)
                hi = min(128 * (i + 1), NM + S)
                if hi > lo:
                    dma(p, C.out.ap()[lo - NM:hi - NM, :], hs[hb][lo - 128 * i:hi - 128 * i, :], [t_hs[hb]], [Tok()])

    def op_ld(i):
        dma(p, oTs[i % 2][:], C.oT[:, 128 * i:128 * (i + 1)].rearrange("(kc p) t -> p kc t", p=128), [], [t_oTs[i % 2]])
        dma(p, hs[i % 4][:], C.h[128 * i:128 * (i + 1), :], [], [t_hs[i % 4]])

    op_ld(0)
    for i in range(NT):
        if i + 1 < NT:
            op_ld(i + 1)
        hb = i % 4
        for nck in range(2):
            b = nbc[0] % 7
            nbc[0] += 1
            for kc in range(8):
                mm(p, ps[:, 512 * b:512 * (b + 1)], oTs[i % 2][:, kc, :], Wo[:, kc, 512 * nck:512 * (nck + 1)],
                   kc == 0, kc == 7, [t_oTs[i % 2], t_Wo], [t_bank[b]])
            tt(p, "dve", hs[hb][:, 512 * nck:512 * (nck + 1)], ps[:, 512 * b:512 * (b + 1)],
               hs[hb][:, 512 * nck:512 * (nck + 1)], ALU.add, [t_bank[b], t_hs[hb]], [t_hs[hb]])
        dma(p, C.h[128 * i:128 * (i + 1), :], hs[hb][:], [t_hs[hb]], [t_hd[i]])

    stA1(0)
    stA2(0)
    for c in range(nchunk):
        if c + 1 < nchunk:
            stA1(c + 1)
        stB(c)
        if c + 1 < nchunk:
            stA2(c + 1)
        stC(c)
    ph.finish()


def qk_path(p, q, b, pin, t_pin, cs_i):
    sq, ss, kn = q.sq[b], q.ss[b], q.kn[b]
    t_sq, t_ss, t_kn = q.t_sq[b], q.t_ss[b], q.t_kn[b]
    gqk, t_gqk = q.gqk, q.t_gqk
    pv = pin.rearrange("p (g d) -> p g d", d=64)
    act(p, sq[:, 0:512], pin[:, 0:512], AF.Square, t_pin, [t_sq])
    act(p, sq[:, 512:1024], pin[:, 512:1024], AF.Square, t_pin, [t_sq])
    p.op("dve", lambda e: e.tensor_reduce(out=ss[:, 0:16], in_=sq[:].rearrange("p (g d) -> p g d", d=64),
                                          axis=AX.X, op=ALU.add), [t_sq], [t_ss])
    act(p, ss[:, 16:32], ss[:, 0:16], AF.Ln, [t_ss], [t_ss], scale=1.0 / 64, bias=EPS)
    act(p, ss[:, 32:48], ss[:, 16:32], AF.Exp, [t_ss], [t_ss], scale=-0.5)
    rb_ = bass.AP(ss, 32, [[48, 128], [1, 16], [0, 64]])
    knv = kn[:].rearrange("p (g d) -> p g d", d=64)
    tt(p, "dve", knv, pv, rb_, ALU.mult, t_pin + [t_ss], [t_kn])
    kn8 = kn[:].rearrange("p (h d) -> p h d", d=128)
    gb_ = bass.AP(gqk, 0, [[128, 128], [0, 8], [1, 128]])
    tt(p, "pool", kn8, kn8, gb_, ALU.mult, [t_kn, t_gqk], [t_kn])


def qk_path2(p, q, b, cs_i):
    kn, kb, tmp = q.kn[b], q.kb[b], q.tmp[b]
    t_kn, t_kb, t_tmp = q.t_kn[b], q.t_kb[b], q.t_tmp[b]
    t_cs = q.t_cs
    knv = kn[:].rearrange("p (g d) -> p g d", d=64)
    cp(p, "act", kb[:], kn[:], [t_kn], [t_kb])
    x1 = knv[:, :, 0:8]
    x2 = knv[:, :, 8:16]
    cb_ = cs_i[:, 0:8]
    sb_ = cs_i[:, 8:16]
    cB = bass.AP(cb_.tensor, cb_.offset, [list(cb_.ap[0]), [0, 16], [1, 8]])
    sB = bass.AP(sb_.tensor, sb_.offset, [list(sb_.ap[0]), [0, 16], [1, 8]])
    tv = tmp[:].rearrange("p (a g d) -> p a g d", a=4, d=8)
    tt(p, "dve", tv[:, 0], x1, cB, ALU.mult, [t_kn, t_cs], [t_tmp])
    tt(p, "pool", tv[:, 1], x2, sB, ALU.mult, [t_kn, t_cs], [t_tmp])
    tt(p, "dve", tv[:, 2], x2, cB, ALU.mult, [t_kn, t_cs], [t_tmp])
    tt(p, "pool", tv[:, 3], x1, sB, ALU.mult, [t_kn, t_cs], [t_tmp])
    kbv = kb[:].rearrange("p (g d) -> p g d", d=64)
    tt(p, "dve", kbv[:, :, 0:8], tv[:, 0], tv[:, 1], ALU.subtract, [t_tmp, t_kb], [t_kb])
    tt(p, "dve", kbv[:, :, 8:16], tv[:, 2], tv[:, 3], ALU.add, [t_tmp, t_kb], [t_kb])


def alloc_qk(ph):
    d = Ctx()
    d.sq = [ph.sb([128, D], F32) for _ in range(2)]
    d.ss = [ph.sb([128, 48], F32) for _ in range(2)]
    d.kn = [ph.sb([128, D], F32) for _ in range(2)]
    d.kb = [ph.sb([128, D], BF16) for _ in range(2)]
    d.tmp = [ph.sb([128, 4 * 16 * 8], F32) for _ in range(2)]
    d.t_sq, d.t_ss, d.t_kn, d.t_tmp, d.t_kb = toks(2), toks(2), toks(2), toks(2), toks(2)
    d.gqk = ph.sb([128, 128], F32)
    d.t_gqk = Tok()
    d.cs = ph.sb([128, NT, 16], F32)
    d.t_cs = Tok()
    dma(ph.p, d.cs[:], ph.C.rope.ap().rearrange("(i p) c -> p i c", p=128), [], [d.t_cs])
    return d


def proj_qk_tiles(ph, C, q, W, t_W, ident, t_c, ps, psbf, t_bank, sink, vproj=None):
    p = ph.p
    hs = [ph.sb([128, D], F32) for _ in range(2)]
    t_hs = toks(2)
    hnb = [ph.sb([128, D], BF16) for _ in range(2)]
    t_hnb = toks(2)
    junk = ph.sb([128, D], BF16)
    t_junk = Tok()
    stat = [ph.sb([128, 4], F32) for _ in range(2)]
    t_stat = toks(2)
    hnTt = [ph.sb([128, 8, 128], BF16) for _ in range(2)]
    t_hnTt = toks(2)
    def ld(i):
        dma(p, hs[i % 2][:], C.h[128 * i:128 * (i + 1), :], [], [t_hs[i % 2]])

    def g0(i):
        b = i % 2
        norm_tile(p, hs[b][:], t_hs[b], hnb[b][:], t_hnb[b], stat[b], t_stat[b], junk[:], t_junk)

    def g1(i):
        b = i % 2
        transpose8(p, hnb[b], t_hnb[b], psbf[:, 7168:8192], t_bank[7], ident, t_c, hnTt[b][:], t_hnTt[b], ev="dve")

    def g2(i):
        b = i % 2
        kb0 = 2 * (i % 2)
        for nck in range(2):
            for kc in range(8):
                mm(p, ps[:, 512 * (kb0 + nck):512 * (kb0 + nck + 1)], hnTt[b][:, kc, :],
                   W[:, kc, 512 * nck:512 * (nck + 1)], kc == 0, kc == 7, [t_hnTt[b], t_W],
                   [t_bank[kb0], t_bank[kb0 + 1]])
        if vproj is not None:
            vproj(i, b, hnTt[b], t_hnTt[b])

    def g3(i):
        b = i % 2
        kb0 = 2 * (i % 2)
        qk_path(p, q, b, ps[:, 512 * kb0:512 * (kb0 + 2)], [t_bank[kb0], t_bank[kb0 + 1]], q.cs[:, i, :])

    def g4(i):
        qk_path2(p, q, i % 2, q.cs[:, i, :])

    def g5(i):
        sink(i, i % 2)

    stages = [g0, g1, g2, g3, g4, g5]
    ld(0)
    for n in range(NT + len(stages) - 1):
        if n + 1 < NT:
            ld(n + 1)
        for k in range(len(stages) - 1, -1, -1):
            if 0 <= n - k < NT:
                stages[k](n - k)


def phase_kv(C):
    ph = Phase(C, "kv")
    p = ph.p
    cst, t_c = ph.consts()
    ident = cst[:, 0, :]
    Wkv = ph.sb([128, 8, 2048], BF16)
    t_W = Tok()
    gK = ph.sb([128, 8], F32)
    t_gK = Tok()
    dma(p, gK[:], C.kv_norm_g.ap(), [], [t_gK])
    stg = [(ph.sb([128, 2048], F32), Tok()) for _ in range(3)]
    load_w(ph, stg, Wkv, t_W, C.kv_w.ap(), 8, 2048, gain=gK, gain_tok=t_gK, engs=("dve", "act"))
    q = alloc_qk(ph)
    dma(p, q.gqk[:], bass.AP(C.kv_k_norm_g, 0, [[0, 128], [1, 128]]), [], [q.t_gqk])
    KTt = [ph.sb([128, 8, 128], BF16) for _ in range(2)]
    t_KTt = toks(2)
    Vt = [ph.sb([128, D], BF16) for _ in range(2)]
    t_Vt = toks(2)
    ps = ph.psum()
    psbf = ps.bitcast(BF16)
    t_bank = toks(8)
    t_kd, t_vd = Tok(), Tok()

    def sink(i, b):
        transpose8(p, q.kb[b], q.t_kb[b], psbf[:, 6144:7168], t_bank[6], ident, t_c, KTt[b][:], t_KTt[b], ev="act")
        dma(p, C.KTd[:, :, 128 * i:128 * (i + 1)].rearrange("h f t -> f h t"), KTt[b][:], [t_KTt[b]], [Tok()])

    def vproj(i, b, hnTt, t_hnTt):
        for nck in range(2):
            for kc in range(8):
                mm(p, ps[:, 512 * (4 + nck):512 * (5 + nck)], hnTt[:, kc, :],
                   Wkv[:, kc, 1024 + 512 * nck:1024 + 512 * (nck + 1)], kc == 0, kc == 7, [t_hnTt, t_W], [t_bank[4]])
        cp(p, "dve", Vt[b][:], ps[:, 2048:3072], [t_bank[4]], [t_Vt[b]])
        dma(p, C.Vd[128 * i:128 * (i + 1), :], Vt[b][:], [t_Vt[b]], [Tok()])

    proj_qk_tiles(ph, C, q, Wkv, t_W, ident, t_c, ps, psbf, t_bank, sink, vproj)
    ph.finish()


def phase_b_attn(C, l):
    j = l - 2
    nc = C.nc
    lam_init = 0.8 - 0.6 * math.exp(-0.3 * l)
    outer = ExitStack()
    QT = outer.enter_context(nc.sbuf_tensor(f"bq{l}_QT", [128, 8, T], BF16))
    lw = outer.enter_context(nc.sbuf_tensor(f"bq{l}_lw", [128, 136], F32))
    gsub = outer.enter_context(nc.sbuf_tensor(f"bq{l}_gs", [128, 2], F32))
    ph = Phase(C, f"bq{l}")
    p = ph.p
    cst, t_c = ph.consts()
    ident = cst[:, 0, :]
    Wq = ph.sb([128, 8, D], BF16)
    t_Wq = Tok()
    gB = ph.sb([128, 8], F32)
    t_gB = Tok()
    dma(p, gB[:], C.b_norm_g.ap()[j], [], [t_gB])
    stg = [(ph.sb([128, 2048], F32), Tok()) for _ in range(3)]
    load_w(ph, stg, Wq, t_Wq, C.b_w_q.ap()[j], 8, D, gain=gB, gain_tok=t_gB, engs=("dve", "act"))
    q = alloc_qk(ph)
    dma(p, q.gqk[:], bass.AP(C.b_q_norm_g, j * 128, [[0, 128], [1, 128]]), [], [q.t_gqk])
    ts(p, "dve", q.gqk[:], q.gqk[:], 0.125, ALU.mult, [q.t_gqk], [q.t_gqk])
    lpb = ph.sb([128, 256], F32)
    t_l = Tok()
    dma(p, lpb[:], bass.AP(C.b_lambda, j * 256, [[0, 128], [1, 256]]), [], [t_l])
    tt(p, "dve", lw[:, 0:64], lpb[:, 0:64], lpb[:, 64:128], ALU.mult, [t_l], [t_l])
    tt(p, "dve", lw[:, 64:128], lpb[:, 128:192], lpb[:, 192:256], ALU.mult, [t_l], [t_l])
    p.op("dve", lambda e: e.tensor_reduce(out=lw[:, 128:130], in_=lw[:, 0:128].rearrange("p (a b) -> p a b", a=2),
                                          axis=AX.X, op=ALU.add), [t_l], [t_l])
    act(p, lw[:, 130:132], lw[:, 128:130], AF.Exp, [t_l], [t_l])
    tt(p, "dve", lw[:, 132:133], lw[:, 131:132], lw[:, 130:131], ALU.subtract, [t_l], [t_l])
    ts(p, "dve", lw[:, 133:134], lw[:, 132:133], -lam_init, ALU.add, [t_l], [t_l])
    t_gs = Tok()
    dma(p, gsub[:, 0:1], C.b_subln_g.ap()[j].rearrange("(p o) -> p o", o=1), [], [t_gs])
    ts(p, "dve", gsub[:, 1:2], gsub[:, 0:1], 1.0 - lam_init, ALU.mult, [t_gs], [t_gs])
    ps = ph.psum()
    psbf = ps.bitcast(BF16)
    t_bank = toks(8)
    t_QT = Tok()

    def sink(i, b):
        transpose8(p, q.kb[b], q.t_kb[b], psbf[:, 6144:7168], t_bank[6], ident, t_c,
                   QT[:, :, 128 * i:128 * (i + 1)], t_QT, ev="act")

    proj_qk_tiles(ph, C, q, Wq, t_Wq, ident, t_c, ps, psbf, t_bank, sink)
    ph.finish()

    ph = Phase(C, f"bat{l}")
    p = ph.p
    cst, t_c = ph.consts()
    ident = cst[:, 0, :]
    maskI2 = bass.AP(cst, 4 * 128, [[768, 128], [0, 2], [1, 128]])
    neglam = lw[:, 133:134]
    t_QT, t_l = Tok(), Tok()
    WQ = 384
    CH3 = [(WQ * c, WQ) for c in range(T // WQ)]
    gsB = ph.sb([128, 128], F32)
    t_gs = Tok()
    dma(p, gsB[:], bass.AP(C.b_subln_g, j * 128, [[0, 128], [1, 128]]), [], [t_gs])
    ts(p, "dve", gsB[:], gsB[:], 1.0 - lam_init, ALU.mult, [t_gs], [t_gs])
    gsB3 = bass.AP(gsB, 0, [[128, 128], [0, 3], [1, 128]])
    KTh = [ph.sb([128, T], BF16) for _ in range(2)]
    Vh = [ph.sb([128, NT, 129], BF16) for _ in range(2)]
    t_KTh, t_Vh = toks(2), toks(2)
    for hb in range(2):
        p.op("pool", (lambda hb_: lambda e: e.memset(Vh[hb_][:, :, 128:129], 1.0))(hb), [], [t_Vh[hb]])
    Pb = [ph.sb([128, 2, WQ], BF16) for _ in range(5)]
    t_P = toks(5)
    rr = ph.sb([128, 2, 3], F32)
    o1 = ph.sb([128, 3, 128], F32)
    o2 = ph.sb([128, 3, 128], F32)
    sqv = ph.sb([128, 3, 128], F32)
    ssb = ph.sb([128, 12], F32)
    onb = ph.sb([128, 3, 128], BF16)
    oS = [ph.sb([128, WQ], BF16) for _ in range(2)]
    t_ep = Tok()
    t_oS = toks(2)
    ps = ph.psum()
    psbf = ps.bitcast(BF16)
    t_S = toks(2)
    t_PV = [toks(2) for _ in range(2)]
    ns = [0]

    def sset():
        k = ns[0] % 2
        ns[0] += 1
        return k

    def sview(k, W, off=0):
        return ps[:, 1024 * k:1024 * (k + 1)].rearrange("p (m w) -> p m w", m=2)[:, :, off:W]

    def load_head(hd):
        hb = hd % 2
        dma(p, KTh[hb][:], C.KTd[hd], [], [t_KTh[hb]])
        dma(p, Vh[hb][:, :, 0:128], C.Vd[:, 128 * hd:128 * (hd + 1)].rearrange("(i p) d -> p i d", p=128),
            [], [t_Vh[hb]])

    jobs = []
    gci = 0
    for hd in range(8):
        for ci, (t0, W) in enumerate(CH3):
            imax = (t0 + W) // 128 - 1
            for i in range(imax + 1):
                jobs.append((hd, ci, gci, t0, W, imax, i))
            gci += 1

    def s0(n, job):
        hd, ci, g_, t0, W, imax, i = job
        hb = hd % 2
        off = max(0, 128 * i - t0)
        k = sset()
        for m in range(2):
            mm(p, ps[:, 512 * (2 * k + m) + off:512 * (2 * k + m) + W], KTh[hb][64 * m:64 * m + 64, 128 * i:128 * (i + 1)],
               QT[64 * m:64 * m + 64, hd, t0 + off:t0 + W], True, True, [t_KTh[hb], t_QT], [t_S[k]])
        act(p, Pb[n % 5][:, :, off:W], sview(k, W, off), AF.Exp, [t_S[k]], [t_P[n % 5]])
        if 128 * i - t0 >= 0:
            tt(p, "pool", Pb[n % 5][:, :, off:off + 128], Pb[n % 5][:, :, off:off + 128], maskI2, ALU.mult,
               [t_P[n % 5], t_c], [t_P[n % 5]])

    def s1(n, job):
        hd, ci, g_, t0, W, imax, i = job
        hb = hd % 2
        off = max(0, 128 * i - t0)
        pvs = g_ % 2
        for m in range(2):
            b_ = 4 + 2 * pvs + m
            for jb in range(off // 128, 3):
                last_i = t0 // 128 + jb
                mm(p, ps[:, 512 * b_ + 129 * jb:512 * b_ + 129 * (jb + 1)], Pb[n % 5][:, m, 128 * jb:128 * (jb + 1)],
                   Vh[hb][:, i, :], (i == 0 and jb == 0), (i == imax and jb == 2), [t_P[n % 5], t_Vh[hb]], [t_PV[pvs][m]])
            for _d in range(NDUMMY):
                mm(p, ps[:, 512 * b_ + 388:512 * b_ + 508], ident, Pb[n % 5][:, m, 0:120], False, False,
                   [t_P[n % 5], t_c], [t_PV[pvs][m]])
        if i == imax:
            pvv = ps[:, 512 * (4 + 2 * pvs):512 * (6 + 2 * pvs)].rearrange("p (m w) -> p m w", m=2)[:, :, 0:387] \
                .rearrange("p m (j c) -> p m j c", c=129)
            tpv = [t_PV[pvs][0], t_PV[pvs][1]]
            recip(p, rr[:], pvv[:, :, :, 128], tpv + [t_ep], [t_ep])
            ts(p, "dve", rr[:, 1, :], rr[:, 1, :], neglam, ALU.mult, [t_ep, t_l], [t_ep])
            r0 = bass.AP(rr, 0, [[6, 128], [1, 3], [0, 128]])
            r1_ = bass.AP(rr, 3, [[6, 128], [1, 3], [0, 128]])
            tt(p, "dve", o1[:], pvv[:, 0, :, 0:128], r0, ALU.mult, tpv + [t_ep], [t_ep])
            tt(p, "dve", o2[:], pvv[:, 1, :, 0:128], r1_, ALU.mult, tpv + [t_ep], [t_ep])
            tt(p, "pool", o1[:], o1[:], o2[:], ALU.add, [t_ep], [t_ep])
            act(p, sqv[:], o1[:], AF.Square, [t_ep], [t_ep])
            p.op("dve", lambda e: e.tensor_reduce(out=ssb[:, 0:3], in_=sqv[:], axis=AX.X, op=ALU.add), [t_ep], [t_ep])
            act(p, ssb[:, 4:7], ssb[:, 0:3], AF.Ln, [t_ep], [t_ep], scale=1.0 / 128, bias=EPS)
            act(p, ssb[:, 8:11], ssb[:, 4:7], AF.Exp, [t_ep], [t_ep], scale=-0.5)
            rsB = bass.AP(ssb, 8, [[12, 128], [1, 3], [0, 128]])
            tt(p, "dve", o2[:], o1[:], rsB, ALU.mult, [t_ep], [t_ep])
            tt(p, "pool", onb[:], o2[:], gsB3, ALU.mult, [t_ep, t_gs], [t_ep])
            k2 = sset()
            for jb in range(3):
                tr(p, psbf[:, 2048 * k2 + 128 * jb:2048 * k2 + 128 * (jb + 1)], onb[:, jb, :], ident, [t_ep, t_c], [t_S[k2]])
            ob = g_ % 2
            cp(p, "act", oS[ob][:], psbf[:, 2048 * k2:2048 * k2 + 384], [t_S[k2]], [t_oS[ob]])
            dma(p, C.oT[128 * hd:128 * (hd + 1), t0:t0 + W], oS[ob][:], [t_oS[ob]], [Tok()])
            if ci == len(CH3) - 1 and hd + 2 < 8:
                load_head(hd + 2)

    load_head(0)
    load_head(1)
    LAG = 3
    nj = len(jobs)
    for n in range(nj + LAG):
        if n < nj:
            s0(n, jobs[n])
        if n - LAG >= 0:
            s1(n - LAG, jobs[n - LAG])
    ph.finish()
    outer.close()


WNAMES = ["a_norm_g", "a_w_qkv", "a_w_o", "kv_norm_g", "kv_w", "kv_k_norm_g", "b_norm_g", "b_w_q",
          "b_q_norm_g", "b_lambda", "b_subln_g", "b_w_o", "mlp_norm_g", "mlp_w_in", "mlp_w_out"]
WSHAPES = {"a_norm_g": [2, 128, 8], "a_w_qkv": [2, 1024, 3072], "a_w_o": [2, 1024, 1024], "kv_norm_g": [128, 8],
           "kv_w": [1024, 2048], "kv_k_norm_g": [2, 64], "b_norm_g": [2, 128, 8], "b_w_q": [2, 1024, 1024],
           "b_q_norm_g": [2, 2, 64], "b_lambda": [2, 4, 64], "b_subln_g": [2, 128], "b_w_o": [2, 1024, 1024],
           "mlp_norm_g": [4, 128, 8], "mlp_w_in": [4, 1024, 4096], "mlp_w_out": [4, 4096, 1024]}


def build(steps=None, h_in=False, h_out=False):
    nc = bass.Bass("TRN2", target_bir_lowering=False)
    C = Ctx()
    C.nc = nc
    C._uid = [0]
    C.uid = lambda: (C._uid.__setitem__(0, C._uid[0] + 1), C._uid[0])[1]
    C.x = nc.dram_tensor("x", [S, D], F32, kind="ExternalInput")
    C.meta = nc.dram_tensor("meta_tokens", [NM, D], F32, kind="ExternalInput")
    for n in WNAMES:
        setattr(C, n, nc.dram_tensor(n, WSHAPES[n], F32, kind="ExternalInput"))
    C.cst = nc.dram_tensor("cst", [128, 768], BF16, kind="ExternalInput")
    C.rope = nc.dram_tensor("rope", [T, 16], F32, kind="ExternalInput")
    if h_in:
        C.hin = nc.dram_tensor("h_in", [T, D], F32, kind="ExternalInput")
    if h_out:
        C.h = nc.dram_tensor("h_out", [T, D], F32, kind="ExternalOutput").ap()
        C.out = None
    else:
        C.h = nc.dram_tensor("h_res", [T, D], F32).ap()
        C.out = nc.dram_tensor("out", [S, D], F32, kind="ExternalOutput")
    C.oT = nc.dram_tensor("oT_s", [D, T], BF16).ap()
    C.KTd = nc.dram_tensor("KT_s", [8, 128, T], BF16).ap()
    C.Vd = nc.dram_tensor("V_s", [T, D], BF16).ap()
    if steps is None:
        steps = ["init"]
        for l in range(4):
            if l == 2:
                steps.append("kv")
            steps += [f"attn{l}", f"post{l}"]
    zero_oT_pad.C = C
    with ExitStack() as top:
        C.sy = Sync(nc, top)
        for s in steps:
            if s == "init":
                phase_init(C)
            elif s == "copyin":
                ph = Phase(C, "cpin")
                th = Tok()
                zz = ph.sb([128, D], F32)
                tzz = Tok()
                ph.p.op("pool", lambda e: e.memset(zz[:], 0.0), [], [tzz])
                zero_oT_pad(ph.p, zz, tzz)
                for j in range(8):
                    r0, r1 = 528 * j, 528 * (j + 1)
                    dma(ph.p, C.h[r0:r1, :], C.hin.ap()[r0:r1, :], [], [th])
                ph.finish()
            elif s == "kv":
                phase_kv(C)
            elif s.startswith("attn"):
                l = int(s[4:])
                if l < 2:
                    phase_a_attn(C, l)
                else:
                    phase_b_attn(C, l)
            elif s.startswith("post"):
                l = int(s[4:])
                phase_post(C, l, C.a_w_o.ap()[l] if l < 2 else C.b_w_o.ap()[l - 2], final=(l == 3 and not h_out))
            elif s.startswith("oproj"):
                l = int(s[5:])
                phase_oproj(C, C.a_w_o.ap()[l] if l < 2 else C.b_w_o.ap()[l - 2])
            elif s.startswith("mlp"):
                l = int(s[3:])
                phase_mlp(C, l, final=(l == 3 and not h_out))
    return nc


def make_consts():
    j = np.arange(128)[:, None]
    s = np.arange(128)[None, :]
    ident = (j == s).astype(np.float32)
    negtri = -(j >= s).astype(np.float32)
    ones = np.ones((128, 128), np.float32)
    maskS = (j < s).astype(np.float32)
    maskI = (j <= s).astype(np.float32)
    onesS = ones / 128.0
    cst = np.concatenate([ident, negtri, ones, maskS, maskI, onesS], axis=1).astype(ml_dtypes.bfloat16)
    inv_freq = np.power(np.float32(500000.0), -np.arange(0, 16, 2, dtype=np.float32) / np.float32(16))
    ang = np.arange(T, dtype=np.float32)[:, None] * inv_freq[None, :]
    rope = np.concatenate([np.cos(ang), np.sin(ang)], axis=1).astype(np.float32)
    return cst, rope


_NC_CACHE = {}


def kernel(**inputs):
    x = np.ascontiguousarray(inputs["x"], dtype=np.float32)
    B = x.shape[0]
    cst, rope = make_consts()
    if "full" not in _NC_CACHE:
        _NC_CACHE["full"] = build()
    nc = _NC_CACHE["full"]
    shared = {n: np.ascontiguousarray(inputs[n], dtype=np.float32) for n in WNAMES}
    for n in ("a_norm_g", "kv_norm_g", "b_norm_g", "mlp_norm_g"):
        g = shared[n]
        shared[n] = np.ascontiguousarray(g.reshape(g.shape[:-1] + (8, 128)).swapaxes(-1, -2))
    shared["meta_tokens"] = np.ascontiguousarray(inputs["meta_tokens"], dtype=np.float32)
    shared["cst"] = cst
    shared["rope"] = rope
    in_maps = []
    for b in range(B):
        m = dict(shared)
        m["x"] = x[b]
        in_maps.append(m)
    res = run_bass_kernel_spmd(nc, in_maps, core_ids=list(range(B)))
    return np.stack([np.asarray(r["out"], dtype=np.float32) for r in res.results], axis=0)
```

```python
import math
from contextlib import ExitStack

import numpy as np
import ml_dtypes

import concourse.bass as bass
import concourse.mybir as mybir
from concourse.bass_utils import run_bass_kernel_spmd

F32 = mybir.dt.float32
BF16 = mybir.dt.bfloat16
AF = mybir.ActivationFunctionType
ALU = mybir.AluOpType
AX = mybir.AxisListType

D = 1024
S = 4096
NM = 16
T = 4224
NT = 33
DFF = 4096
EPS = 1e-6
CHUNKS = [(0, 128)] + [(128 + 512 * c, 512) for c in range(8)]
CHUNKS_A = [(0, 128)] + [(128 + 512 * c, 512) for c in range(7)] + [(3712, NM + S - 3712)]

ENGS = ("pe", "act", "dve", "pool", "sp")
NRING = 8
NDUMMY = 0
DMAQ = ("sp", "act")


class Tok:
    __slots__ = ("w", "rs")

    def __init__(self):
        self.w = None
        self.rs = []


def toks(n):
    return [Tok() for _ in range(n)]


class Op:
    __slots__ = ("eng", "fn", "deps", "sig", "val", "dma", "ring", "rval", "waits")

    def __init__(self, eng, fn, dma):
        self.eng = eng
        self.fn = fn
        self.deps = ()
        self.sig = False
        self.val = 0
        self.dma = dma
        self.ring = 0
        self.rval = 0
        self.waits = None


class Sync:
    def __init__(self, nc, stack):
        self.sem = {e: stack.enter_context(nc.semaphore(f"s_{e}")) for e in ENGS}
        self.cnt = {e: 0 for e in ENGS}
        self.ring = {q: [stack.enter_context(nc.semaphore(f"ring_{q}{i}")) for i in range(NRING)] for q in DMAQ}
        self.ndma = {q: 0 for q in DMAQ}


class Prog:
    def __init__(self, sync):
        self.sy = sync
        self.ops = []

    def op(self, eng, fn, r=(), w=(), dma=False):
        o = Op(eng, fn, dma)
        deps = set()
        for t in r:
            if t.w is not None:
                deps.add(t.w)
        for t in w:
            if t.w is not None:
                deps.add(t.w)
            deps.update(t.rs)
        o.deps = deps
        for t in r:
            t.rs.append(o)
        for t in w:
            t.w = o
            t.rs = []
        self.ops.append(o)
        return o

    def emit(self, block):
        sy = self.sy
        for o in self.ops:
            for a in o.deps:
                if a.dma:
                    continue
                if a.eng == "pe" and o.eng == "pe":
                    continue
                a.sig = True
        waited = {e: {} for e in ENGS}
        ring_out = {}
        for o in self.ops:
            ws = {}
            wd = waited[o.eng]
            for a in o.deps:
                if a.dma:
                    key = ("r", a.eng, a.ring)
                    v = a.rval
                else:
                    if a.eng == "pe" and o.eng == "pe":
                        continue
                    key = ("e", a.eng)
                    v = a.val
                if wd.get(key, 0) >= v:
                    continue
                if ws.get(key, 0) < v:
                    ws[key] = v
            if o.dma:
                j = sy.ndma[o.eng]
                sy.ndma[o.eng] += 1
                o.ring = j % NRING
                o.rval = 16 * (j // NRING + 1)
                if j >= NRING:
                    key = ("r", o.eng, o.ring)
                    v = 16 * (j // NRING)
                    if wd.get(key, 0) < v and ws.get(key, 0) < v:
                        ws[key] = v
                ring_out[(o.eng, o.ring)] = o.rval
            elif o.sig:
                sy.cnt[o.eng] += 1
                o.val = sy.cnt[o.eng]
            for k, v in ws.items():
                wd[k] = v
            o.waits = list(ws.items())
        final_sp = [(("r", q, r), v) for (q, r), v in ring_out.items() if waited["sp"].get(("r", q, r), 0) < v]
        per = {e: [o for o in self.ops if o.eng == e] for e in ENGS}

        def semof(key):
            return sy.ring[key[1]][key[2]] if key[0] == "r" else sy.sem[key[1]]

        def run(e, eng):
            for o in per[e]:
                for k, v in o.waits:
                    eng.wait_ge(semof(k), v)
                ins = o.fn(eng)
                if o.dma:
                    ins.then_inc(sy.ring[e][o.ring], 16)
                elif o.sig:
                    ins.then_inc(sy.sem[e], 1)
            if e == "sp":
                for k, v in final_sp:
                    eng.wait_ge(semof(k), v)

        block.tensor(lambda eng: run("pe", eng))
        block.scalar(lambda eng: run("act", eng))
        block.vector(lambda eng: run("dve", eng))
        block.gpsimd(lambda eng: run("pool", eng))
        block.sync(lambda eng: run("sp", eng))


def mm(p, out, lhsT, rhs, start, stop, r, w, skip=False):
    if skip:
        p.op("pe", lambda e: e.matmul(out, lhsT=lhsT, rhs=rhs, start=start, stop=stop, skip_group_check=True), r, w)
    else:
        p.op("pe", lambda e: e.matmul(out, lhsT=lhsT, rhs=rhs, start=start, stop=stop), r, w)


def tr(p, out, in_, ident, r, w):
    p.op("pe", lambda e: e.transpose(out, in_, ident), r, w)


def act(p, out, in_, func, r, w, **kw):
    p.op("act", lambda e: e.activation(out=out, in_=in_, func=func, **kw), r, w)


def tt(p, eng, out, in0, in1, op, r, w):
    p.op(eng, lambda e: e.tensor_tensor(out=out, in0=in0, in1=in1, op=op), r, w)


def ts(p, eng, out, in0, s1, op0, r, w, s2=None, op1=None):
    if op1 is None:
        p.op(eng, lambda e: e.tensor_scalar(out=out, in0=in0, scalar1=s1, scalar2=None, op0=op0), r, w)
    else:
        p.op(eng, lambda e: e.tensor_scalar(out=out, in0=in0, scalar1=s1, scalar2=s2, op0=op0, op1=op1), r, w)


def stt(p, eng, out, in0, scalar, in1, op0, op1, r, w):
    p.op(eng, lambda e: e.scalar_tensor_tensor(out=out, in0=in0, scalar=scalar, in1=in1, op0=op0, op1=op1), r, w)


def cp(p, eng, out, in_, r, w, scale=None):
    if eng == "act":
        if scale is None:
            p.op("act", lambda e: e.activation(out=out, in_=in_, func=AF.Copy), r, w)
        else:
            p.op("act", lambda e: e.activation(out=out, in_=in_, func=AF.Copy, scale=scale), r, w)
    else:
        p.op(eng, lambda e: e.tensor_copy(out=out, in_=in_), r, w)


def recip(p, out, in_, r, w):
    p.op("dve", lambda e: e.reciprocal(out=out, in_=in_), r, w)


def dma(p, out, in_, r, w, q="sp"):
    p.op(q, lambda e: e.dma_start(out=out, in_=in_), r, w, dma=True)


class Ctx:
    pass


class Phase:
    def __init__(self, C, name):
        self.C = C
        self.nc = C.nc
        self.name = name
        self.st = ExitStack()
        self.p = Prog(C.sy)
        self.n = 0

    def sb(self, shape, dt):
        self.n += 1
        return self.st.enter_context(self.nc.sbuf_tensor(f"{self.name}_t{self.n}", shape, dt))

    def psum(self):
        return self.st.enter_context(self.nc.psum_tensor(f"{self.name}_ps", [128, 4096], F32))

    def consts(self):
        c = self.sb([128, 6, 128], BF16)
        t = Tok()
        dma(self.p, c[:], self.C.cst.ap().rearrange("p (a b) -> p a b", a=6), [], [t])
        return c, t

    def finish(self):
        blk = self.st.enter_context(self.nc.Block())
        self.p.emit(blk)
        self.st.close()


def load_w(ph, stg, dst, dst_tok, src, nk, ncols, gain=None, gain_tok=None, engs=("pool",), cnt=[0], cap=2048, q="sp"):
    p = ph.p
    srcv = src.rearrange("(kc p) n -> p kc n", p=128)
    cw = min(ncols, cap)
    kb = max(1, min(nk, cap // cw))
    chunks = []
    for k0 in range(0, nk, kb):
        k1 = min(nk, k0 + kb)
        for c0 in range(0, ncols, cw):
            sa, stok = stg[cnt[0] % len(stg)]
            eng = engs[cnt[0] % len(engs)]
            cnt[0] += 1
            chunks.append((k0, k1, c0, sa, stok, eng))

    def issue(ch):
        k0, k1, c0, sa, stok, eng = ch
        sv = sa[:, 0:(k1 - k0) * cw].rearrange("p (k n) -> p k n", n=cw)
        dma(p, sv, srcv[:, k0:k1, c0:c0 + cw], [], [stok], q=q)

    def conv(ch):
        k0, k1, c0, sa, stok, eng = ch
        sv = sa[:, 0:(k1 - k0) * cw].rearrange("p (k n) -> p k n", n=cw)
        if gain is None:
            cp(p, eng, dst[:, k0:k1, c0:c0 + cw], sv, [stok], [dst_tok])
        else:
            for k in range(k0, k1):
                if eng == "act":
                    act(p, dst[:, k, c0:c0 + cw], sv[:, k - k0, :], AF.Copy, [stok, gain_tok], [dst_tok],
                        scale=gain[:, k:k + 1])
                else:
                    ts(p, eng, dst[:, k, c0:c0 + cw], sv[:, k - k0, :], gain[:, k:k + 1], ALU.mult,
                       [stok, gain_tok], [dst_tok])

    depth = len(stg) - 1
    n = len(chunks)
    for k in range(n + depth):
        if k < n:
            issue(chunks[k])
        if k - depth >= 0:
            conv(chunks[k - depth])


def norm_tile(p, hs, t_hs, hnb, t_hnb, stt_, t_st, junk, t_junk):
    act(p, junk, hs, AF.Square, [t_hs], [t_junk, t_st], accum_out=stt_[:, 0:1])
    act(p, stt_[:, 1:2], stt_[:, 0:1], AF.Ln, [t_st], [t_st], scale=1.0 / D, bias=EPS)
    act(p, stt_[:, 2:3], stt_[:, 1:2], AF.Exp, [t_st], [t_st], scale=-0.5)
    ts(p, "dve", hnb, hs, stt_[:, 2:3], ALU.mult, [t_hs, t_st], [t_hnb])


def transpose8(p, src, t_src, psbf_bank, t_bank, ident, t_c, dst, t_dst, ev="dve"):
    for kc in range(8):
        tr(p, psbf_bank[:, kc * 128:(kc + 1) * 128], src[:, kc * 128:(kc + 1) * 128], ident, [t_src, t_c], [t_bank])
    cp(p, ev, dst, psbf_bank.rearrange("p (a b) -> p a b", a=8), [t_bank], [t_dst])


def zero_oT_pad(p, z, tz, C=None):
    npad = T - NM - S
    zb = z.bitcast(BF16)[:, 0:8 * npad].rearrange("p (k t) -> p k t", k=8)
    dma(p, zero_oT_pad.C.oT[:, NM + S:T].rearrange("(kc p) t -> p kc t", p=128), zb, [tz], [Tok()])


def phase_init(C):
    ph = Phase(C, "init")
    p = ph.p
    z = ph.sb([128, D], F32)
    tz = Tok()
    th = Tok()
    p.op("pool", lambda e: e.memset(z[:], 0.0), [], [tz])
    dma(p, C.h[0:NM, :], C.meta.ap(), [], [th])
    for j in range(8):
        dma(p, C.h[NM + 512 * j:NM + 512 * (j + 1), :], C.x.ap()[512 * j:512 * (j + 1), :], [], [th])
    dma(p, C.h[NM + S:T, :], z[0:T - NM - S, :], [tz], [th])
    zero_oT_pad(p, z, tz)
    ph.finish()


def phase_a_attn(C, l):
    ph = Phase(C, f"aat{l}")
    p = ph.p
    nc = C.nc
    cst, t_c = ph.consts()
    ident, negtri, ones, maskS = cst[:, 0, :], cst[:, 1, :], cst[:, 2, :], cst[:, 3, :]
    hnT = ph.sb([128, 8, T], BF16)
    t_hnT = toks(NT)
    hs = [ph.sb([128, D], F32) for _ in range(2)]
    t_hs = toks(2)
    hnb = [ph.sb([128, D], BF16) for _ in range(2)]
    t_hnb = toks(2)
    junk = ph.sb([128, D], BF16)
    t_junk = Tok()
    stat = [ph.sb([128, 4], F32) for _ in range(2)]
    t_stat = toks(2)
    gA = ph.sb([128, 8], F32)
    t_gA = Tok()
    wst = [ph.sb([128, 8 * 384], F32) for _ in range(2)]
    t_wst = toks(2)
    wbf = [ph.sb([128, 8, 384], BF16) for _ in range(2)]
    t_wbf = toks(2)
    QT2 = [ph.sb([128, T], BF16) for _ in range(2)]
    KT2 = [ph.sb([128, T], BF16) for _ in range(2)]
    V2 = [ph.sb([128, NT, 128], BF16) for _ in range(2)]
    t_QT2, t_KT2, t_V2 = toks(2), toks(2), toks(2)
    eb = [ph.sb([128, 512], F32) for _ in range(3)]
    t_e = toks(3)
    SPt = [ph.sb([128, 512], BF16) for _ in range(4)]
    t_S = toks(4)
    Xb = [ph.sb([128, 512], F32) for _ in range(3)]
    t_X = toks(3)
    wt = [ph.sb([128, 512], BF16) for _ in range(4)]
    t_W = toks(4)
    Racc = [ph.sb([128, 512], F32) for _ in range(2)]
    t_R = toks(2)
    oS = [ph.sb([128, 512], BF16) for _ in range(2)]
    t_oS = toks(2)
    ps = ph.psum()
    psbf = ps.bitcast(BF16)
    t_bank = toks(8)
    t_Oh = toks(2)
    t_od = Tok()

    def bank(b):
        return ps[:, 512 * b:512 * (b + 1)]

    dma(p, gA[:], C.a_norm_g.ap()[l], [], [t_gA])
    def p1_norm(i):
        b = i % 2
        dma(p, hs[b][:], C.h[128 * i:128 * (i + 1), :], [], [t_hs[b]])
        norm_tile(p, hs[b][:], t_hs[b], hnb[b][:], t_hnb[b], stat[b], t_stat[b], junk[:], t_junk)

    p1_norm(0)
    for i in range(NT):
        b = i % 2
        if i + 1 < NT:
            p1_norm(i + 1)
        transpose8(p, hnb[b], t_hnb[b], psbf[:, 7168:8192], t_bank[7], ident, t_c,
                   hnT[:, :, 128 * i:128 * (i + 1)], t_hnT[i], ev="dve")
    wq = C.a_w_qkv.ap()[l]

    def proj_units(g):
        wb_ = g % 2
        QT, KT, V = QT2[g % 2], KT2[g % 2], V2[g % 2]
        t_QT, t_KT, t_V = t_QT2[g % 2], t_KT2[g % 2], t_V2[g % 2]
        units = []

        def u_w():
            for sec in range(3):
                dma(p, wst[wb_][:].rearrange("p (k s n) -> p k s n", k=8, s=3)[:, :, sec, :],
                    wq[:, sec * D + g * 128: sec * D + (g + 1) * 128].rearrange("(kc p) n -> p kc n", p=128),
                    [], [t_wst[wb_]])
            for kc in range(8):
                ts(p, "dve", wbf[wb_][:, kc, :], wst[wb_][:, kc * 384:(kc + 1) * 384], gA[:, kc:kc + 1], ALU.mult,
                   [t_wst[wb_], t_gA], [t_wbf[wb_]])
        units.append(u_w)

        def mk_qk(sec, dst, t_dst, scl, t0, W):
            def u():
                for kc in range(8):
                    mm(p, bank(7)[:, 0:W], wbf[wb_][:, kc, sec * 128:(sec + 1) * 128], hnT[:, kc, t0:t0 + W],
                       kc == 0, kc == 7, [t_wbf[wb_]] + t_hnT[t0 // 128:(t0 + W) // 128], [t_bank[7]])
                if scl is None:
                    cp(p, "dve", dst[:, t0:t0 + W], bank(7)[:, 0:W], [t_bank[7]], [t_dst])
                else:
                    ts(p, "dve", dst[:, t0:t0 + W], bank(7)[:, 0:W], scl, ALU.mult, [t_bank[7]], [t_dst])
            return u

        def mk_v(q4):
            def u():
                tiles = list(range(4 * q4, min(NT, 4 * q4 + 4)))
                for j, i in enumerate(tiles):
                    for kc in range(8):
                        mm(p, bank(7)[:, 128 * j:128 * (j + 1)], hnT[:, kc, 128 * i:128 * (i + 1)],
                           wbf[wb_][:, kc, 256:384], kc == 0 and j == 0, kc == 7 and j == len(tiles) - 1,
                           [t_wbf[wb_], t_hnT[i]], [t_bank[7]])
                n = len(tiles)
                cp(p, "dve", V[:, tiles[0]:tiles[0] + n, :],
                   bank(7)[:, 0:128 * n].rearrange("p (a b) -> p a b", b=128), [t_bank[7]], [t_V])
            return u

        for (t0, W) in CHUNKS:
            units.append(mk_qk(0, QT, t_QT, 0.125, t0, W))
            units.append(mk_qk(1, KT, t_KT, None, t0, W))
        for q4 in range(9):
            units.append(mk_v(q4))
        return units

    for u in proj_units(0):
        u()
    for g in range(8):
        QT, KT, V = QT2[g % 2], KT2[g % 2], V2[g % 2]
        t_QT, t_KT, t_V = t_QT2[g % 2], t_KT2[g % 2], t_V2[g % 2]
        nxt = proj_units(g + 1) if g + 1 < 8 else []
        jobs = []
        for ci, (t0, W) in enumerate(CHUNKS_A):
            imax = (t0 + W + 127) // 128 - 1
            for i in range(imax, -1, -1):
                for hh in range(2):
                    jobs.append((hh, ci, t0, W, imax, i))

        def stA(n, job):
            hh, ci, t0, W, imax, i = job
            off = max(0, 128 * i - t0)
            diag = 128 * i - t0 >= 0
            pb = 64 * hh
            zb, e_, s_ = n % 2, n % 3, n % 4
            if i == imax:
                p.op("pool", lambda e: e.memset(Racc[hh][:, 0:W], 0.0), [], [t_R[hh]])
            if hh == 0:
                for d_ in range(2):
                    mm(p, bank((n + d_) % 2)[:, off:W], KT[64 * d_:64 * d_ + 64, 128 * i:128 * (i + 1)],
                       QT[64 * d_:64 * d_ + 64, t0 + off:t0 + W], True, True, [t_KT, t_QT], [t_bank[(n + d_) % 2]])
            act(p, eb[e_][:, off:W], bank(zb)[:, off:W], AF.Exp, [t_bank[zb]], [t_e[e_]])
            act(p, SPt[s_][:, off:W], eb[e_][:, off:W], AF.Ln, [t_e[e_]], [t_S[s_]], bias=1.0)
            if diag:
                hi = min(off + 128, W)
                tt(p, "pool", SPt[s_][:, off:hi], SPt[s_][:, off:hi], maskS[:, 0:hi - off], ALU.mult,
                   [t_S[s_], t_c], [t_S[s_]])

        def stB(n, job):
            hh, ci, t0, W, imax, i = job
            off = max(0, 128 * i - t0)
            zb, s_, x_, cb = 2 + n % 2, n % 4, n % 3, 4 + n % 2
            last = i == 0
            pb = 64 * hh
            if hh == 0:
                for d_ in range(2):
                    mm(p, bank(2 + (n + d_) % 2)[:, off:W], KT[64 * d_:64 * d_ + 64, 128 * i:128 * (i + 1)],
                       QT[64 * d_:64 * d_ + 64, t0 + off:t0 + W], True, False, [t_KT, t_QT],
                       [t_bank[2 + (n + d_) % 2]])
            mm(p, bank(zb)[:, off:W], negtri, SPt[s_][:, off:W], False, True, [t_S[s_], t_c], [t_bank[zb]])
            if not last:
                mm(p, bank(cb)[:, off:W], ones, SPt[s_][:, off:W], True, True, [t_S[s_], t_c], [t_bank[cb]])
            tt(p, "dve", Xb[x_][:, off:W], bank(zb)[:, off:W], Racc[hh][:, off:W], ALU.subtract,
               [t_bank[zb], t_R[hh]], [t_X[x_]])
            if not last:
                tt(p, "dve", Racc[hh][:, off:W], bank(cb)[:, off:W], Racc[hh][:, off:W], ALU.add,
                   [t_bank[cb], t_R[hh]], [t_R[hh]])

        def stC(n, job):
            hh, ci, t0, W, imax, i = job
            off = max(0, 128 * i - t0)
            diag = 128 * i - t0 >= 0
            x_, w_ = n % 3, n % 4
            act(p, wt[w_][:, off:W], Xb[x_][:, off:W], AF.Exp, [t_X[x_]], [t_W[w_]])
            if diag:
                hi = min(off + 128, W)
                tt(p, "pool", wt[w_][:, off:hi], wt[w_][:, off:hi], maskS[:, 0:hi - off], ALU.mult,
                   [t_W[w_], t_c], [t_W[w_]])
                if off > 0:
                    p.op("pool", (lambda ap: lambda e: e.memset(ap, 0.0))(wt[w_][:, 0:off]), [], [t_W[w_]])

        def stD(n, job):
            hh, ci, t0, W, imax, i = job
            off = max(0, 128 * i - t0)
            pb = 64 * hh
            w_ = n % 4
            mm(p, bank(6)[pb:pb + 64, 0:W], V[:, i, pb:pb + 64], wt[w_][:, 0:W], i == imax, i == 0,
               [t_W[w_], t_V], [t_Oh[hh]])
            if i == 0:
                ob = ci % 2
                cp(p, "dve", oS[ob][pb:pb + 64, 0:W], bank(6)[pb:pb + 64, 0:W], [t_Oh[hh]], [t_oS[ob]])
                row = (2 * g + hh) * 64
                dma(p, C.oT[row:row + 64, t0:t0 + W], oS[ob][pb:pb + 64, 0:W], [t_oS[ob]], [Tok()])

        nj = len(jobs)
        step = max(1, (nj - 20) // max(1, len(nxt)))
        ui = 0
        for n in range(nj + 2):
            if n < nj:
                stA(n, jobs[n])
            if 0 <= n - 1 < nj:
                stB(n - 1, jobs[n - 1])
                stC(n - 1, jobs[n - 1])
            if 0 <= n - 2 < nj:
                stD(n - 2, jobs[n - 2])
            if ui < len(nxt) and n >= 4 and (n - 4) % step == 0:
                nxt[ui]()
                ui += 1
        while ui < len(nxt):
            nxt[ui]()
            ui += 1
    ph.finish()


def phase_oproj(C, w_o):
    ph = Phase(C, f"op{C.uid()}")
    p = ph.p
    Wo = ph.sb([128, 8, D], BF16)
    t_Wo = Tok()
    stg = [(ph.sb([128, 2048], F32), Tok()) for _ in range(3)]
    load_w(ph, stg, Wo, t_Wo, w_o, 8, D, engs=("dve", "act"))
    oTs = [ph.sb([128, 8, 512], BF16) for _ in range(2)]
    t_oTs = toks(2)
    hs = [ph.sb([128, D], F32) for _ in range(4)]
    t_hs = toks(4)
    ps = ph.psum()
    t_bank = toks(8)
    t_hd = toks(NT)
    nb = 0
    for ci, (t0, W) in enumerate(CHUNKS):
        ob = ci % 2
        dma(p, oTs[ob][:, :, 0:W], C.oT[:, t0:t0 + W].rearrange("(kc p) t -> p kc t", p=128), [], [t_oTs[ob]])
        for j in range(W // 128):
            i = t0 // 128 + j
            hb = i % 4
            dma(p, hs[hb][:], C.h[128 * i:128 * (i + 1), :], [t_hd[i]], [t_hs[hb]])
            for nck in range(2):
                b = nb % 8
                nb += 1
                for kc in range(8):
                    mm(p, ps[:, 512 * b:512 * (b + 1)], oTs[ob][:, kc, 128 * j:128 * (j + 1)],
                       Wo[:, kc, 512 * nck:512 * (nck + 1)], kc == 0, kc == 7, [t_oTs[ob], t_Wo], [t_bank[b]])
                tt(p, "dve", hs[hb][:, 512 * nck:512 * (nck + 1)], ps[:, 512 * b:512 * (b + 1)],
                   hs[hb][:, 512 * nck:512 * (nck + 1)], ALU.add, [t_bank[b], t_hs[hb]], [t_hs[hb]])
            dma(p, C.h[128 * i:128 * (i + 1), :], hs[hb][:], [t_hs[hb]], [t_hd[i]])
    ph.finish()


def phase_mlp(C, l, final):
    ph = Phase(C, f"mlp{l}")
    p = ph.p
    cst, t_c = ph.consts()
    ident = cst[:, 0, :]
    W1 = ph.sb([128, 8, DFF], BF16)
    W2 = ph.sb([128, 32, D], BF16)
    t_W1, t_W2 = Tok(), Tok()
    gM = ph.sb([128, 8], F32)
    t_gM = Tok()
    dma(p, gM[:], C.mlp_norm_g.ap()[l], [], [t_gM])
    stg = [(ph.sb([128, 2048], F32), Tok()) for _ in range(3)]
    load_w(ph, stg, W1, t_W1, C.mlp_w_in.ap()[l], 8, DFF, gain=gM, gain_tok=t_gM, engs=("dve", "act"))
    load_w(ph, stg, W2, t_W2, C.mlp_w_out.ap()[l], 32, D, engs=("dve", "act"))
    hs = [ph.sb([128, D], F32) for _ in range(4)]
    t_hs = toks(4)
    hnb = [ph.sb([128, D], BF16) for _ in range(2)]
    t_hnb = toks(2)
    junk = ph.sb([128, D], BF16)
    t_junk = Tok()
    stat = [ph.sb([128, 4], F32) for _ in range(2)]
    t_stat = toks(2)
    hnTc = [ph.sb([128, 8, 256], BF16) for _ in range(2)]
    t_hnTc = toks(2)
    aT = ph.sb([128, 32, 256], BF16)
    t_aT = Tok()
    rb = [ph.sb([128, 256], F32) for _ in range(3)]
    t_rb = toks(3)
    ps = ph.psum()
    psbf = ps.bitcast(BF16)
    t_bank = toks(8)
    t_hd = Tok()
    nchunk = (NT + 1) // 2
    nbc = [0]

    def tiles_of(c):
        return list(range(2 * c, min(NT, 2 * c + 2)))

    def stA1(c):
        for j, i in enumerate(tiles_of(c)):
            hb = i % 4
            b2 = i % 2
            dma(p, hs[hb][:], C.h[128 * i:128 * (i + 1), :], [], [t_hs[hb]])
            norm_tile(p, hs[hb][:], t_hs[hb], hnb[b2][:], t_hnb[b2], stat[b2], t_stat[b2], junk[:], t_junk)

    def stA2(c):
        cb = c % 2
        for j, i in enumerate(tiles_of(c)):
            b2 = i % 2
            transpose8(p, hnb[b2], t_hnb[b2], psbf[:, 7168:8192], t_bank[7], ident, t_c,
                       hnTc[cb][:, :, 128 * j:128 * (j + 1)], t_hnTc[cb], ev="dve")

    def stB(c):
        cb = c % 2
        W = 128 * len(tiles_of(c))
        for f in range(32):
            b = nbc[0] % 7
            nbc[0] += 1
            r_ = f % 3
            for kc in range(8):
                mm(p, ps[:, 512 * b:512 * b + W], W1[:, kc, 128 * f:128 * (f + 1)], hnTc[cb][:, kc, 0:W],
                   kc == 0, kc == 7, [t_W1, t_hnTc[cb]], [t_bank[b]])
            act(p, rb[r_][:, 0:W], ps[:, 512 * b:512 * b + W], AF.Relu, [t_bank[b]], [t_rb[r_]])
            tt(p, "pool" if f % 2 else "dve", aT[:, f, 0:W], rb[r_][:, 0:W], rb[r_][:, 0:W], ALU.mult,
               [t_rb[r_]], [t_aT])

    def stC(c):
        for j, i in enumerate(tiles_of(c)):
            hb = i % 4
            for nck in range(2):
                b = nbc[0] % 7
                nbc[0] += 1
                for f in range(32):
                    mm(p, ps[:, 512 * b:512 * (b + 1)], aT[:, f, 128 * j:128 * (j + 1)],
                       W2[:, f, 512 * nck:512 * (nck + 1)], f == 0, f == 31, [t_aT, t_W2], [t_bank[b]])
                tt(p, "dve", hs[hb][:, 512 * nck:512 * (nck + 1)], ps[:, 512 * b:512 * (b + 1)],
                   hs[hb][:, 512 * nck:512 * (nck + 1)], ALU.add, [t_bank[b], t_hs[hb]], [t_hs[hb]])
            if not final:
                dma(p, C.h[128 * i:128 * (i + 1), :], hs[hb][:], [t_hs[hb]], [Tok()])
            else:
                lo = max(128 * i, NM)
                hi = min(128 * (i + 1), NM + S)
                if hi > lo:
                    dma(p, C.out.ap()[lo - NM:hi - NM, :], hs[hb][lo - 128 * i:hi - 128 * i, :], [t_hs[hb]], [Tok()])

    stA1(0)
    stA2(0)
    for c in range(nchunk):
        if c + 1 < nchunk:
            stA1(c + 1)
        stB(c)
        if c + 1 < nchunk:
            stA2(c + 1)
        stC(c)
    ph.finish()


def phase_post(C, l, w_o, final):
    ph = Phase(C, f"post{l}")
    p = ph.p
    cst, t_c = ph.consts()
    ident = cst[:, 0, :]
    W1 = ph.sb([128, 8, DFF], BF16)
    W2 = ph.sb([128, 32, D], BF16)
    t_W1, t_W2 = Tok(), Tok()
    gM = ph.sb([128, 8], F32)
    t_gM = Tok()
    dma(p, gM[:], C.mlp_norm_g.ap()[l], [], [t_gM])
    Wo = ph.sb([128, 8, D], BF16)
    t_Wo = Tok()
    stg = [(ph.sb([128, 512], F32), Tok()) for _ in range(4)]
    load_w(ph, stg, Wo, t_Wo, w_o, 8, D, engs=("act",), cap=512, q="act")
    load_w(ph, stg, W1, t_W1, C.mlp_w_in.ap()[l], 8, DFF, gain=gM, gain_tok=t_gM, engs=("act",), cap=512, q="act")
    load_w(ph, stg, W2, t_W2, C.mlp_w_out.ap()[l], 32, D, engs=("act",), cap=512, q="act")
    hs = [ph.sb([128, D], F32) for _ in range(4)]
    t_hs = toks(4)
    hnb = [ph.sb([128, D], BF16) for _ in range(2)]
    t_hnb = toks(2)
    oTs = [ph.sb([128, 8, 128], BF16) for _ in range(2)]
    t_oTs = toks(2)
    t_hd = toks(NT)
    stat = [ph.sb([128, 4], F32) for _ in range(2)]
    t_stat = toks(2)
    hnTc = [ph.sb([128, 8, 256], BF16) for _ in range(2)]
    t_hnTc = toks(2)
    aT = ph.sb([128, 32, 256], BF16)
    t_aT = Tok()
    rb = [ph.sb([128, 256], F32) for _ in range(3)]
    t_rb = toks(3)
    ps = ph.psum()
    psbf = ps.bitcast(BF16)
    t_bank = toks(8)
    nchunk = (NT + 1) // 2
    nbc = [0]

    def tiles_of(c):
        return list(range(2 * c, min(NT, 2 * c + 2)))

    def stA1(c):
        for j, i in enumerate(tiles_of(c)):
            hb = i % 4
            b2 = i % 2
            dma(p, hs[hb][:], C.h[128 * i:128 * (i + 1), :], [t_hd[i]], [t_hs[hb]])
            norm_tile(p, hs[hb][:], t_hs[hb], hnb[b2][:], t_hnb[b2], stat[b2], t_stat[b2], hnb[b2][:], t_hnb[b2])

    def stA2(c):
        cb = c % 2
        for j, i in enumerate(tiles_of(c)):
            b2 = i % 2
            transpose8(p, hnb[b2], t_hnb[b2], psbf[:, 7168:8192], t_bank[7], ident, t_c,
                       hnTc[cb][:, :, 128 * j:128 * (j + 1)], t_hnTc[cb], ev="dve")

    def stB(c):
        cb = c % 2
        W = 128 * len(tiles_of(c))
        for f in range(32):
            b = nbc[0] % 7
            nbc[0] += 1
            r_ = f % 3
            for kc in range(8):
                mm(p, ps[:, 512 * b:512 * b + W], W1[:, kc, 128 * f:128 * (f + 1)], hnTc[cb][:, kc, 0:W],
                   kc == 0, kc == 7, [t_W1, t_hnTc[cb]], [t_bank[b]])
            act(p, rb[r_][:, 0:W], ps[:, 512 * b:512 * b + W], AF.Relu, [t_bank[b]], [t_rb[r_]])
            tt(p, "pool" if f % 2 else "dve", aT[:, f, 0:W], rb[r_][:, 0:W], rb[r_][:, 0:W], ALU.mult,
               [t_rb[r_]], [t_aT])

    def stC(c):
        for j, i in enumerate(tiles_of(c)):
            hb = i % 4
            for nck in range(2):
                b = nbc[0] % 7
                nbc[0] += 1
                for f in range(32):
                    mm(p, ps[:, 512 * b:512 * (b + 1)], aT[:, f, 128 * j:128 * (j + 1)],
                       W2[:, f, 512 * nck:512 * (nck + 1)], f == 0, f == 31, [t_aT, t_W2], [t_bank[b]])
                tt(p, "dve", hs[hb][:, 512 * nck:512 * (nck + 1)], ps[:, 512 * b:512 * (b + 1)],
                   hs[hb][:, 512 * nck:512 * (nck + 1)], ALU.add, [t_bank[b], t_hs[hb]], [t_hs[hb]])
            if not final:
                dma(p, C.h[128 * i:128 * (i + 1), :], hs[hb][:], [t_hs[hb]], [Tok()])
            else:
                lo = max(128 * i, NM)
                hi = min(128 * (i + 1), NM + S)
                if hi > lo:
                    dma(p, C.out.ap()[lo - NM:hi - NM, :], hs[hb][lo - 128 * i:hi - 128 * i, :], [t_hs[hb]], [Tok()])

    def op_ld(i):
        dma(p, oTs[i % 2][:], C.oT[:, 128 * i:128 * (i + 1)].rearrange("(kc p) t -> p kc t", p=128), [], [t_oTs[i % 2]])
        dma(p, hs[i % 4][:], C.h[128 * i:128 * (i + 1), :], [], [t_hs[i % 4]])

    op_ld(0)
    for i in range(NT):
        if i + 1 < NT:
            op_ld(i + 1)
        hb = i % 4
        for nck in range(2):
            b = nbc[0] % 7
            nbc[0] += 1
            for kc in range(8):
                mm(p, ps[:, 512 * b:512 * (b + 1)], oTs[i % 2][:, kc, :], Wo[:, kc, 512 * nck:512 * (nck + 1)],
                   kc == 0, kc == 7, [t_oTs[i % 2], t_Wo], [t_bank[b]])
            tt(p, "dve", hs[hb][:, 512 * nck:512 * (nck + 1)], ps[:, 512 * b:512 * (b + 1)],
               hs[hb][:, 512 * nck:512 * (nck + 1)], ALU.add, [t_bank[b], t_hs[hb]], [t_hs[hb]])
        dma(p, C.h[128 * i:128 * (i + 1), :], hs[hb][:], [t_hs[hb]], [t_hd[i]])

    stA1(0)
    stA2(0)
    for c in range(nchunk):
        if c + 1 < nchunk:
            stA1(c + 1)
        stB(c)
        if c + 1 < nchunk:
            stA2(c + 1)
        stC(c)
    ph.finish()


def qk_path(p, q, b, pin, t_pin, cs_i):
    sq, ss, kn = q.sq[b], q.ss[b], q.kn[b]
    t_sq, t_ss, t_kn = q.t_sq[b], q.t_ss[b], q.t_kn[b]
    gqk, t_gqk = q.gqk, q.t_gqk
    pv = pin.rearrange("p (g d) -> p g d", d=64)
    act(p, sq[:, 0:512], pin[:, 0:512], AF.Square, t_pin, [t_sq])
    act(p, sq[:, 512:1024], pin[:, 512:1024], AF.Square, t_pin, [t_sq])
    p.op("dve", lambda e: e.tensor_reduce(out=ss[:, 0:16], in_=sq[:].rearrange("p (g d) -> p g d", d=64),
                                          axis=AX.X, op=ALU.add), [t_sq], [t_ss])
    act(p, ss[:, 16:32], ss[:, 0:16], AF.Ln, [t_ss], [t_ss], scale=1.0 / 64, bias=EPS)
    act(p, ss[:, 32:48], ss[:, 16:32], AF.Exp, [t_ss], [t_ss], scale=-0.5)
    rb_ = bass.AP(ss, 32, [[48, 128], [1, 16], [0, 64]])
    knv = kn[:].rearrange("p (g d) -> p g d", d=64)
    tt(p, "dve", knv, pv, rb_, ALU.mult, t_pin + [t_ss], [t_kn])
    kn8 = kn[:].rearrange("p (h d) -> p h d", d=128)
    gb_ = bass.AP(gqk, 0, [[128, 128], [0, 8], [1, 128]])
    tt(p, "pool", kn8, kn8, gb_, ALU.mult, [t_kn, t_gqk], [t_kn])


def qk_path2(p, q, b, cs_i):
    kn, kb, tmp = q.kn[b], q.kb[b], q.tmp[b]
    t_kn, t_kb, t_tmp = q.t_kn[b], q.t_kb[b], q.t_tmp[b]
    t_cs = q.t_cs
    knv = kn[:].rearrange("p (g d) -> p g d", d=64)
    cp(p, "act", kb[:], kn[:], [t_kn], [t_kb])
    x1 = knv[:, :, 0:8]
    x2 = knv[:, :, 8:16]
    cb_ = cs_i[:, 0:8]
    sb_ = cs_i[:, 8:16]
    cB = bass.AP(cb_.tensor, cb_.offset, [list(cb_.ap[0]), [0, 16], [1, 8]])
    sB = bass.AP(sb_.tensor, sb_.offset, [list(sb_.ap[0]), [0, 16], [1, 8]])
    tv = tmp[:].rearrange("p (a g d) -> p a g d", a=4, d=8)
    tt(p, "dve", tv[:, 0], x1, cB, ALU.mult, [t_kn, t_cs], [t_tmp])
    tt(p, "pool", tv[:, 1], x2, sB, ALU.mult, [t_kn, t_cs], [t_tmp])
    tt(p, "dve", tv[:, 2], x2, cB, ALU.mult, [t_kn, t_cs], [t_tmp])
    tt(p, "pool", tv[:, 3], x1, sB, ALU.mult, [t_kn, t_cs], [t_tmp])
    kbv = kb[:].rearrange("p (g d) -> p g d", d=64)
    tt(p, "dve", kbv[:, :, 0:8], tv[:, 0], tv[:, 1], ALU.subtract, [t_tmp, t_kb], [t_kb])
    tt(p, "dve", kbv[:, :, 8:16], tv[:, 2], tv[:, 3], ALU.add, [t_tmp, t_kb], [t_kb])


def alloc_qk(ph):
    d = Ctx()
    d.sq = [ph.sb([128, D], F32) for _ in range(2)]
    d.ss = [ph.sb([128, 48], F32) for _ in range(2)]
    d.kn = [ph.sb([128, D], F32) for _ in range(2)]
    d.kb = [ph.sb([128, D], BF16) for _ in range(2)]
    d.tmp = [ph.sb([128, 4 * 16 * 8], F32) for _ in range(2)]
    d.t_sq, d.t_ss, d.t_kn, d.t_tmp, d.t_kb = toks(2), toks(2), toks(2), toks(2), toks(2)
    d.gqk = ph.sb([128, 128], F32)
    d.t_gqk = Tok()
    d.cs = ph.sb([128, NT, 16], F32)
    d.t_cs = Tok()
    dma(ph.p, d.cs[:], ph.C.rope.ap().rearrange("(i p) c -> p i c", p=128), [], [d.t_cs])
    return d


def proj_qk_tiles(ph, C, q, W, t_W, ident, t_c, ps, psbf, t_bank, sink, vproj=None):
    p = ph.p
    hs = [ph.sb([128, D], F32) for _ in range(2)]
    t_hs = toks(2)
    hnb = [ph.sb([128, D], BF16) for _ in range(2)]
    t_hnb = toks(2)
    junk = ph.sb([128, D], BF16)
    t_junk = Tok()
    stat = [ph.sb([128, 4], F32) for _ in range(2)]
    t_stat = toks(2)
    hnTt = [ph.sb([128, 8, 128], BF16) for _ in range(2)]
    t_hnTt = toks(2)
    def ld(i):
        dma(p, hs[i % 2][:], C.h[128 * i:128 * (i + 1), :], [], [t_hs[i % 2]])

    def g0(i):
        b = i % 2
        norm_tile(p, hs[b][:], t_hs[b], hnb[b][:], t_hnb[b], stat[b], t_stat[b], junk[:], t_junk)

    def g1(i):
        b = i % 2
        transpose8(p, hnb[b], t_hnb[b], psbf[:, 7168:8192], t_bank[7], ident, t_c, hnTt[b][:], t_hnTt[b], ev="dve")

    def g2(i):
        b = i % 2
        kb0 = 2 * (i % 2)
        for nck in range(2):
            for kc in range(8):
                mm(p, ps[:, 512 * (kb0 + nck):512 * (kb0 + nck + 1)], hnTt[b][:, kc, :],
                   W[:, kc, 512 * nck:512 * (nck + 1)], kc == 0, kc == 7, [t_hnTt[b], t_W],
                   [t_bank[kb0], t_bank[kb0 + 1]])
        if vproj is not None:
            vproj(i, b, hnTt[b], t_hnTt[b])

    def g3(i):
        b = i % 2
        kb0 = 2 * (i % 2)
        qk_path(p, q, b, ps[:, 512 * kb0:512 * (kb0 + 2)], [t_bank[kb0], t_bank[kb0 + 1]], q.cs[:, i, :])

    def g4(i):
        qk_path2(p, q, i % 2, q.cs[:, i, :])

    def g5(i):
        sink(i, i % 2)

    stages = [g0, g1, g2, g3, g4, g5]
    ld(0)
    for n in range(NT + len(stages) - 1):
        if n + 1 < NT:
            ld(n + 1)
        for k in range(len(stages) - 1, -1, -1):
            if 0 <= n - k < NT:
                stages[k](n - k)


def phase_kv(C):
    ph = Phase(C, "kv")
    p = ph.p
    cst, t_c = ph.consts()
    ident = cst[:, 0, :]
    Wkv = ph.sb([128, 8, 2048], BF16)
    t_W = Tok()
    gK = ph.sb([128, 8], F32)
    t_gK = Tok()
    dma(p, gK[:], C.kv_norm_g.ap(), [], [t_gK])
    stg = [(ph.sb([128, 2048], F32), Tok()) for _ in range(3)]
    load_w(ph, stg, Wkv, t_W, C.kv_w.ap(), 8, 2048, gain=gK, gain_tok=t_gK, engs=("dve", "act"))
    q = alloc_qk(ph)
    dma(p, q.gqk[:], bass.AP(C.kv_k_norm_g, 0, [[0, 128], [1, 128]]), [], [q.t_gqk])
    KTt = [ph.sb([128, 8, 128], BF16) for _ in range(2)]
    t_KTt = toks(2)
    Vt = [ph.sb([128, D], BF16) for _ in range(2)]
    t_Vt = toks(2)
    ps = ph.psum()
    psbf = ps.bitcast(BF16)
    t_bank = toks(8)
    t_kd, t_vd = Tok(), Tok()

    def sink(i, b):
        transpose8(p, q.kb[b], q.t_kb[b], psbf[:, 6144:7168], t_bank[6], ident, t_c, KTt[b][:], t_KTt[b], ev="act")
        dma(p, C.KTd[:, :, 128 * i:128 * (i + 1)].rearrange("h f t -> f h t"), KTt[b][:], [t_KTt[b]], [Tok()])

    def vproj(i, b, hnTt, t_hnTt):
        for nck in range(2):
            for kc in range(8):
                mm(p, ps[:, 512 * (4 + nck):512 * (5 + nck)], hnTt[:, kc, :],
                   Wkv[:, kc, 1024 + 512 * nck:1024 + 512 * (nck + 1)], kc == 0, kc == 7, [t_hnTt, t_W], [t_bank[4]])
        cp(p, "dve", Vt[b][:], ps[:, 2048:3072], [t_bank[4]], [t_Vt[b]])
        dma(p, C.Vd[128 * i:128 * (i + 1), :], Vt[b][:], [t_Vt[b]], [Tok()])

    proj_qk_tiles(ph, C, q, Wkv, t_W, ident, t_c, ps, psbf, t_bank, sink, vproj)
    ph.finish()


def phase_b_attn(C, l):
    j = l - 2
    nc = C.nc
    lam_init = 0.8 - 0.6 * math.exp(-0.3 * l)
    outer = ExitStack()
    QT = outer.enter_context(nc.sbuf_tensor(f"bq{l}_QT", [128, 8, T], BF16))
    lw = outer.enter_context(nc.sbuf_tensor(f"bq{l}_lw", [128, 136], F32))
    gsub = outer.enter_context(nc.sbuf_tensor(f"bq{l}_gs", [128, 2], F32))
    ph = Phase(C, f"bq{l}")
    p = ph.p
    cst, t_c = ph.consts()
    ident = cst[:, 0, :]
    Wq = ph.sb([128, 8, D], BF16)
    t_Wq = Tok()
    gB = ph.sb([128, 8], F32)
    t_gB = Tok()
    dma(p, gB[:], C.b_norm_g.ap()[j], [], [t_gB])
    stg = [(ph.sb([128, 2048], F32), Tok()) for _ in range(3)]
    load_w(ph, stg, Wq, t_Wq, C.b_w_q.ap()[j], 8, D, gain=gB, gain_tok=t_gB, engs=("dve", "act"))
    q = alloc_qk(ph)
    dma(p, q.gqk[:], bass.AP(C.b_q_norm_g, j * 128, [[0, 128], [1, 128]]), [], [q.t_gqk])
    ts(p, "dve", q.gqk[:], q.gqk[:], 0.125, ALU.mult, [q.t_gqk], [q.t_gqk])
    lpb = ph.sb([128, 256], F32)
    t_l = Tok()
    dma(p, lpb[:], bass.AP(C.b_lambda, j * 256, [[0, 128], [1, 256]]), [], [t_l])
    tt(p, "dve", lw[:, 0:64], lpb[:, 0:64], lpb[:, 64:128], ALU.mult, [t_l], [t_l])
    tt(p, "dve", lw[:, 64:128], lpb[:, 128:192], lpb[:, 192:256], ALU.mult, [t_l], [t_l])
    p.op("dve", lambda e: e.tensor_reduce(out=lw[:, 128:130], in_=lw[:, 0:128].rearrange("p (a b) -> p a b", a=2),
                                          axis=AX.X, op=ALU.add), [t_l], [t_l])
    act(p, lw[:, 130:132], lw[:, 128:130], AF.Exp, [t_l], [t_l])
    tt(p, "dve", lw[:, 132:133], lw[:, 131:132], lw[:, 130:131], ALU.subtract, [t_l], [t_l])
    ts(p, "dve", lw[:, 133:134], lw[:, 132:133], -lam_init, ALU.add, [t_l], [t_l])
    t_gs = Tok()
    dma(p, gsub[:, 0:1], C.b_subln_g.ap()[j].rearrange("(p o) -> p o", o=1), [], [t_gs])
    ts(p, "dve", gsub[:, 1:2], gsub[:, 0:1], 1.0 - lam_init, ALU.mult, [t_gs], [t_gs])
    ps = ph.psum()
    psbf = ps.bitcast(BF16)
    t_bank = toks(8)
    t_QT = Tok()

    def sink(i, b):
        transpose8(p, q.kb[b], q.t_kb[b], psbf[:, 6144:7168], t_bank[6], ident, t_c,
                   QT[:, :, 128 * i:128 * (i + 1)], t_QT, ev="act")

    proj_qk_tiles(ph, C, q, Wq, t_Wq, ident, t_c, ps, psbf, t_bank, sink)
    ph.finish()

    ph = Phase(C, f"bat{l}")
    p = ph.p
    cst, t_c = ph.consts()
    ident = cst[:, 0, :]
    maskI2 = bass.AP(cst, 4 * 128, [[768, 128], [0, 2], [1, 128]])
    neglam = lw[:, 133:134]
    t_QT, t_l = Tok(), Tok()
    WQ = 384
    CH3 = [(WQ * c, WQ) for c in range(T // WQ)]
    gsB = ph.sb([128, 128], F32)
    t_gs = Tok()
    dma(p, gsB[:], bass.AP(C.b_subln_g, j * 128, [[0, 128], [1, 128]]), [], [t_gs])
    ts(p, "dve", gsB[:], gsB[:], 1.0 - lam_init, ALU.mult, [t_gs], [t_gs])
    gsB3 = bass.AP(gsB, 0, [[128, 128], [0, 3], [1, 128]])
    KTh = [ph.sb([128, T], BF16) for _ in range(2)]
    Vh = [ph.sb([128, NT, 129], BF16) for _ in range(2)]
    t_KTh, t_Vh = toks(2), toks(2)
    for hb in range(2):
        p.op("pool", (lambda hb_: lambda e: e.memset(Vh[hb_][:, :, 128:129], 1.0))(hb), [], [t_Vh[hb]])
    Pb = [ph.sb([128, 2, WQ], BF16) for _ in range(5)]
    t_P = toks(5)
    rr = ph.sb([128, 2, 3], F32)
    o1 = ph.sb([128, 3, 128], F32)
    o2 = ph.sb([128, 3, 128], F32)
    sqv = ph.sb([128, 3, 128], F32)
    ssb = ph.sb([128, 12], F32)
    onb = ph.sb([128, 3, 128], BF16)
    oS = [ph.sb([128, WQ], BF16) for _ in range(2)]
    t_ep = Tok()
    t_oS = toks(2)
    ps = ph.psum()
    psbf = ps.bitcast(BF16)
    t_S = toks(2)
    t_PV = [toks(2) for _ in range(2)]
    ns = [0]

    def sset():
        k = ns[0] % 2
        ns[0] += 1
        return k

    def sview(k, W, off=0):
        return ps[:, 1024 * k:1024 * (k + 1)].rearrange("p (m w) -> p m w", m=2)[:, :, off:W]

    def load_head(hd):
        hb = hd % 2
        dma(p, KTh[hb][:], C.KTd[hd], [], [t_KTh[hb]])
        dma(p, Vh[hb][:, :, 0:128], C.Vd[:, 128 * hd:128 * (hd + 1)].rearrange("(i p) d -> p i d", p=128),
            [], [t_Vh[hb]])

    jobs = []
    gci = 0
    for hd in range(8):
        for ci, (t0, W) in enumerate(CH3):
            imax = (t0 + W) // 128 - 1
            for i in range(imax + 1):
                jobs.append((hd, ci, gci, t0, W, imax, i))
            gci += 1

    def s0(n, job):
        hd, ci, g_, t0, W, imax, i = job
        hb = hd % 2
        off = max(0, 128 * i - t0)
        k = sset()
        for m in range(2):
            mm(p, ps[:, 512 * (2 * k + m) + off:512 * (2 * k + m) + W], KTh[hb][64 * m:64 * m + 64, 128 * i:128 * (i + 1)],
               QT[64 * m:64 * m + 64, hd, t0 + off:t0 + W], True, True, [t_KTh[hb], t_QT], [t_S[k]])
        act(p, Pb[n % 5][:, :, off:W], sview(k, W, off), AF.Exp, [t_S[k]], [t_P[n % 5]])
        if 128 * i - t0 >= 0:
            tt(p, "pool", Pb[n % 5][:, :, off:off + 128], Pb[n % 5][:, :, off:off + 128], maskI2, ALU.mult,
               [t_P[n % 5], t_c], [t_P[n % 5]])

    def s1(n, job):
        hd, ci, g_, t0, W, imax, i = job
        hb = hd % 2
        off = max(0, 128 * i - t0)
        pvs = g_ % 2
        for m in range(2):
            b_ = 4 + 2 * pvs + m
            for jb in range(off // 128, 3):
                last_i = t0 // 128 + jb
                mm(p, ps[:, 512 * b_ + 129 * jb:512 * b_ + 129 * (jb + 1)], Pb[n % 5][:, m, 128 * jb:128 * (jb + 1)],
                   Vh[hb][:, i, :], (i == 0 and jb == 0), (i == imax and jb == 2), [t_P[n % 5], t_Vh[hb]], [t_PV[pvs][m]])
            for _d in range(NDUMMY):
                mm(p, ps[:, 512 * b_ + 388:512 * b_ + 508], ident, Pb[n % 5][:, m, 0:120], False, False,
                   [t_P[n % 5], t_c], [t_PV[pvs][m]])
        if i == imax:
            pvv = ps[:, 512 * (4 + 2 * pvs):512 * (6 + 2 * pvs)].rearrange("p (m w) -> p m w", m=2)[:, :, 0:387] \
                .rearrange("p m (j c) -> p m j c", c=129)
            tpv = [t_PV[pvs][0], t_PV[pvs][1]]
            recip(p, rr[:], pvv[:, :, :, 128], tpv + [t_ep], [t_ep])
            ts(p, "dve", rr[:, 1, :], rr[:, 1, :], neglam, ALU.mult, [t_ep, t_l], [t_ep])
            r0 = bass.AP(rr, 0, [[6, 128], [1, 3], [0, 128]])
            r1_ = bass.AP(rr, 3, [[6, 128], [1, 3], [0, 128]])
            tt(p, "dve", o1[:], pvv[:, 0, :, 0:128], r0, ALU.mult, tpv + [t_ep], [t_ep])
            tt(p, "dve", o2[:], pvv[:, 1, :, 0:128], r1_, ALU.mult, tpv + [t_ep], [t_ep])
            tt(p, "pool", o1[:], o1[:], o2[:], ALU.add, [t_ep], [t_ep])
            act(p, sqv[:], o1[:], AF.Square, [t_ep], [t_ep])
            p.op("dve", lambda e: e.tensor_reduce(out=ssb[:, 0:3], in_=sqv[:], axis=AX.X, op=ALU.add), [t_ep], [t_ep])
            act(p, ssb[:, 4:7], ssb[:, 0:3], AF.Ln, [t_ep], [t_ep], scale=1.0 / 128, bias=EPS)
            act(p, ssb[:, 8:11], ssb[:, 4:7], AF.Exp, [t_ep], [t_ep], scale=-0.5)
            rsB = bass.AP(ssb, 8, [[12, 128], [1, 3], [0, 128]])
            tt(p, "dve", o2[:], o1[:], rsB, ALU.mult, [t_ep], [t_ep])
            tt(p, "pool", onb[:], o2[:], gsB3, ALU.mult, [t_ep, t_gs], [t_ep])
            k2 = sset()
            for jb in range(3):
                tr(p, psbf[:, 2048 * k2 + 128 * jb:2048 * k2 + 128 * (jb + 1)], onb[:, jb, :], ident, [t_ep, t_c], [t_S[k2]])
            ob = g_ % 2
            cp(p, "act", oS[ob][:], psbf[:, 2048 * k2:2048 * k2 + 384], [t_S[k2]], [t_oS[ob]])
            dma(p, C.oT[128 * hd:128 * (hd + 1), t0:t0 + W], oS[ob][:], [t_oS[ob]], [Tok()])
            if ci == len(CH3) - 1 and hd + 2 < 8:
                load_head(hd + 2)

    load_head(0)
    load_head(1)
    LAG = 3
    nj = len(jobs)
    for n in range(nj + LAG):
        if n < nj:
            s0(n, jobs[n])
        if n - LAG >= 0:
            s1(n - LAG, jobs[n - LAG])
    ph.finish()
    outer.close()


WNAMES = ["a_norm_g", "a_w_qkv", "a_w_o", "kv_norm_g", "kv_w", "kv_k_norm_g", "b_norm_g", "b_w_q",
          "b_q_norm_g", "b_lambda", "b_subln_g", "b_w_o", "mlp_norm_g", "mlp_w_in", "mlp_w_out"]
WSHAPES = {"a_norm_g": [2, 128, 8], "a_w_qkv": [2, 1024, 3072], "a_w_o": [2, 1024, 1024], "kv_norm_g": [128, 8],
           "kv_w": [1024, 2048], "kv_k_norm_g": [2, 64], "b_norm_g": [2, 128, 8], "b_w_q": [2, 1024, 1024],
           "b_q_norm_g": [2, 2, 64], "b_lambda": [2, 4, 64], "b_subln_g": [2, 128], "b_w_o": [2, 1024, 1024],
           "mlp_norm_g": [4, 128, 8], "mlp_w_in": [4, 1024, 4096], "mlp_w_out": [4, 4096, 1024]}


def build(steps=None, h_in=False, h_out=False):
    nc = bass.Bass("TRN2", target_bir_lowering=False)
    C = Ctx()
    C.nc = nc
    C._uid = [0]
    C.uid = lambda: (C._uid.__setitem__(0, C._uid[0] + 1), C._uid[0])[1]
    C.x = nc.dram_tensor("x", [S, D], F32, kind="ExternalInput")
    C.meta = nc.dram_tensor("meta_tokens", [NM, D], F32, kind="ExternalInput")
    for n in WNAMES:
        setattr(C, n, nc.dram_tensor(n, WSHAPES[n], F32, kind="ExternalInput"))
    C.cst = nc.dram_tensor("cst", [128, 768], BF16, kind="ExternalInput")
    C.rope = nc.dram_tensor("rope", [T, 16], F32, kind="ExternalInput")
    if h_in:
        C.hin = nc.dram_tensor("h_in", [T, D], F32, kind="ExternalInput")
    if h_out:
        C.h = nc.dram_tensor("h_out", [T, D], F32, kind="ExternalOutput").ap()
        C.out = None
    else:
        C.h = nc.dram_tensor("h_res", [T, D], F32).ap()
        C.out = nc.dram_tensor("out", [S, D], F32, kind="ExternalOutput")
    C.oT = nc.dram_tensor("oT_s", [D, T], BF16).ap()
    C.KTd = nc.dram_tensor("KT_s", [8, 128, T], BF16).ap()
    C.Vd = nc.dram_tensor("V_s", [T, D], BF16).ap()
    if steps is None:
        steps = ["init"]
        for l in range(4):
            if l == 2:
                steps.append("kv")
            steps += [f"attn{l}", f"post{l}"]
    zero_oT_pad.C = C
    with ExitStack() as top:
        C.sy = Sync(nc, top)
        for s in steps:
            if s == "init":
                phase_init(C)
            elif s == "copyin":
                ph = Phase(C, "cpin")
                th = Tok()
                zz = ph.sb([128, D], F32)
                tzz = Tok()
                ph.p.op("pool", lambda e: e.memset(zz[:], 0.0), [], [tzz])
                zero_oT_pad(ph.p, zz, tzz)
                for j in range(8):
                    r0, r1 = 528 * j, 528 * (j + 1)
                    dma(ph.p, C.h[r0:r1, :], C.hin.ap()[r0:r1, :], [], [th])
                ph.finish()
            elif s == "kv":
                phase_kv(C)
            elif s.startswith("attn"):
                l = int(s[4:])
                if l < 2:
                    phase_a_attn(C, l)
                else:
                    phase_b_attn(C, l)
            elif s.startswith("post"):
                l = int(s[4:])
                phase_post(C, l, C.a_w_o.ap()[l] if l < 2 else C.b_w_o.ap()[l - 2], final=(l == 3 and not h_out))
            elif s.startswith("oproj"):
                l = int(s[5:])
                phase_oproj(C, C.a_w_o.ap()[l] if l < 2 else C.b_w_o.ap()[l - 2])
            elif s.startswith("mlp"):
                l = int(s[3:])
                phase_mlp(C, l, final=(l == 3 and not h_out))
    return nc


def make_consts():
    j = np.arange(128)[:, None]
    s = np.arange(128)[None, :]
    ident = (j == s).astype(np.float32)
    negtri = -(j >= s).astype(np.float32)
    ones = np.ones((128, 128), np.float32)
    maskS = (j < s).astype(np.float32)
    maskI = (j <= s).astype(np.float32)
    onesS = ones / 128.0
    cst = np.concatenate([ident, negtri, ones, maskS, maskI, onesS], axis=1).astype(ml_dtypes.bfloat16)
    inv_freq = np.power(np.float32(500000.0), -np.arange(0, 16, 2, dtype=np.float32) / np.float32(16))
    ang = np.arange(T, dtype=np.float32)[:, None] * inv_freq[None, :]
    rope = np.concatenate([np.cos(ang), np.sin(ang)], axis=1).astype(np.float32)
    return cst, rope


_NC_CACHE = {}


def kernel(**inputs):
    x = np.ascontiguousarray(inputs["x"], dtype=np.float32)
    B = x.shape[0]
    cst, rope = make_consts()
    if "full" not in _NC_CACHE:
        _NC_CACHE["full"] = build()
    nc = _NC_CACHE["full"]
    shared = {n: np.ascontiguousarray(inputs[n], dtype=np.float32) for n in WNAMES}
    for n in ("a_norm_g", "kv_norm_g", "b_norm_g", "mlp_norm_g"):
        g = shared[n]
        shared[n] = np.ascontiguousarray(g.reshape(g.shape[:-1] + (8, 128)).swapaxes(-1, -2))
    shared["meta_tokens"] = np.ascontiguousarray(inputs["meta_tokens"], dtype=np.float32)
    shared["cst"] = cst
    shared["rope"] = rope
    in_maps = []
    for b in range(B):
        m = dict(shared)
        m["x"] = x[b]
        in_maps.append(m)
    res = run_bass_kernel_spmd(nc, in_maps, core_ids=list(range(B)))
    return np.stack([np.asarray(r["out"], dtype=np.float32) for r in res.results], axis=0)
```

```python
import math
from contextlib import ExitStack

import numpy as np
import ml_dtypes

import concourse.bass as bass
import concourse.mybir as mybir
from concourse.bass_utils import run_bass_kernel_spmd

F32 = mybir.dt.float32
BF16 = mybir.dt.bfloat16
AF = mybir.ActivationFunctionType
ALU = mybir.AluOpType
AX = mybir.AxisListType

D = 1024
S = 4096
NM = 16
T = 4224
NT = 33
DFF = 4096
EPS = 1e-6
CHUNKS = [(0, 128)] + [(128 + 512 * c, 512) for c in range(8)]
CHUNKS_A = [(0, 128)] + [(128 + 512 * c, 512) for c in range(7)] + [(3712, NM + S - 3712)]

ENGS = ("pe", "act", "dve", "pool", "sp")
NRING = 8
NDUMMY = 0
DMAQ = ("sp", "act")


class Tok:
    __slots__ = ("w", "rs")

    def __init__(self):
        self.w = None
        self.rs = []


def toks(n):
    return [Tok() for _ in range(n)]


class Op:
    __slots__ = ("eng", "fn", "deps", "sig", "val", "dma", "ring", "rval", "waits")

    def __init__(self, eng, fn, dma):
        self.eng = eng
        self.fn = fn
        self.deps = ()
        self.sig = False
        self.val = 0
        self.dma = dma
        self.ring = 0
        self.rval = 0
        self.waits = None


class Sync:
    def __init__(self, nc, stack):
        self.sem = {e: stack.enter_context(nc.semaphore(f"s_{e}")) for e in ENGS}
        self.cnt = {e: 0 for e in ENGS}
        self.ring = {q: [stack.enter_context(nc.semaphore(f"ring_{q}{i}")) for i in range(NRING)] for q in DMAQ}
        self.ndma = {q: 0 for q in DMAQ}


class Prog:
    def __init__(self, sync):
        self.sy = sync
        self.ops = []

    def op(self, eng, fn, r=(), w=(), dma=False):
        o = Op(eng, fn, dma)
        deps = set()
        for t in r:
            if t.w is not None:
                deps.add(t.w)
        for t in w:
            if t.w is not None:
                deps.add(t.w)
            deps.update(t.rs)
        o.deps = deps
        for t in r:
            t.rs.append(o)
        for t in w:
            t.w = o
            t.rs = []
        self.ops.append(o)
        return o

    def emit(self, block):
        sy = self.sy
        for o in self.ops:
            for a in o.deps:
                if a.dma:
                    continue
                if a.eng == "pe" and o.eng == "pe":
                    continue
                a.sig = True
        waited = {e: {} for e in ENGS}
        ring_out = {}
        for o in self.ops:
            ws = {}
            wd = waited[o.eng]
            for a in o.deps:
                if a.dma:
                    key = ("r", a.eng, a.ring)
                    v = a.rval
                else:
                    if a.eng == "pe" and o.eng == "pe":
                        continue
                    key = ("e", a.eng)
                    v = a.val
                if wd.get(key, 0) >= v:
                    continue
                if ws.get(key, 0) < v:
                    ws[key] = v
            if o.dma:
                j = sy.ndma[o.eng]
                sy.ndma[o.eng] += 1
                o.ring = j % NRING
                o.rval = 16 * (j // NRING + 1)
                if j >= NRING:
                    key = ("r", o.eng, o.ring)
                    v = 16 * (j // NRING)
                    if wd.get(key, 0) < v and ws.get(key, 0) < v:
                        ws[key] = v
                ring_out[(o.eng, o.ring)] = o.rval
            elif o.sig:
                sy.cnt[o.eng] += 1
                o.val = sy.cnt[o.eng]
            for k, v in ws.items():
                wd[k] = v
            o.waits = list(ws.items())
        final_sp = [(("r", q, r), v) for (q, r), v in ring_out.items() if waited["sp"].get(("r", q, r), 0) < v]
        per = {e: [o for o in self.ops if o.eng == e] for e in ENGS}

        def semof(key):
            return sy.ring[key[1]][key[2]] if key[0] == "r" else sy.sem[key[1]]

        def run(e, eng):
            for o in per[e]:
                for k, v in o.waits:
                    eng.wait_ge(semof(k), v)
                ins = o.fn(eng)
                if o.dma:
                    ins.then_inc(sy.ring[e][o.ring], 16)
                elif o.sig:
                    ins.then_inc(sy.sem[e], 1)
            if e == "sp":
                for k, v in final_sp:
                    eng.wait_ge(semof(k), v)

        block.tensor(lambda eng: run("pe", eng))
        block.scalar(lambda eng: run("act", eng))
        block.vector(lambda eng: run("dve", eng))
        block.gpsimd(lambda eng: run("pool", eng))
        block.sync(lambda eng: run("sp", eng))


def mm(p, out, lhsT, rhs, start, stop, r, w, skip=False):
    if skip:
        p.op("pe", lambda e: e.matmul(out, lhsT=lhsT, rhs=rhs, start=start, stop=stop, skip_group_check=True), r, w)
    else:
        p.op("pe", lambda e: e.matmul(out, lhsT=lhsT, rhs=rhs, start=start, stop=stop), r, w)


def tr(p, out, in_, ident, r, w):
    p.op("pe", lambda e: e.transpose(out, in_, ident), r, w)


def act(p, out, in_, func, r, w, **kw):
    p.op("act", lambda e: e.activation(out=out, in_=in_, func=func, **kw), r, w)


def tt(p, eng, out, in0, in1, op, r, w):
    p.op(eng, lambda e: e.tensor_tensor(out=out, in0=in0, in1=in1, op=op), r, w)


def ts(p, eng, out, in0, s1, op0, r, w, s2=None, op1=None):
    if op1 is None:
        p.op(eng, lambda e: e.tensor_scalar(out=out, in0=in0, scalar1=s1, scalar2=None, op0=op0), r, w)
    else:
        p.op(eng, lambda e: e.tensor_scalar(out=out, in0=in0, scalar1=s1, scalar2=s2, op0=op0, op1=op1), r, w)


def stt(p, eng, out, in0, scalar, in1, op0, op1, r, w):
    p.op(eng, lambda e: e.scalar_tensor_tensor(out=out, in0=in0, scalar=scalar, in1=in1, op0=op0, op1=op1), r, w)


def cp(p, eng, out, in_, r, w, scale=None):
    if eng == "act":
        if scale is None:
            p.op("act", lambda e: e.activation(out=out, in_=in_, func=AF.Copy), r, w)
        else:
            p.op("act", lambda e: e.activation(out=out, in_=in_, func=AF.Copy, scale=scale), r, w)
    else:
        p.op(eng, lambda e: e.tensor_copy(out=out, in_=in_), r, w)


def recip(p, out, in_, r, w):
    p.op("dve", lambda e: e.reciprocal(out=out, in_=in_), r, w)


def dma(p, out, in_, r, w, q="sp"):
    p.op(q, lambda e: e.dma_start(out=out, in_=in_), r, w, dma=True)


class Ctx:
    pass


class Phase:
    def __init__(self, C, name):
        self.C = C
        self.nc = C.nc
        self.name = name
        self.st = ExitStack()
        self.p = Prog(C.sy)
        self.n = 0

    def sb(self, shape, dt):
        self.n += 1
        return self.st.enter_context(self.nc.sbuf_tensor(f"{self.name}_t{self.n}", shape, dt))

    def psum(self):
        return self.st.enter_context(self.nc.psum_tensor(f"{self.name}_ps", [128, 4096], F32))

    def consts(self):
        c = self.sb([128, 6, 128], BF16)
        t = Tok()
        dma(self.p, c[:], self.C.cst.ap().rearrange("p (a b) -> p a b", a=6), [], [t])
        return c, t

    def finish(self):
        blk = self.st.enter_context(self.nc.Block())
        self.p.emit(blk)
        self.st.close()


def load_w(ph, stg, dst, dst_tok, src, nk, ncols, gain=None, gain_tok=None, engs=("pool",), cnt=[0], cap=2048, q="sp"):
    p = ph.p
    srcv = src.rearrange("(kc p) n -> p kc n", p=128)
    cw = min(ncols, cap)
    kb = max(1, min(nk, cap // cw))
    chunks = []
    for k0 in range(0, nk, kb):
        k1 = min(nk, k0 + kb)
        for c0 in range(0, ncols, cw):
            sa, stok = stg[cnt[0] % len(stg)]
            eng = engs[cnt[0] % len(engs)]
            cnt[0] += 1
            chunks.append((k0, k1, c0, sa, stok, eng))

    def issue(ch):
        k0, k1, c0, sa, stok, eng = ch
        sv = sa[:, 0:(k1 - k0) * cw].rearrange("p (k n) -> p k n", n=cw)
        dma(p, sv, srcv[:, k0:k1, c0:c0 + cw], [], [stok], q=q)

    def conv(ch):
        k0, k1, c0, sa, stok, eng = ch
        sv = sa[:, 0:(k1 - k0) * cw].rearrange("p (k n) -> p k n", n=cw)
        if gain is None:
            cp(p, eng, dst[:, k0:k1, c0:c0 + cw], sv, [stok], [dst_tok])
        else:
            for k in range(k0, k1):
                if eng == "act":
                    act(p, dst[:, k, c0:c0 + cw], sv[:, k - k0, :], AF.Copy, [stok, gain_tok], [dst_tok],
                        scale=gain[:, k:k + 1])
                else:
                    ts(p, eng, dst[:, k, c0:c0 + cw], sv[:, k - k0, :], gain[:, k:k + 1], ALU.mult,
                       [stok, gain_tok], [dst_tok])

    depth = len(stg) - 1
    n = len(chunks)
    for k in range(n + depth):
        if k < n:
            issue(chunks[k])
        if k - depth >= 0:
            conv(chunks[k - depth])


def norm_tile(p, hs, t_hs, hnb, t_hnb, stt_, t_st, junk, t_junk):
    act(p, junk, hs, AF.Square, [t_hs], [t_junk, t_st], accum_out=stt_[:, 0:1])
    act(p, stt_[:, 1:2], stt_[:, 0:1], AF.Ln, [t_st], [t_st], scale=1.0 / D, bias=EPS)
    act(p, stt_[:, 2:3], stt_[:, 1:2], AF.Exp, [t_st], [t_st], scale=-0.5)
    ts(p, "dve", hnb, hs, stt_[:, 2:3], ALU.mult, [t_hs, t_st], [t_hnb])


def transpose8(p, src, t_src, psbf_bank, t_bank, ident, t_c, dst, t_dst, ev="dve"):
    for kc in range(8):
        tr(p, psbf_bank[:, kc * 128:(kc + 1) * 128], src[:, kc * 128:(kc + 1) * 128], ident, [t_src, t_c], [t_bank])
    cp(p, ev, dst, psbf_bank.rearrange("p (a b) -> p a b", a=8), [t_bank], [t_dst])


def zero_oT_pad(p, z, tz, C=None):
    npad = T - NM - S
    zb = z.bitcast(BF16)[:, 0:8 * npad].rearrange("p (k t) -> p k t", k=8)
    dma(p, zero_oT_pad.C.oT[:, NM + S:T].rearrange("(kc p) t -> p kc t", p=128), zb, [tz], [Tok()])


def phase_init(C):
    ph = Phase(C, "init")
    p = ph.p
    z = ph.sb([128, D], F32)
    tz = Tok()
    th = Tok()
    p.op("pool", lambda e: e.memset(z[:], 0.0), [], [tz])
    dma(p, C.h[0:NM, :], C.meta.ap(), [], [th])
    for j in range(8):
        dma(p, C.h[NM + 512 * j:NM + 512 * (j + 1), :], C.x.ap()[512 * j:512 * (j + 1), :], [], [th])
    dma(p, C.h[NM + S:T, :], z[0:T - NM - S, :], [tz], [th])
    zero_oT_pad(p, z, tz)
    ph.finish()


def phase_a_attn(C, l):
    ph = Phase(C, f"aat{l}")
    p = ph.p
    nc = C.nc
    cst, t_c = ph.consts()
    ident, negtri, ones, maskS = cst[:, 0, :], cst[:, 1, :], cst[:, 2, :], cst[:, 3, :]
    hnT = ph.sb([128, 8, T], BF16)
    t_hnT = toks(NT)
    hs = [ph.sb([128, D], F32) for _ in range(2)]
    t_hs = toks(2)
    hnb = [ph.sb([128, D], BF16) for _ in range(2)]
    t_hnb = toks(2)
    junk = ph.sb([128, D], BF16)
    t_junk = Tok()
    stat = [ph.sb([128, 4], F32) for _ in range(2)]
    t_stat = toks(2)
    gA = ph.sb([128, 8], F32)
    t_gA = Tok()
    wst = [ph.sb([128, 8 * 384], F32) for _ in range(2)]
    t_wst = toks(2)
    wbf = [ph.sb([128, 8, 384], BF16) for _ in range(2)]
    t_wbf = toks(2)
    QT2 = [ph.sb([128, T], BF16) for _ in range(2)]
    KT2 = [ph.sb([128, T], BF16) for _ in range(2)]
    V2 = [ph.sb([128, NT, 128], BF16) for _ in range(2)]
    t_QT2, t_KT2, t_V2 = toks(2), toks(2), toks(2)
    eb = [ph.sb([128, 512], F32) for _ in range(3)]
    t_e = toks(3)
    SPt = [ph.sb([128, 512], BF16) for _ in range(4)]
    t_S = toks(4)
    Xb = [ph.sb([128, 512], F32) for _ in range(3)]
    t_X = toks(3)
    wt = [ph.sb([128, 512], BF16) for _ in range(4)]
    t_W = toks(4)
    Racc = [ph.sb([128, 512], F32) for _ in range(2)]
    t_R = toks(2)
    oS = [ph.sb([128, 512], BF16) for _ in range(2)]
    t_oS = toks(2)
    ps = ph.psum()
    psbf = ps.bitcast(BF16)
    t_bank = toks(8)
    t_Oh = toks(2)
    t_od = Tok()

    def bank(b):
        return ps[:, 512 * b:512 * (b + 1)]

    dma(p, gA[:], C.a_norm_g.ap()[l], [], [t_gA])
    def p1_norm(i):
        b = i % 2
        dma(p, hs[b][:], C.h[128 * i:128 * (i + 1), :], [], [t_hs[b]])
        norm_tile(p, hs[b][:], t_hs[b], hnb[b][:], t_hnb[b], stat[b], t_stat[b], junk[:], t_junk)

    p1_norm(0)
    for i in range(NT):
        b = i % 2
        if i + 1 < NT:
            p1_norm(i + 1)
        transpose8(p, hnb[b], t_hnb[b], psbf[:, 7168:8192], t_bank[7], ident, t_c,
                   hnT[:, :, 128 * i:128 * (i + 1)], t_hnT[i], ev="dve")
    wq = C.a_w_qkv.ap()[l]

    def proj_units(g):
        wb_ = g % 2
        QT, KT, V = QT2[g % 2], KT2[g % 2], V2[g % 2]
        t_QT, t_KT, t_V = t_QT2[g % 2], t_KT2[g % 2], t_V2[g % 2]
        units = []

        def u_w():
            for sec in range(3):
                dma(p, wst[wb_][:].rearrange("p (k s n) -> p k s n", k=8, s=3)[:, :, sec, :],
                    wq[:, sec * D + g * 128: sec * D + (g + 1) * 128].rearrange("(kc p) n -> p kc n", p=128),
                    [], [t_wst[wb_]])
            for kc in range(8):
                ts(p, "dve", wbf[wb_][:, kc, :], wst[wb_][:, kc * 384:(kc + 1) * 384], gA[:, kc:kc + 1], ALU.mult,
                   [t_wst[wb_], t_gA], [t_wbf[wb_]])
        units.append(u_w)

        def mk_qk(sec, dst, t_dst, scl, t0, W):
            def u():
                for kc in range(8):
                    mm(p, bank(7)[:, 0:W], wbf[wb_][:, kc, sec * 128:(sec + 1) * 128], hnT[:, kc, t0:t0 + W],
                       kc == 0, kc == 7, [t_wbf[wb_]] + t_hnT[t0 // 128:(t0 + W) // 128], [t_bank[7]])
                if scl is None:
                    cp(p, "dve", dst[:, t0:t0 + W], bank(7)[:, 0:W], [t_bank[7]], [t_dst])
                else:
                    ts(p, "dve", dst[:, t0:t0 + W], bank(7)[:, 0:W], scl, ALU.mult, [t_bank[7]], [t_dst])
            return u

        def mk_v(q4):
            def u():
                tiles = list(range(4 * q4, min(NT, 4 * q4 + 4)))
                for j, i in enumerate(tiles):
                    for kc in range(8):
                        mm(p, bank(7)[:, 128 * j:128 * (j + 1)], hnT[:, kc, 128 * i:128 * (i + 1)],
                           wbf[wb_][:, kc, 256:384], kc == 0 and j == 0, kc == 7 and j == len(tiles) - 1,
                           [t_wbf[wb_], t_hnT[i]], [t_bank[7]])
                n = len(tiles)
                cp(p, "dve", V[:, tiles[0]:tiles[0] + n, :],
                   bank(7)[:, 0:128 * n].rearrange("p (a b) -> p a b", b=128), [t_bank[7]], [t_V])
            return u

        for (t0, W) in CHUNKS:
            units.append(mk_qk(0, QT, t_QT, 0.125, t0, W))
            units.append(mk_qk(1, KT, t_KT, None, t0, W))
        for q4 in range(9):
            units.append(mk_v(q4))
        return units

    for u in proj_units(0):
        u()
    for g in range(8):
        QT, KT, V = QT2[g % 2], KT2[g % 2], V2[g % 2]
        t_QT, t_KT, t_V = t_QT2[g % 2], t_KT2[g % 2], t_V2[g % 2]
        nxt = proj_units(g + 1) if g + 1 < 8 else []
        jobs = []
        for ci, (t0, W) in enumerate(CHUNKS_A):
            imax = (t0 + W + 127) // 128 - 1
            for i in range(imax, -1, -1):
                for hh in range(2):
                    jobs.append((hh, ci, t0, W, imax, i))

        def stA(n, job):
            hh, ci, t0, W, imax, i = job
            off = max(0, 128 * i - t0)
            diag = 128 * i - t0 >= 0
            pb = 64 * hh
            zb, e_, s_ = n % 2, n % 3, n % 4
            if i == imax:
                p.op("pool", lambda e: e.memset(Racc[hh][:, 0:W], 0.0), [], [t_R[hh]])
            if hh == 0:
                for d_ in range(2):
                    mm(p, bank((n + d_) % 2)[:, off:W], KT[64 * d_:64 * d_ + 64, 128 * i:128 * (i + 1)],
                       QT[64 * d_:64 * d_ + 64, t0 + off:t0 + W], True, True, [t_KT, t_QT], [t_bank[(n + d_) % 2]])
            act(p, eb[e_][:, off:W], bank(zb)[:, off:W], AF.Exp, [t_bank[zb]], [t_e[e_]])
            act(p, SPt[s_][:, off:W], eb[e_][:, off:W], AF.Ln, [t_e[e_]], [t_S[s_]], bias=1.0)
            if diag:
                hi = min(off + 128, W)
                tt(p, "pool", SPt[s_][:, off:hi], SPt[s_][:, off:hi], maskS[:, 0:hi - off], ALU.mult,
                   [t_S[s_], t_c], [t_S[s_]])

        def stB(n, job):
            hh, ci, t0, W, imax, i = job
            off = max(0, 128 * i - t0)
            zb, s_, x_, cb = 2 + n % 2, n % 4, n % 3, 4 + n % 2
            last = i == 0
            pb = 64 * hh
            if hh == 0:
                for d_ in range(2):
                    mm(p, bank(2 + (n + d_) % 2)[:, off:W], KT[64 * d_:64 * d_ + 64, 128 * i:128 * (i + 1)],
                       QT[64 * d_:64 * d_ + 64, t0 + off:t0 + W], True, False, [t_KT, t_QT],
                       [t_bank[2 + (n + d_) % 2]])
            mm(p, bank(zb)[:, off:W], negtri, SPt[s_][:, off:W], False, True, [t_S[s_], t_c], [t_bank[zb]])
            if not last:
                mm(p, bank(cb)[:, off:W], ones, SPt[s_][:, off:W], True, True, [t_S[s_], t_c], [t_bank[cb]])
            tt(p, "dve", Xb[x_][:, off:W], bank(zb)[:, off:W], Racc[hh][:, off:W], ALU.subtract,
               [t_bank[zb], t_R[hh]], [t_X[x_]])
            if not last:
                tt(p, "dve", Racc[hh][:, off:W], bank(cb)[:, off:W], Racc[hh][:, off:W], ALU.add,
                   [t_bank[cb], t_R[hh]], [t_R[hh]])

        def stC(n, job):
            hh, ci, t0, W, imax, i = job
            off = max(0, 128 * i - t0)
            diag = 128 * i - t0 >= 0
            x_, w_ = n % 3, n % 4
            act(p, wt[w_][:, off:W], Xb[x_][:, off:W], AF.Exp, [t_X[x_]], [t_W[w_]])
            if diag:
                hi = min(off + 128, W)
                tt(p, "pool", wt[w_][:, off:hi], wt[w_][:, off:hi], maskS[:, 0:hi - off], ALU.mult,
                   [t_W[w_], t_c], [t_W[w_]])
                if off > 0:
                    p.op("pool", (lambda ap: lambda e: e.memset(ap, 0.0))(wt[w_][:, 0:off]), [], [t_W[w_]])

        def stD(n, job):
            hh, ci, t0, W, imax, i = job
            off = max(0, 128 * i - t0)
            pb = 64 * hh
            w_ = n % 4
            mm(p, bank(6)[pb:pb + 64, 0:W], V[:, i, pb:pb + 64], wt[w_][:, 0:W], i == imax, i == 0,
               [t_W[w_], t_V], [t_Oh[hh]])
            if i == 0:
                ob = ci % 2
                cp(p, "dve", oS[ob][pb:pb + 64, 0:W], bank(6)[pb:pb + 64, 0:W], [t_Oh[hh]], [t_oS[ob]])
                row = (2 * g + hh) * 64
                dma(p, C.oT[row:row + 64, t0:t0 + W], oS[ob][pb:pb + 64, 0:W], [t_oS[ob]], [Tok()])

        nj = len(jobs)
        step = max(1, (nj - 20) // max(1, len(nxt)))
        ui = 0
        for n in range(nj + 2):
            if n < nj:
                stA(n, jobs[n])
            if 0 <= n - 1 < nj:
                stB(n - 1, jobs[n - 1])
                stC(n - 1, jobs[n - 1])
            if 0 <= n - 2 < nj:
                stD(n - 2, jobs[n - 2])
            if ui < len(nxt) and n >= 4 and (n - 4) % step == 0:
                nxt[ui]()
                ui += 1
        while ui < len(nxt):
            nxt[ui]()
            ui += 1
    ph.finish()


def phase_oproj(C, w_o):
    ph = Phase(C, f"op{C.uid()}")
    p = ph.p
    Wo = ph.sb([128, 8, D], BF16)
    t_Wo = Tok()
    stg = [(ph.sb([128, 2048], F32), Tok()) for _ in range(3)]
    load_w(ph, stg, Wo, t_Wo, w_o, 8, D, engs=("dve", "act"))
    oTs = [ph.sb([128, 8, 512], BF16) for _ in range(2)]
    t_oTs = toks(2)
    hs = [ph.sb([128, D], F32) for _ in range(4)]
    t_hs = toks(4)
    ps = ph.psum()
    t_bank = toks(8)
    t_hd = toks(NT)
    nb = 0
    for ci, (t0, W) in enumerate(CHUNKS):
        ob = ci % 2
        dma(p, oTs[ob][:, :, 0:W], C.oT[:, t0:t0 + W].rearrange("(kc p) t -> p kc t", p=128), [], [t_oTs[ob]])
        for j in range(W // 128):
            i = t0 // 128 + j
            hb = i % 4
            dma(p, hs[hb][:], C.h[128 * i:128 * (i + 1), :], [t_hd[i]], [t_hs[hb]])
            for nck in range(2):
                b = nb % 8
                nb += 1
                for kc in range(8):
                    mm(p, ps[:, 512 * b:512 * (b + 1)], oTs[ob][:, kc, 128 * j:128 * (j + 1)],
                       Wo[:, kc, 512 * nck:512 * (nck + 1)], kc == 0, kc == 7, [t_oTs[ob], t_Wo], [t_bank[b]])
                tt(p, "dve", hs[hb][:, 512 * nck:512 * (nck + 1)], ps[:, 512 * b:512 * (b + 1)],
                   hs[hb][:, 512 * nck:512 * (nck + 1)], ALU.add, [t_bank[b], t_hs[hb]], [t_hs[hb]])
            dma(p, C.h[128 * i:128 * (i + 1), :], hs[hb][:], [t_hs[hb]], [t_hd[i]])
    ph.finish()


def phase_mlp(C, l, final):
    ph = Phase(C, f"mlp{l}")
    p = ph.p
    cst, t_c = ph.consts()
    ident = cst[:, 0, :]
    W1 = ph.sb([128, 8, DFF], BF16)
    W2 = ph.sb([128, 32, D], BF16)
    t_W1, t_W2 = Tok(), Tok()
    gM = ph.sb([128, 8], F32)
    t_gM = Tok()
    dma(p, gM[:], C.mlp_norm_g.ap()[l], [], [t_gM])
    stg = [(ph.sb([128, 2048], F32), Tok()) for _ in range(3)]
    load_w(ph, stg, W1, t_W1, C.mlp_w_in.ap()[l], 8, DFF, gain=gM, gain_tok=t_gM, engs=("dve", "act"))
    load_w(ph, stg, W2, t_W2, C.mlp_w_out.ap()[l], 32, D, engs=("dve", "act"))
    hs = [ph.sb([128, D], F32) for _ in range(4)]
    t_hs = toks(4)
    hnb = [ph.sb([128, D], BF16) for _ in range(2)]
    t_hnb = toks(2)
    junk = ph.sb([128, D], BF16)
    t_junk = Tok()
    stat = [ph.sb([128, 4], F32) for _ in range(2)]
    t_stat = toks(2)
    hnTc = [ph.sb([128, 8, 256], BF16) for _ in range(2)]
    t_hnTc = toks(2)
    aT = ph.sb([128, 32, 256], BF16)
    t_aT = Tok()
    rb = [ph.sb([128, 256], F32) for _ in range(3)]
    t_rb = toks(3)
    ps = ph.psum()
    psbf = ps.bitcast(BF16)
    t_bank = toks(8)
    t_hd = Tok()
    nchunk = (NT + 1) // 2
    nbc = [0]

    def tiles_of(c):
        return list(range(2 * c, min(NT, 2 * c + 2)))

    def stA1(c):
        for j, i in enumerate(tiles_of(c)):
            hb = i % 4
            b2 = i % 2
            dma(p, hs[hb][:], C.h[128 * i:128 * (i + 1), :], [], [t_hs[hb]])
            norm_tile(p, hs[hb][:], t_hs[hb], hnb[b2][:], t_hnb[b2], stat[b2], t_stat[b2], junk[:], t_junk)

    def stA2(c):
        cb = c % 2
        for j, i in enumerate(tiles_of(c)):
            b2 = i % 2
            transpose8(p, hnb[b2], t_hnb[b2], psbf[:, 7168:8192], t_bank[7], ident, t_c,
                       hnTc[cb][:, :, 128 * j:128 * (j + 1)], t_hnTc[cb], ev="dve")

    def stB(c):
        cb = c % 2
        W = 128 * len(tiles_of(c))
        for f in range(32):
            b = nbc[0] % 7
            nbc[0] += 1
            r_ = f % 3
            for kc in range(8):
                mm(p, ps[:, 512 * b:512 * b + W], W1[:, kc, 128 * f:128 * (f + 1)], hnTc[cb][:, kc, 0:W],
                   kc == 0, kc == 7, [t_W1, t_hnTc[cb]], [t_bank[b]])
            act(p, rb[r_][:, 0:W], ps[:, 512 * b:512 * b + W], AF.Relu, [t_bank[b]], [t_rb[r_]])
            tt(p, "pool" if f % 2 else "dve", aT[:, f, 0:W], rb[r_][:, 0:W], rb[r_][:, 0:W], ALU.mult,
               [t_rb[r_]], [t_aT])

    def stC(c):
        for j, i in enumerate(tiles_of(c)):
            hb = i % 4
            for nck in range(2):
                b = nbc[0] % 7
                nbc[0] += 1
                for f in range(32):
                    mm(p, ps[:, 512 * b:512 * (b + 1)], aT[:, f, 128 * j:128 * (j + 1)],
                       W2[:, f, 512 * nck:512 * (nck + 1)], f == 0, f == 31, [t_aT, t_W2], [t_bank[b]])
                tt(p, "dve", hs[hb][:, 512 * nck:512 * (nck + 1)], ps[:, 512 * b:512 * (b + 1)],
                   hs[hb][:, 512 * nck:512 * (nck + 1)], ALU.add, [t_bank[b], t_hs[hb]], [t_hs[hb]])
            if not final:
                dma(p, C.h[128 * i:128 * (i + 1), :], hs[hb][:], [t_hs[hb]], [Tok()])
            else:
                lo = max(128 * i, NM)
                hi = min(128 * (i + 1), NM + S)
                if hi > lo:
                    dma(p, C.out.ap()[lo - NM:hi - NM, :], hs[hb][lo - 128 * i:hi - 128 * i, :], [t_hs[hb]], [Tok()])

    stA1(0)
    stA2(0)
    for c in range(nchunk):
        if c + 1 < nchunk:
            stA1(c + 1)
        stB(c)
        if c + 1 < nchunk:
            stA2(c + 1)
        stC(c)
    ph.finish()


def phase_post(C, l, w_o, final):
    ph = Phase(C, f"post{l}")
    p = ph.p
    cst, t_c = ph.consts()
    ident = cst[:, 0, :]
    W1 = ph.sb([128, 8, DFF], BF16)
    W2 = ph.sb([128, 32, D], BF16)
    t_W1, t_W2 = Tok(), Tok()
    gM = ph.sb([128, 8], F32)
    t_gM = Tok()
    dma(p, gM[:], C.mlp_norm_g.ap()[l], [], [t_gM])
    Wo = ph.sb([128, 8, D], BF16)
    t_Wo = Tok()
    stg = [(ph.sb([128, 512], F32), Tok()) for _ in range(4)]
    load_w(ph, stg, Wo, t_Wo, w_o, 8, D, engs=("act",), cap=512, q="act")
    load_w(ph, stg, W1, t_W1, C.mlp_w_in.ap()[l], 8, DFF, gain=gM, gain_tok=t_gM, engs=("act",), cap=512, q="act")
    load_w(ph, stg, W2, t_W2, C.mlp_w_out.ap()[l], 32, D, engs=("act",), cap=512, q="act")
    hs = [ph.sb([128, D], F32) for _ in range(4)]
    t_hs = toks(4)
    hnb = [ph.sb([128, D], BF16) for _ in range(2)]
    t_hnb = toks(2)
    oTs = [ph.sb([128, 8, 128], BF16) for _ in range(2)]
    t_oTs = toks(2)
    t_hd = toks(NT)
    stat = [ph.sb([128, 4], F32) for _ in range(2)]
    t_stat = toks(2)
    hnTc = [ph.sb([128, 8, 256], BF16) for _ in range(2)]
    t_hnTc = toks(2)
    aT = ph.sb([128, 32, 256], BF16)
    t_aT = Tok()
    rb = [ph.sb([128, 256], F32) for _ in range(3)]
    t_rb = toks(3)
    ps = ph.psum()
    psbf = ps.bitcast(BF16)
    t_bank = toks(8)
    nchunk = (NT + 1) // 2
    nbc = [0]

    def tiles_of(c):
        return list(range(2 * c, min(NT, 2 * c + 2)))

    def stA1(c):
        for j, i in enumerate(tiles_of(c)):
            hb = i % 4
            b2 = i % 2
            dma(p, hs[hb][:], C.h[128 * i:128 * (i + 1), :], [t_hd[i]], [t_hs[hb]])
            norm_tile(p, hs[hb][:], t_hs[hb], hnb[b2][:], t_hnb[b2], stat[b2], t_stat[b2], hnb[b2][:], t_hnb[b2])

    def stA2(c):
        cb = c % 2
        for j, i in enumerate(tiles_of(c)):
            b2 = i % 2
            transpose8(p, hnb[b2], t_hnb[b2], psbf[:, 7168:8192], t_bank[7], ident, t_c,
                       hnTc[cb][:, :, 128 * j:128 * (j + 1)], t_hnTc[cb], ev="dve")

    def stB(c):
        cb = c % 2
        W = 128 * len(tiles_of(c))
        for f in range(32):
            b = nbc[0] % 7
            nbc[0] += 1
            r_ = f % 3
            for kc in range(8):
                mm(p, ps[:, 512 * b:512 * b + W], W1[:, kc, 128 * f:128 * (f + 1)], hnTc[cb][:, kc, 0:W],
                   kc == 0, kc == 7, [t_W1, t_hnTc[cb]], [t_bank[b]])
            act(p, rb[r_][:, 0:W], ps[:, 512 * b:512 * b + W], AF.Relu, [t_bank[b]], [t_rb[r_]])
            tt(p, "pool" if f % 2 else "dve", aT[:, f, 0:W], rb[r_][:, 0:W], rb[r_][:, 0:W], ALU.mult,
               [t_rb[r_]], [t_aT])

    def stC(c):
        for j, i in enumerate(tiles_of(c)):
            hb = i % 4
            for nck in range(2):
                b = nbc[0] % 7
                nbc[0] += 1
                for f in range(32):
                    mm(p, ps[:, 512 * b:512 * (b + 1)], aT[:, f, 128 * j:128 * (j + 1)],
                       W2[:, f, 512 * nck:512 * (nck + 1)], f == 0, f == 31, [t_aT, t_W2], [t_bank[b]])
                tt(p, "dve", hs[hb][:, 512 * nck:512 * (nck + 1)], ps[:, 512 * b:512 * (b + 1)],
                   hs[hb][:, 512 * nck:512 * (nck + 1)], ALU.add, [t_bank[b], t_hs[hb]], [t_hs[hb]])
            if not final:
                dma(p, C.h[128 * i:128 * (i + 1), :], hs[hb][:], [t_hs[hb]], [Tok()])
            else:
                lo = max(128 * i, NM)
                hi = min(128 * (i + 1), NM + S)
                if hi > lo:
                    dma(p, C.out.ap()[lo - NM:hi - NM, :], hs[hb][lo - 128 * i:hi - 128 * i, :], [t_hs[hb]], [Tok()])

    def op_ld(i):
        dma(p, oTs[i % 2][:], C.oT[:, 128 * i:128 * (i + 1)].rearrange("(kc p) t -> p kc t", p=128), [], [t_oTs[i % 2]])
        dma(p, hs[i % 4][:], C.h[128 * i:128 * (i + 1), :], [], [t_hs[i % 4]])

    op_ld(0)
    for i in range(NT):
        if i + 1 < NT:
            op_ld(i + 1)
        hb = i % 4
        for nck in range(2):
            b = nbc[0] % 7
            nbc[0] += 1
            for kc in range(8):
                mm(p, ps[:, 512 * b:512 * (b + 1)], oTs[i % 2][:, kc, :], Wo[:, kc, 512 * nck:512 * (nck + 1)],
                   kc == 0, kc == 7, [t_oTs[i % 2], t_Wo], [t_bank[b]])
            tt(p, "dve", hs[hb][:, 512 * nck:512 * (nck + 1)], ps[:, 512 * b:512 * (b + 1)],
               hs[hb][:, 512 * nck:512 * (nck + 1)], ALU.add, [t_bank[b], t_hs[hb]], [t_hs[hb]])
        dma(p, C.h[128 * i:128 * (i + 1), :], hs[hb][:], [t_hs[hb]], [t_hd[i]])

    stA1(0)
    stA2(0)
    for c in range(nchunk):
        if c + 1 < nchunk:
            stA1(c + 1)
        stB(c)
        if c + 1 < nchunk:
            stA2(c + 1)
        stC(c)
    ph.finish()


def qk_path(p, q, b, pin, t_pin, cs_i):
    sq, ss, kn = q.sq[b], q.ss[b], q.kn[b]
    t_sq, t_ss, t_kn = q.t_sq[b], q.t_ss[b], q.t_kn[b]
    gqk, t_gqk = q.gqk, q.t_gqk
    pv = pin.rearrange("p (g d) -> p g d", d=64)
    act(p, sq[:, 0:512], pin[:, 0:512], AF.Square, t_pin, [t_sq])
    act(p, sq[:, 512:1024], pin[:, 512:1024], AF.Square, t_pin, [t_sq])
    p.op("dve", lambda e: e.tensor_reduce(out=ss[:, 0:16], in_=sq[:].rearrange("p (g d) -> p g d", d=64),
                                          axis=AX.X, op=ALU.add), [t_sq], [t_ss])
    act(p, ss[:, 16:32], ss[:, 0:16], AF.Ln, [t_ss], [t_ss], scale=1.0 / 64, bias=EPS)
    act(p, ss[:, 32:48], ss[:, 16:32], AF.Exp, [t_ss], [t_ss], scale=-0.5)
    rb_ = bass.AP(ss, 32, [[48, 128], [1, 16], [0, 64]])
    knv = kn[:].rearrange("p (g d) -> p g d", d=64)
    tt(p, "dve", knv, pv, rb_, ALU.mult, t_pin + [t_ss], [t_kn])
    kn8 = kn[:].rearrange("p (h d) -> p h d", d=128)
    gb_ = bass.AP(gqk, 0, [[128, 128], [0, 8], [1, 128]])
    tt(p, "pool", kn8, kn8, gb_, ALU.mult, [t_kn, t_gqk], [t_kn])


def qk_path2(p, q, b, cs_i):
    kn, kb, tmp = q.kn[b], q.kb[b], q.tmp[b]
    t_kn, t_kb, t_tmp = q.t_kn[b], q.t_kb[b], q.t_tmp[b]
    t_cs = q.t_cs
    knv = kn[:].rearrange("p (g d) -> p g d", d=64)
    cp(p, "act", kb[:], kn[:], [t_kn], [t_kb])
    x1 = knv[:, :, 0:8]
    x2 = knv[:, :, 8:16]
    cb_ = cs_i[:, 0:8]
    sb_ = cs_i[:, 8:16]
    cB = bass.AP(cb_.tensor, cb_.offset, [list(cb_.ap[0]), [0, 16], [1, 8]])
    sB = bass.AP(sb_.tensor, sb_.offset, [list(sb_.ap[0]), [0, 16], [1, 8]])
    tv = tmp[:].rearrange("p (a g d) -> p a g d", a=4, d=8)
    tt(p, "dve", tv[:, 0], x1, cB, ALU.mult, [t_kn, t_cs], [t_tmp])
    tt(p, "pool", tv[:, 1], x2, sB, ALU.mult, [t_kn, t_cs], [t_tmp])
    tt(p, "dve", tv[:, 2], x2, cB, ALU.mult, [t_kn, t_cs], [t_tmp])
    tt(p, "pool", tv[:, 3], x1, sB, ALU.mult, [t_kn, t_cs], [t_tmp])
    kbv = kb[:].rearrange("p (g d) -> p g d", d=64)
    tt(p, "dve", kbv[:, :, 0:8], tv[:, 0], tv[:, 1], ALU.subtract, [t_tmp, t_kb], [t_kb])
    tt(p, "dve", kbv[:, :, 8:16], tv[:, 2], tv[:, 3], ALU.add, [t_tmp, t_kb], [t_kb])


def alloc_qk(ph):
    d = Ctx()
    d.sq = [ph.sb([128, D], F32) for _ in range(2)]
    d.ss = [ph.sb([128, 48], F32) for _ in range(2)]
    d.kn = [ph.sb([128, D], F32) for _ in range(2)]
    d.kb = [ph.sb([128, D], BF16) for _ in range(2)]
    d.tmp = [ph.sb([128, 4 * 16 * 8], F32) for _ in range(2)]
    d.t_sq, d.t_ss, d.t_kn, d.t_tmp, d.t_kb = toks(2), toks(2), toks(2), toks(2), toks(2)
    d.gqk = ph.sb([128, 128], F32)
    d.t_gqk = Tok()
    d.cs = ph.sb([128, NT, 16], F32)
    d.t_cs = Tok()
    dma(ph.p, d.cs[:], ph.C.rope.ap().rearrange("(i p) c -> p i c", p=128), [], [d.t_cs])
    return d


def proj_qk_tiles(ph, C, q, W, t_W, ident, t_c, ps, psbf, t_bank, sink, vproj=None):
    p = ph.p
    hs = [ph.sb([128, D], F32) for _ in range(2)]
    t_hs = toks(2)
    hnb = [ph.sb([128, D], BF16) for _ in range(2)]
    t_hnb = toks(2)
    junk = ph.sb([128, D], BF16)
    t_junk = Tok()
    stat = [ph.sb([128, 4], F32) for _ in range(2)]
    t_stat = toks(2)
    hnTt = [ph.sb([128, 8, 128], BF16) for _ in range(2)]
    t_hnTt = toks(2)
    def ld(i):
        dma(p, hs[i % 2][:], C.h[128 * i:128 * (i + 1), :], [], [t_hs[i % 2]])

    def g0(i):
        b = i % 2
        norm_tile(p, hs[b][:], t_hs[b], hnb[b][:], t_hnb[b], stat[b], t_stat[b], junk[:], t_junk)

    def g1(i):
        b = i % 2
        transpose8(p, hnb[b], t_hnb[b], psbf[:, 7168:8192], t_bank[7], ident, t_c, hnTt[b][:], t_hnTt[b], ev="dve")

    def g2(i):
        b = i % 2
        kb0 = 2 * (i % 2)
        for nck in range(2):
            for kc in range(8):
                mm(p, ps[:, 512 * (kb0 + nck):512 * (kb0 + nck + 1)], hnTt[b][:, kc, :],
                   W[:, kc, 512 * nck:512 * (nck + 1)], kc == 0, kc == 7, [t_hnTt[b], t_W],
                   [t_bank[kb0], t_bank[kb0 + 1]])
        if vproj is not None:
            vproj(i, b, hnTt[b], t_hnTt[b])

    def g3(i):
        b = i % 2
        kb0 = 2 * (i % 2)
        qk_path(p, q, b, ps[:, 512 * kb0:512 * (kb0 + 2)], [t_bank[kb0], t_bank[kb0 + 1]], q.cs[:, i, :])

    def g4(i):
        qk_path2(p, q, i % 2, q.cs[:, i, :])

    def g5(i):
        sink(i, i % 2)

    stages = [g0, g1, g2, g3, g4, g5]
    ld(0)
    for n in range(NT + len(stages) - 1):
        if n + 1 < NT:
            ld(n + 1)
        for k in range(len(stages) - 1, -1, -1):
            if 0 <= n - k < NT:
                stages[k](n - k)


def phase_kv(C):
    ph = Phase(C, "kv")
    p = ph.p
    cst, t_c = ph.consts()
    ident = cst[:, 0, :]
    Wkv = ph.sb([128, 8, 2048], BF16)
    t_W = Tok()
    gK = ph.sb([128, 8], F32)
    t_gK = Tok()
    dma(p, gK[:], C.kv_norm_g.ap(), [], [t_gK])
    stg = [(ph.sb([128, 2048], F32), Tok()) for _ in range(3)]
    load_w(ph, stg, Wkv, t_W, C.kv_w.ap(), 8, 2048, gain=gK, gain_tok=t_gK, engs=("dve", "act"))
    q = alloc_qk(ph)
    dma(p, q.gqk[:], bass.AP(C.kv_k_norm_g, 0, [[0, 128], [1, 128]]), [], [q.t_gqk])
    KTt = [ph.sb([128, 8, 128], BF16) for _ in range(2)]
    t_KTt = toks(2)
    Vt = [ph.sb([128, D], BF16) for _ in range(2)]
    t_Vt = toks(2)
    ps = ph.psum()
    psbf = ps.bitcast(BF16)
    t_bank = toks(8)
    t_kd, t_vd = Tok(), Tok()

    def sink(i, b):
        transpose8(p, q.kb[b], q.t_kb[b], psbf[:, 6144:7168], t_bank[6], ident, t_c, KTt[b][:], t_KTt[b], ev="act")
        dma(p, C.KTd[:, :, 128 * i:128 * (i + 1)].rearrange("h f t -> f h t"), KTt[b][:], [t_KTt[b]], [Tok()])

    def vproj(i, b, hnTt, t_hnTt):
        for nck in range(2):
            for kc in range(8):
                mm(p, ps[:, 512 * (4 + nck):512 * (5 + nck)], hnTt[:, kc, :],
                   Wkv[:, kc, 1024 + 512 * nck:1024 + 512 * (nck + 1)], kc == 0, kc == 7, [t_hnTt, t_W], [t_bank[4]])
        cp(p, "dve", Vt[b][:], ps[:, 2048:3072], [t_bank[4]], [t_Vt[b]])
        dma(p, C.Vd[128 * i:128 * (i + 1), :], Vt[b][:], [t_Vt[b]], [Tok()])

    proj_qk_tiles(ph, C, q, Wkv, t_W, ident, t_c, ps, psbf, t_bank, sink, vproj)
    ph.finish()


def phase_b_attn(C, l):
    j = l - 2
    nc = C.nc
    lam_init = 0.8 - 0.6 * math.exp(-0.3 * l)
    outer = ExitStack()
    QT = outer.enter_context(nc.sbuf_tensor(f"bq{l}_QT", [128, 8, T], BF16))
    lw = outer.enter_context(nc.sbuf_tensor(f"bq{l}_lw", [128, 136], F32))
    gsub = outer.enter_context(nc.sbuf_tensor(f"bq{l}_gs", [128, 2], F32))
    ph = Phase(C, f"bq{l}")
    p = ph.p
    cst, t_c = ph.consts()
    ident = cst[:, 0, :]
    Wq = ph.sb([128, 8, D], BF16)
    t_Wq = Tok()
    gB = ph.sb([128, 8], F32)
    t_gB = Tok()
    dma(p, gB[:], C.b_norm_g.ap()[j], [], [t_gB])
    stg = [(ph.sb([128, 2048], F32), Tok()) for _ in range(3)]
    load_w(ph, stg, Wq, t_Wq, C.b_w_q.ap()[j], 8, D, gain=gB, gain_tok=t_gB, engs=("dve", "act"))
    q = alloc_qk(ph)
    dma(p, q.gqk[:], bass.AP(C.b_q_norm_g, j * 128, [[0, 128], [1, 128]]), [], [q.t_gqk])
    ts(p, "dve", q.gqk[:], q.gqk[:], 0.125, ALU.mult, [q.t_gqk], [q.t_gqk])
    lpb = ph.sb([128, 256], F32)
    t_l = Tok()
    dma(p, lpb[:], bass.AP(C.b_lambda, j * 256, [[0, 128], [1, 256]]), [], [t_l])
    tt(p, "dve", lw[:, 0:64], lpb[:, 0:64], lpb[:, 64:128], ALU.mult, [t_l], [t_l])
    tt(p, "dve", lw[:, 64:128], lpb[:, 128:192], lpb[:, 192:256], ALU.mult, [t_l], [t_l])
    p.op("dve", lambda e: e.tensor_reduce(out=lw[:, 128:130], in_=lw[:, 0:128].rearrange("p (a b) -> p a b", a=2),
                                          axis=AX.X, op=ALU.add), [t_l], [t_l])
    act(p, lw[:, 130:132], lw[:, 128:130], AF.Exp, [t_l], [t_l])
    tt(p, "dve", lw[:, 132:133], lw[:, 131:132], lw[:, 130:131], ALU.subtract, [t_l], [t_l])
    ts(p, "dve", lw[:, 133:134], lw[:, 132:133], -lam_init, ALU.add, [t_l], [t_l])
    t_gs = Tok()
    dma(p, gsub[:, 0:1], C.b_subln_g.ap()[j].rearrange("(p o) -> p o", o=1), [], [t_gs])
    ts(p, "dve", gsub[:, 1:2], gsub[:, 0:1], 1.0 - lam_init, ALU.mult, [t_gs], [t_gs])
    ps = ph.psum()
    psbf = ps.bitcast(BF16)
    t_bank = toks(8)
    t_QT = Tok()

    def sink(i, b):
        transpose8(p, q.kb[b], q.t_kb[b], psbf[:, 6144:7168], t_bank[6], ident, t_c,
                   QT[:, :, 128 * i:128 * (i + 1)], t_QT, ev="act")

    proj_qk_tiles(ph, C, q, Wq, t_Wq, ident, t_c, ps, psbf, t_bank, sink)
    ph.finish()

    ph = Phase(C, f"bat{l}")
    p = ph.p
    cst, t_c = ph.consts()
    ident = cst[:, 0, :]
    maskI2 = bass.AP(cst, 4 * 128, [[768, 128], [0, 2], [1, 128]])
    neglam = lw[:, 133:134]
    t_QT, t_l = Tok(), Tok()
    WQ = 384
    CH3 = [(WQ * c, WQ) for c in range(T // WQ)]
    gsB = ph.sb([128, 128], F32)
    t_gs = Tok()
    dma(p, gsB[:], bass.AP(C.b_subln_g, j * 128, [[0, 128], [1, 128]]), [], [t_gs])
    ts(p, "dve", gsB[:], gsB[:], 1.0 - lam_init, ALU.mult, [t_gs], [t_gs])
    gsB3 = bass.AP(gsB, 0, [[128, 128], [0, 3], [1, 128]])
    KTh = [ph.sb([128, T], BF16) for _ in range(2)]
    Vh = [ph.sb([128, NT, 129], BF16) for _ in range(2)]
    t_KTh, t_Vh = toks(2), toks(2)
    for hb in range(2):
        p.op("pool", (lambda hb_: lambda e: e.memset(Vh[hb_][:, :, 128:129], 1.0))(hb), [], [t_Vh[hb]])
    Pb = [ph.sb([128, 2, WQ], BF16) for _ in range(6)]
    t_P = toks(6)
    rr = ph.sb([128, 2, 3], F32)
    o1 = ph.sb([128, 3, 128], F32)
    o2 = ph.sb([128, 3, 128], F32)
    sqv = ph.sb([128, 3, 128], F32)
    ssb = ph.sb([128, 12], F32)
    onb = ph.sb([128, 3, 128], BF16)
    oS = [ph.sb([128, WQ], BF16) for _ in range(2)]
    t_ep = Tok()
    t_oS = toks(2)
    ps = ph.psum()
    psbf = ps.bitcast(BF16)
    t_S = toks(2)
    t_PV = [toks(2) for _ in range(2)]
    ns = [0]

    def sset():
        k = ns[0] % 2
        ns[0] += 1
        return k

    def sview(k, W, off=0):
        return ps[:, 1024 * k:1024 * (k + 1)].rearrange("p (m w) -> p m w", m=2)[:, :, off:W]

    def load_head(hd):
        hb = hd % 2
        dma(p, KTh[hb][:], C.KTd[hd], [], [t_KTh[hb]])
        dma(p, Vh[hb][:, :, 0:128], C.Vd[:, 128 * hd:128 * (hd + 1)].rearrange("(i p) d -> p i d", p=128),
            [], [t_Vh[hb]])

    jobs = []
    gci = 0
    for hd in range(8):
        for ci, (t0, W) in enumerate(CH3):
            imax = (t0 + W) // 128 - 1
            for i in range(imax + 1):
                jobs.append((hd, ci, gci, t0, W, imax, i))
            gci += 1

    def s0(n, job):
        hd, ci, g_, t0, W, imax, i = job
        hb = hd % 2
        off = max(0, 128 * i - t0)
        k = sset()
        for m in range(2):
            mm(p, ps[:, 512 * (2 * k + m) + off:512 * (2 * k + m) + W], KTh[hb][64 * m:64 * m + 64, 128 * i:128 * (i + 1)],
               QT[64 * m:64 * m + 64, hd, t0 + off:t0 + W], True, True, [t_KTh[hb], t_QT], [t_S[k]])
        act(p, Pb[n % 6][:, :, off:W], sview(k, W, off), AF.Exp, [t_S[k]], [t_P[n % 6]])
        if 128 * i - t0 >= 0:
            tt(p, "pool", Pb[n % 6][:, :, off:off + 128], Pb[n % 6][:, :, off:off + 128], maskI2, ALU.mult,
               [t_P[n % 6], t_c], [t_P[n % 6]])

    def s1(n, job):
        hd, ci, g_, t0, W, imax, i = job
        hb = hd % 2
        off = max(0, 128 * i - t0)
        pvs = g_ % 2
        for m in range(2):
            b_ = 4 + 2 * pvs + m
            for jb in range(off // 128, 3):
                last_i = t0 // 128 + jb
                mm(p, ps[:, 512 * b_ + 129 * jb:512 * b_ + 129 * (jb + 1)], Pb[n % 6][:, m, 128 * jb:128 * (jb + 1)],
                   Vh[hb][:, i, :], (i == 0 and jb == 0), (i == imax and jb == 2), [t_P[n % 6], t_Vh[hb]], [t_PV[pvs][m]])
            for _d in range(NDUMMY):
                mm(p, ps[:, 512 * b_ + 388:512 * b_ + 508], ident, Pb[n % 6][:, m, 0:120], False, False,
                   [t_P[n % 6], t_c], [t_PV[pvs][m]])
        if i == imax:
            pvv = ps[:, 512 * (4 + 2 * pvs):512 * (6 + 2 * pvs)].rearrange("p (m w) -> p m w", m=2)[:, :, 0:387] \
                .rearrange("p m (j c) -> p m j c", c=129)
            tpv = [t_PV[pvs][0], t_PV[pvs][1]]
            recip(p, rr[:], pvv[:, :, :, 128], tpv + [t_ep], [t_ep])
            ts(p, "dve", rr[:, 1, :], rr[:, 1, :], neglam, ALU.mult, [t_ep, t_l], [t_ep])
            r0 = bass.AP(rr, 0, [[6, 128], [1, 3], [0, 128]])
            r1_ = bass.AP(rr, 3, [[6, 128], [1, 3], [0, 128]])
            tt(p, "dve", o1[:], pvv[:, 0, :, 0:128], r0, ALU.mult, tpv + [t_ep], [t_ep])
            tt(p, "dve", o2[:], pvv[:, 1, :, 0:128], r1_, ALU.mult, tpv + [t_ep], [t_ep])
            tt(p, "pool", o1[:], o1[:], o2[:], ALU.add, [t_ep], [t_ep])
            act(p, sqv[:], o1[:], AF.Square, [t_ep], [t_ep])
            p.op("dve", lambda e: e.tensor_reduce(out=ssb[:, 0:3], in_=sqv[:], axis=AX.X, op=ALU.add), [t_ep], [t_ep])
            act(p, ssb[:, 4:7], ssb[:, 0:3], AF.Ln, [t_ep], [t_ep], scale=1.0 / 128, bias=EPS)
            act(p, ssb[:, 8:11], ssb[:, 4:7], AF.Exp, [t_ep], [t_ep], scale=-0.5)
            rsB = bass.AP(ssb, 8, [[12, 128], [1, 3], [0, 128]])
            tt(p, "dve", o2[:], o1[:], rsB, ALU.mult, [t_ep], [t_ep])
            tt(p, "pool", onb[:], o2[:], gsB3, ALU.mult, [t_ep, t_gs], [t_ep])
            k2 = sset()
            for jb in range(3):
                tr(p, psbf[:, 2048 * k2 + 128 * jb:2048 * k2 + 128 * (jb + 1)], onb[:, jb, :], ident, [t_ep, t_c], [t_S[k2]])
            ob = g_ % 2
            cp(p, "act", oS[ob][:], psbf[:, 2048 * k2:2048 * k2 + 384], [t_S[k2]], [t_oS[ob]])
            dma(p, C.oT[128 * hd:128 * (hd + 1), t0:t0 + W], oS[ob][:], [t_oS[ob]], [Tok()])
            if ci == len(CH3) - 1 and hd + 2 < 8:
                load_head(hd + 2)

    load_head(0)
    load_head(1)
    LAG = 4
    nj = len(jobs)
    for n in range(nj + LAG):
        if n < nj:
            s0(n, jobs[n])
        if n - LAG >= 0:
            s1(n - LAG, jobs[n - LAG])
    ph.finish()
    outer.close()


WNAMES = ["a_norm_g", "a_w_qkv", "a_w_o", "kv_norm_g", "kv_w", "kv_k_norm_g", "b_norm_g", "b_w_q",
          "b_q_norm_g", "b_lambda", "b_subln_g", "b_w_o", "mlp_norm_g", "mlp_w_in", "mlp_w_out"]
WSHAPES = {"a_norm_g": [2, 128, 8], "a_w_qkv": [2, 1024, 3072], "a_w_o": [2, 1024, 1024], "kv_norm_g": [128, 8],
           "kv_w": [1024, 2048], "kv_k_norm_g": [2, 64], "b_norm_g": [2, 128, 8], "b_w_q": [2, 1024, 1024],
           "b_q_norm_g": [2, 2, 64], "b_lambda": [2, 4, 64], "b_subln_g": [2, 128], "b_w_o": [2, 1024, 1024],
           "mlp_norm_g": [4, 128, 8], "mlp_w_in": [4, 1024, 4096], "mlp_w_out": [4, 4096, 1024]}


def build(steps=None, h_in=False, h_out=False):
    nc = bass.Bass("TRN2", target_bir_lowering=False)
    C = Ctx()
    C.nc = nc
    C._uid = [0]
    C.uid = lambda: (C._uid.__setitem__(0, C._uid[0] + 1), C._uid[0])[1]
    C.x = nc.dram_tensor("x", [S, D], F32, kind="ExternalInput")
    C.meta = nc.dram_tensor("meta_tokens", [NM, D], F32, kind="ExternalInput")
    for n in WNAMES:
        setattr(C, n, nc.dram_tensor(n, WSHAPES[n], F32, kind="ExternalInput"))
    C.cst = nc.dram_tensor("cst", [128, 768], BF16, kind="ExternalInput")
    C.rope = nc.dram_tensor("rope", [T, 16], F32, kind="ExternalInput")
    if h_in:
        C.hin = nc.dram_tensor("h_in", [T, D], F32, kind="ExternalInput")
    if h_out:
        C.h = nc.dram_tensor("h_out", [T, D], F32, kind="ExternalOutput").ap()
        C.out = None
    else:
        C.h = nc.dram_tensor("h_res", [T, D], F32).ap()
        C.out = nc.dram_tensor("out", [S, D], F32, kind="ExternalOutput")
    C.oT = nc.dram_tensor("oT_s", [D, T], BF16).ap()
    C.KTd = nc.dram_tensor("KT_s", [8, 128, T], BF16).ap()
    C.Vd = nc.dram_tensor("V_s", [T, D], BF16).ap()
    if steps is None:
        steps = ["init"]
        for l in range(4):
            if l == 2:
                steps.append("kv")
            steps += [f"attn{l}", f"post{l}"]
    zero_oT_pad.C = C
    with ExitStack() as top:
        C.sy = Sync(nc, top)
        for s in steps:
            if s == "init":
                phase_init(C)
            elif s == "copyin":
                ph = Phase(C, "cpin")
                th = Tok()
                zz = ph.sb([128, D], F32)
                tzz = Tok()
                ph.p.op("pool", lambda e: e.memset(zz[:], 0.0), [], [tzz])
                zero_oT_pad(ph.p, zz, tzz)
                for j in range(8):
                    r0, r1 = 528 * j, 528 * (j + 1)
                    dma(ph.p, C.h[r0:r1, :], C.hin.ap()[r0:r1, :], [], [th])
                ph.finish()
            elif s == "kv":
                phase_kv(C)
            elif s.startswith("attn"):
                l = int(s[4:])
                if l < 2:
                    phase_a_attn(C, l)
                else:
                    phase_b_attn(C, l)
            elif s.startswith("post"):
                l = int(s[4:])
                phase_post(C, l, C.a_w_o.ap()[l] if l < 2 else C.b_w_o.ap()[l - 2], final=(l == 3 and not h_out))
            elif s.startswith("oproj"):
                l = int(s[5:])
                phase_oproj(C, C.a_w_o.ap()[l] if l < 2 else C.b_w_o.ap()[l - 2])
            elif s.startswith("mlp"):
                l = int(s[3:])
                phase_mlp(C, l, final=(l == 3 and not h_out))
    return nc


def make_consts():
    j = np.arange(128)[:, None]
    s = np.arange(128)[None, :]
    ident = (j == s).astype(np.float32)
    negtri = -(j >= s).astype(np.float32)
    ones = np.ones((128, 128), np.float32)
    maskS = (j < s).astype(np.float32)
    maskI = (j <= s).astype(np.float32)
    onesS = ones / 128.0
    cst = np.concatenate([ident, negtri, ones, maskS, maskI, onesS], axis=1).astype(ml_dtypes.bfloat16)
    inv_freq = np.power(np.float32(500000.0), -np.arange(0, 16, 2, dtype=np.float32) / np.float32(16))
    ang = np.arange(T, dtype=np.float32)[:, None] * inv_freq[None, :]
    rope = np.concatenate([np.cos(ang), np.sin(ang)], axis=1).astype(np.float32)
    return cst, rope


_NC_CACHE = {}


def kernel(**inputs):
    x = np.ascontiguousarray(inputs["x"], dtype=np.float32)
    B = x.shape[0]
    cst, rope = make_consts()
    if "full" not in _NC_CACHE:
        _NC_CACHE["full"] = build()
    nc = _NC_CACHE["full"]
    shared = {n: np.ascontiguousarray(inputs[n], dtype=np.float32) for n in WNAMES}
    for n in ("a_norm_g", "kv_norm_g", "b_norm_g", "mlp_norm_g"):
        g = shared[n]
        shared[n] = np.ascontiguousarray(g.reshape(g.shape[:-1] + (8, 128)).swapaxes(-1, -2))
    shared["meta_tokens"] = np.ascontiguousarray(inputs["meta_tokens"], dtype=np.float32)
    shared["cst"] = cst
    shared["rope"] = rope
    in_maps = []
    for b in range(B):
        m = dict(shared)
        m["x"] = x[b]
        in_maps.append(m)
    res = run_bass_kernel_spmd(nc, in_maps, core_ids=list(range(B)))
    return np.stack([np.asarray(r["out"], dtype=np.float32) for r in res.results], axis=0)
```
